# Optimizing a Trainium2 kernel written in Bass

```python
import jax, jax.numpy as jnp
from jax import lax
import numpy as np

D_MODEL = 2048
BATCH = 4
SEQ = 4096
DEPTH = 1

PLE_DIM = 256
EPS = 1e-6
NEG = -1e30
FORCE = 1e3
MAX_POS_OFFSET = 1024

M_WIDTH = D_MODEL // 2
M_HEADS = 4
M_HEAD_DIM = M_WIDTH // M_HEADS
M_CONV = 4
M_CHUNK = 64

N_WIDTH = D_MODEL - M_WIDTH
N_HEAD_DIM = 64
N_HEADS = N_WIDTH // N_HEAD_DIM
N_KV = 4
CMP_BLOCK = 32
CMP_STRIDE = 16
CMP_HIDDEN = 128
SEL_BLOCK = 64
N_SELECT = 16
WINDOW = 512
NSA_Q_BLOCK = 64
ROPE_THETA = 500000.0
ROT_DIM = N_HEAD_DIM // 4

D_MIX = M_WIDTH + N_WIDTH
KV_W = N_KV * N_HEAD_DIM
IN_SPLITS = (M_WIDTH,) * 5 + (M_HEADS, M_HEADS) + (N_WIDTH,) + (KV_W,) * 6 + (3 * N_HEADS, N_WIDTH)
D_IN = sum(IN_SPLITS)

kernel_name = "hybrid_mlstm_nsa_parallel_heads"


def rmsnorm(x, g):
    xf = x.astype(jnp.float32)
    y = xf * lax.rsqrt(jnp.mean(xf * xf, axis=-1, keepdims=True) + EPS)
    return (y * g.astype(jnp.float32)).astype(x.dtype)


def split_cols(h, sizes):
    cuts = [int(c) for c in np.cumsum(sizes)[:-1]]
    return jnp.split(h, cuts, axis=-1)


def heads(a, n):
    return a.reshape(a.shape[0], a.shape[1], n, -1)


def partial_rope(x, positions):
    half = ROT_DIM // 2
    inv = ROPE_THETA ** (-(jnp.arange(half, dtype=jnp.float32) * 2.0 / ROT_DIM))
    ang = positions.astype(jnp.float32)[..., None] * inv
    cos = jnp.cos(ang)[:, :, None, :]
    sin = jnp.sin(ang)[:, :, None, :]
    xr = x[..., :ROT_DIM].astype(jnp.float32)
    x1, x2 = xr[..., :half], xr[..., half:]
    rot = jnp.concatenate([x1 * cos - x2 * sin, x2 * cos + x1 * sin], axis=-1).astype(x.dtype)
    return jnp.concatenate([rot, x[..., ROT_DIM:]], axis=-1)


def causal_conv(a, w, b):
    S = a.shape[1]
    ap = jnp.pad(a, ((0, 0), (M_CONV - 1, 0), (0, 0)))
    y = b + w[M_CONV - 1] * a
    for j in range(M_CONV - 1):
        y = y + w[j] * ap[:, j:j + S]
    return y


def head_layernorm(h, g):
    mu = jnp.mean(h, axis=-1, keepdims=True)
    var = jnp.mean(jnp.square(h - mu), axis=-1, keepdims=True)
    y = (h - mu) * lax.rsqrt(var + EPS)
    return y.reshape(h.shape[0], h.shape[1], -1) * g.astype(jnp.float32)


def mlstm_chunk_step(carry, xs):
    C, n, m = carry
    q, k, v, ig, lf = xs
    L = q.shape[2]
    b = jnp.cumsum(lf, axis=-1)
    causal = jnp.tril(jnp.ones((L, L), dtype=bool))
    D = jnp.where(causal, b[..., :, None] - b[..., None, :] + ig[..., None, :], NEG)
    m_inter = b + m[..., None]
    m_t = jnp.maximum(m_inter, jnp.max(D, axis=-1))
    A = jnp.exp(D - m_t[..., None]) * jnp.einsum('bhtd,bhsd->bhts', q, k)
    inter = jnp.exp(m_inter - m_t)
    num = jnp.einsum('bhts,bhsd->bhtd', A, v) + inter[..., None] * jnp.einsum('bhtk,bhvk->bhtv', q, C)
    den = jnp.sum(A, axis=-1) + inter * jnp.einsum('bhtk,bhk->bht', q, n)
    h = num / jnp.maximum(jnp.abs(den), 1.0)[..., None]
    bL = b[..., -1]
    w_end = bL[..., None] - b + ig
    m_new = jnp.maximum(bL + m, jnp.max(w_end, axis=-1))
    w_s = jnp.exp(w_end - m_new[..., None])
    decay = jnp.exp(bL + m - m_new)
    C = decay[..., None, None] * C + jnp.einsum('bhs,bhsv,bhsk->bhvk', w_s, v, k)
    n = decay[..., None] * n + jnp.einsum('bhs,bhsk->bhk', w_s, k)
    return (C, n, m_new), h


def mlstm_mixer(q, k, v, ig, lf):
    B, S, H, dh = q.shape
    L = M_CHUNK
    NC = S // L

    def chunks(a):
        return a.reshape(B, NC, L, H, dh).transpose(1, 0, 3, 2, 4)

    def gchunks(a):
        return a.reshape(B, NC, L, H).transpose(1, 0, 3, 2)

    init = (jnp.zeros((B, H, dh, dh), jnp.float32), jnp.zeros((B, H, dh), jnp.float32),
            jnp.zeros((B, H), jnp.float32))
    _, hs = lax.scan(mlstm_chunk_step, init, (chunks(q), chunks(k), chunks(v), gchunks(ig), gchunks(lf)))
    return hs.transpose(1, 0, 3, 2, 4).reshape(B, S, H, dh)


def compress(a, pos, w1, w2):
    B, S, G, dh = a.shape
    R = CMP_BLOCK // CMP_STRIDE
    n16 = S // CMP_STRIDE
    nc = n16 - R + 1
    a16 = a.reshape(B, n16, CMP_STRIDE, G, dh)
    blocks = jnp.concatenate([a16[:, r:r + nc] for r in range(R)], axis=2)
    blocks = blocks + pos[None, None, :, None, :]
    flat = blocks.transpose(0, 1, 3, 2, 4).reshape(B, nc, G, CMP_BLOCK * dh)
    return jax.nn.gelu(flat @ w1) @ w2


def sel_overlap(nc, nsel):
    tok = np.arange(nc)[:, None] * CMP_STRIDE + np.arange(CMP_BLOCK)[None, :]
    onehot = (tok[..., None] // SEL_BLOCK) == np.arange(nsel)
    return jnp.asarray(onehot.mean(axis=1).astype(np.float32))


def nsa_mixer(q, kc, vc, ks, vs, kw, vw, gates, pos_k, pos_v, ck_w1, ck_w2, cv_w1, cv_w2):
    B, S, H, dh = q.shape
    G = kc.shape[2]
    J = H // G
    QB = NSA_Q_BLOCK
    NB = S // QB
    kcmp = compress(kc, pos_k, ck_w1, ck_w2)
    vcmp = compress(vc, pos_v, cv_w1, cv_w2)
    nc = kcmp.shape[1]
    c_end = jnp.arange(nc) * CMP_STRIDE + CMP_BLOCK - 1
    nsel = S // SEL_BLOCK
    n_top = min(N_SELECT, nsel)
    overlap = sel_overlap(nc, nsel)
    ksb = ks.reshape(B, nsel, SEL_BLOCK, G, dh).transpose(0, 3, 1, 2, 4)
    vsb = vs.reshape(B, nsel, SEL_BLOCK, G, dh).transpose(0, 3, 1, 2, 4)
    kw_pad = jnp.pad(kw, ((0, 0), (WINDOW, 0), (0, 0), (0, 0)))
    vw_pad = jnp.pad(vw, ((0, 0), (WINDOW, 0), (0, 0), (0, 0)))
    bi = jnp.arange(B)[:, None, None, None]
    gi = jnp.arange(G)[None, :, None, None]
    blk = jnp.arange(nsel)

    def block_fn(args):
        c, qc, gc = args
        t = c * QB + jnp.arange(QB)
        qg = qc.reshape(B, QB, G, J, dh).transpose(0, 2, 3, 1, 4)
        mask_c = c_end[None, :] <= t[:, None]
        s = jnp.einsum('bgjqd,bcgd->bgjqc', qg, kcmp).astype(jnp.float32)
        p_c = jax.nn.softmax(jnp.where(mask_c, s, NEG), axis=-1) * mask_c
        o_cmp = jnp.einsum('bgjqc,bcgd->bgjqd', p_c.astype(vcmp.dtype), vcmp)
        imp = jnp.einsum('bgjqc,cn->bgqn', p_c, overlap)
        cur = (t // SEL_BLOCK)[:, None]
        valid = blk[None, :] * SEL_BLOCK <= t[:, None]
        forced = (blk[None, :] == 0) | (blk[None, :] == cur) | (blk[None, :] == cur - 1)
        score = jnp.where(valid, imp + jnp.where(forced, FORCE, 0.0), NEG)
        _, idx = lax.top_k(score, n_top)
        kg = ksb[bi, gi, idx]
        vg = vsb[bi, gi, idx]
        kpos = idx[..., None] * SEL_BLOCK + jnp.arange(SEL_BLOCK)
        mask_s = (kpos <= t[:, None, None])[:, :, None]
        s = jnp.einsum('bgjqd,bgqnld->bgjqnl', qg, kg).astype(jnp.float32)
        s = jnp.where(mask_s, s, NEG).reshape(B, G, J, QB, n_top * SEL_BLOCK)
        p_s = jax.nn.softmax(s, axis=-1).reshape(B, G, J, QB, n_top, SEL_BLOCK)
        o_sel = jnp.einsum('bgjqnl,bgqnld->bgjqd', p_s.astype(vg.dtype), vg)
        kwc = lax.dynamic_slice_in_dim(kw_pad, c * QB, QB + WINDOW, axis=1)
        vwc = lax.dynamic_slice_in_dim(vw_pad, c * QB, QB + WINDOW, axis=1)
        kp = c * QB - WINDOW + jnp.arange(QB + WINDOW)
        mask_w = (kp[None, :] <= t[:, None]) & (kp[None, :] > t[:, None] - WINDOW) & (kp[None, :] >= 0)
        s = jnp.einsum('bgjqd,bkgd->bgjqk', qg, kwc).astype(jnp.float32)
        p_w = jax.nn.softmax(jnp.where(mask_w, s, NEG), axis=-1)
        o_win = jnp.einsum('bgjqk,bkgd->bgjqd', p_w.astype(vwc.dtype), vwc)
        g = gc.reshape(B, QB, G, J, 3).transpose(0, 2, 3, 1, 4)
        o = g[..., 0:1] * o_cmp + g[..., 1:2] * o_sel + g[..., 2:3] * o_win
        return o.transpose(0, 3, 1, 2, 4).reshape(B, QB, H * dh)

    q_blocks = q.reshape(B, NB, QB, H, dh).swapaxes(0, 1)
    g_blocks = gates.reshape(B, NB, QB, H, 3).swapaxes(0, 1)
    out = lax.map(block_fn, (jnp.arange(NB), q_blocks, g_blocks))
    return out.swapaxes(0, 1).reshape(B, S, H * dh)


def setup_inputs(seed: int = 0) -> dict:
    key = jax.random.key(seed)
    k = jax.random.split(key, 24)
    f32 = jnp.float32

    def nrm(kk, shape, scale):
        return jax.random.normal(kk, shape, f32) * scale

    x = nrm(k[0], (BATCH, SEQ, D_MODEL), 1.0)
    p = nrm(k[1], (DEPTH, BATCH, SEQ, PLE_DIM), 1.0)
    offset = jax.random.randint(k[2], (BATCH, 1), 0, MAX_POS_OFFSET, dtype=jnp.int32)
    positions = offset + jnp.arange(SEQ, dtype=jnp.int32)[None, :]
    norm_g = 1.0 + nrm(k[3], (DEPTH, D_MODEL), 0.02)
    w_in = nrm(k[4], (DEPTH, D_MODEL, D_IN), D_MODEL ** -0.5)
    conv_w = nrm(k[5], (DEPTH, M_CONV, 2 * M_WIDTH), M_CONV ** -0.5)
    conv_b = nrm(k[6], (DEPTH, 2 * M_WIDTH), 0.01)
    b_igate = nrm(k[7], (DEPTH, M_HEADS), 0.1)
    b_fgate = jnp.linspace(3.0, 6.0, M_HEADS, dtype=f32)[None, :] + nrm(k[8], (DEPTH, M_HEADS), 0.1)
    m_norm_g = 1.0 + nrm(k[9], (DEPTH, M_WIDTH), 0.02)
    cmp_pos_k = nrm(k[10], (DEPTH, CMP_BLOCK, N_HEAD_DIM), 0.02)
    cmp_pos_v = nrm(k[11], (DEPTH, CMP_BLOCK, N_HEAD_DIM), 0.02)
    cmp_k_w1 = nrm(k[12], (DEPTH, CMP_BLOCK * N_HEAD_DIM, CMP_HIDDEN), (CMP_BLOCK * N_HEAD_DIM) ** -0.5)
    cmp_k_w2 = nrm(k[13], (DEPTH, CMP_HIDDEN, N_HEAD_DIM), CMP_HIDDEN ** -0.5)
    cmp_v_w1 = nrm(k[14], (DEPTH, CMP_BLOCK * N_HEAD_DIM, CMP_HIDDEN), (CMP_BLOCK * N_HEAD_DIM) ** -0.5)
    cmp_v_w2 = nrm(k[15], (DEPTH, CMP_HIDDEN, N_HEAD_DIM), CMP_HIDDEN ** -0.5)
    w_out = nrm(k[16], (DEPTH, D_MIX, D_MODEL), D_MIX ** -0.5)
    ple_gate_w = nrm(k[17], (DEPTH, D_MODEL, D_MODEL), D_MODEL ** -0.5)
    ple_proj_w = nrm(k[18], (DEPTH, PLE_DIM, D_MODEL), PLE_DIM ** -0.5)
    ple_norm_g = 1.0 + nrm(k[19], (DEPTH, D_MODEL), 0.02)
    final_norm_g = 1.0 + nrm(k[20], (D_MODEL,), 0.02)
    return {"x": x, "p": p, "positions": positions, "norm_g": norm_g, "w_in": w_in,
            "conv_w": conv_w, "conv_b": conv_b, "b_igate": b_igate, "b_fgate": b_fgate,
            "m_norm_g": m_norm_g, "cmp_pos_k": cmp_pos_k, "cmp_pos_v": cmp_pos_v,
            "cmp_k_w1": cmp_k_w1, "cmp_k_w2": cmp_k_w2, "cmp_v_w1": cmp_v_w1, "cmp_v_w2": cmp_v_w2,
            "w_out": w_out, "ple_gate_w": ple_gate_w, "ple_proj_w": ple_proj_w,
            "ple_norm_g": ple_norm_g, "final_norm_g": final_norm_g}


def reference(x, p, positions, norm_g, w_in, conv_w, conv_b, b_igate, b_fgate, m_norm_g,
              cmp_pos_k, cmp_pos_v, cmp_k_w1, cmp_k_w2, cmp_v_w1, cmp_v_w2, w_out,
              ple_gate_w, ple_proj_w, ple_norm_g, final_norm_g):
    B, S, _ = x.shape
    f32 = jnp.float32
    for l in range(DEPTH):
        h = rmsnorm(x, norm_g[l])
        (mq, mk, mv, mo, mz, mi, mf, nq, kc, vc, ks, vs, kw, vw, ng, nz) = split_cols(h @ w_in[l], IN_SPLITS)
        qk = jax.nn.silu(causal_conv(jnp.concatenate([mq, mk], axis=-1), conv_w[l], conv_b[l]))
        mq_c, mk_c = qk[..., :M_WIDTH], qk[..., M_WIDTH:]
        ig = mi.astype(f32) + b_igate[l].astype(f32)
        lf = jax.nn.log_sigmoid(mf.astype(f32) + b_fgate[l].astype(f32))
        hm = mlstm_mixer(heads(mq_c, M_HEADS).astype(f32),
                         (heads(mk_c, M_HEADS) * (M_HEAD_DIM ** -0.5)).astype(f32),
                         heads(mv, M_HEADS).astype(f32), ig, lf)
        hm = head_layernorm(hm, m_norm_g[l]).astype(x.dtype)
        m_out = hm * jax.nn.sigmoid(mo) * jax.nn.silu(mz)
        q = partial_rope(heads(nq, N_HEADS) * (N_HEAD_DIM ** -0.5), positions)
        n_att = nsa_mixer(q,
                          partial_rope(heads(kc, N_KV), positions), heads(vc, N_KV),
                          partial_rope(heads(ks, N_KV), positions), heads(vs, N_KV),
                          partial_rope(heads(kw, N_KV), positions), heads(vw, N_KV),
                          jax.nn.sigmoid(ng).reshape(B, S, N_HEADS, 3),
                          cmp_pos_k[l], cmp_pos_v[l], cmp_k_w1[l], cmp_k_w2[l], cmp_v_w1[l], cmp_v_w2[l])
        n_out = n_att * jax.nn.silu(nz)
        x = x + jnp.concatenate([m_out, n_out], axis=-1) @ w_out[l]
        ple = rmsnorm(p[l] @ ple_proj_w[l], ple_norm_g[l])
        x = x + jax.nn.sigmoid(x @ ple_gate_w[l]) * ple
    return rmsnorm(x, final_norm_g)
```

```python
import contextlib
import math
import numpy as np
import concourse.bass as bass
import concourse.mybir as mybir
from concourse.bass_utils import run_bass_kernel_spmd

F32 = mybir.dt.float32
BF16 = mybir.dt.bfloat16
I32 = mybir.dt.int32
U8 = mybir.dt.uint8
AF = mybir.ActivationFunctionType
ALU = mybir.AluOpType
AX = mybir.AxisListType
ENG = ['pe', 'act', 'dve', 'pool', 'sp']
DT_SIZE = {F32: 4, BF16: 2, I32: 4, U8: 1}

SEQ = 4096
DM = 2048
NT = SEQ // 128
EPS = 1e-6
FORCE = 1e3
NEG = -1e30


class Buf:
    __slots__ = ('name', 'w', 'r')

    def __init__(self, name=''):
        self.name = name
        self.w = None
        self.r = {}


class Sched:
    def __init__(self, nc, stack, n_dma_sems=8):
        self.nc = nc
        self.prog = {e: [] for e in ENG}
        self.cnt = {e: 0 for e in ENG}
        self.seen = {e: {} for e in ENG}
        self.esem = {e: stack.enter_context(nc.semaphore('es_' + e)) for e in ENG if e != 'sp'}
        self.dq = {}
        self.dsem = []
        self.dcnt = []
        for q in ('sp', 'act', 'pool'):
            ids = []
            for i in range(n_dma_sems):
                self.dsem.append(stack.enter_context(nc.semaphore('ds_%s%d' % (q, i))))
                self.dcnt.append(0)
                ids.append(len(self.dsem) - 1)
            self.dq[q] = [ids, 0]
        self.arena_bytes = 204 * 1024
        self.arena = stack.enter_context(nc.sbuf_tensor('arena', [128, self.arena_bytes], U8))
        self.arena_off = 0
        self.marks = []
        self.psum = [stack.enter_context(nc.psum_tensor('ps%d' % i, [128, 512], F32)) for i in range(8)]
        self.psbuf = [Buf('ps%d' % i) for i in range(8)]

    def alloc(self, shape, dtype, name=''):
        n = 1
        for s in shape[1:]:
            n *= s
        nbytes = (n * DT_SIZE[dtype] + 63) // 64 * 64
        off = self.arena_off
        assert off + nbytes <= self.arena_bytes, ('SBUF arena overflow', name, off, nbytes)
        self.arena_off += nbytes
        v = self.arena[0:shape[0], off:off + n * DT_SIZE[dtype]]
        if dtype != U8:
            v = v.bitcast(dtype)
        if len(shape) > 2:
            names = ' '.join('d%d' % i for i in range(len(shape) - 1))
            kw = {'d%d' % i: shape[i + 1] for i in range(len(shape) - 1)}
            v = v.rearrange('p (%s) -> p %s' % (names, names), **kw)
        return v, Buf(name)

    def mark(self):
        self.marks.append(self.arena_off)

    def release(self):
        self.arena_off = self.marks.pop()

    def ps(self, i, dtype=F32):
        v = self.psum[i][:, :]
        if dtype != F32:
            v = v.bitcast(dtype)
        return v

    @staticmethod
    def _key(tok):
        if tok[0] == 'e':
            return ('e', tok[1]), tok[2]
        return ('d', tok[1]), 16 * tok[2]

    def _waits(self, eng, need):
        out = []
        seen = self.seen[eng]
        for k, v in need.items():
            if k == ('e', 'pe') and eng == 'pe':
                continue
            if seen.get(k, 0) >= v:
                continue
            seen[k] = v
            sem = self.esem[k[1]] if k[0] == 'e' else self.dsem[k[1]]
            out.append((sem, v))
        return out

    @staticmethod
    def _need(reads, writes, extra=()):
        need = {}

        def add(k, v):
            if need.get(k, 0) < v:
                need[k] = v
        for b in reads:
            if b.w is not None:
                add(*b.w)
        for b in writes:
            if b.w is not None:
                add(*b.w)
            for k, v in b.r.items():
                add(k, v)
        for k, v in extra:
            add(k, v)
        return need

    @staticmethod
    def _commit(kv, reads, writes):
        k, v = kv
        for b in reads:
            if b.r.get(k, 0) < v:
                b.r[k] = v
        for b in writes:
            b.w = kv
            b.r = {}

    def op(self, eng, fn, reads=(), writes=()):
        waits = self._waits(eng, self._need(reads, writes))
        self.cnt[eng] += 1
        self._commit((('e', eng), self.cnt[eng]), reads, writes)
        self.prog[eng].append((waits, fn, (self.esem[eng], 1)))

    def dma(self, q, out, in_, reads=(), writes=(), **kw):
        ids, rr = self.dq[q]
        j = ids[rr % len(ids)]
        self.dq[q][1] = rr + 1
        extra = [(('d', j), 16 * self.dcnt[j])] if self.dcnt[j] > 0 else []
        waits = self._waits(q, self._need(reads, writes, extra))
        self.dcnt[j] += 1
        self._commit((('d', j), 16 * self.dcnt[j]), reads, writes)
        self.prog[q].append((waits, (lambda e, o=out, i=in_, kw=kw: e.dma_start(out=o, in_=i, **kw)),
                             (self.dsem[j], 16)))

    def barrier(self):
        need = {}
        for e in ENG:
            if e != 'sp' and self.cnt[e] > 0:
                need[('e', e)] = self.cnt[e]
        for j, c in enumerate(self.dcnt):
            if c > 0:
                need[('d', j)] = 16 * c
        for e in ENG:
            waits = self._waits(e, need)
            if waits:
                self.prog[e].append((waits, None, None))

    def emit(self):
        nc = self.nc
        with nc.Block() as block:
            def replay(lst, eng):
                for waits, fn, inc in lst:
                    for sem, v in waits:
                        eng.wait_ge(sem, v)
                    if fn is not None:
                        fn(eng).then_inc(inc[0], inc[1])

            @block.sync
            def _(e):
                replay(self.prog['sp'], e)

            @block.scalar
            def _(e):
                replay(self.prog['act'], e)

            @block.vector
            def _(e):
                replay(self.prog['dve'], e)

            @block.gpsimd
            def _(e):
                replay(self.prog['pool'], e)

            @block.tensor
            def _(e):
                replay(self.prog['pe'], e)


OFF = {}
_o = 0
for _n, _w in [('mq', 1024), ('mk', 1024), ('mv', 1024), ('mo', 1024), ('mz', 1024), ('mi', 4), ('mf', 4),
               ('nq', 1024), ('kc', 256), ('vc', 256), ('ks', 256), ('vs', 256), ('kw', 256), ('vw', 256),
               ('ng', 48), ('nz', 1024)]:
    OFF[_n] = _o
    _o += _w
assert _o == 8760


class Cfg:
    def __init__(self, heads, pairs, tf, debug=False, phases='ABCD'):
        self.heads = heads
        self.pairs = pairs
        self.NH = len(heads)
        self.NP = len(pairs)
        self.TF = tf
        self.debug = debug
        self.phases = phases
        blocks = []
        cols = []

        def add(kind, idx, c):
            blocks.append((kind, idx, len(cols), len(c)))
            cols.extend(c)
        hv = []
        for h in heads:
            hv.extend(range(OFF['mv'] + h * 256, OFF['mv'] + (h + 1) * 256))
        for i in range(0, len(hv), 512):
            add('v_m', i // 512, hv[i:i + 512])
        for i, h in enumerate(heads):
            add('gate_m', i, list(range(OFF['mo'] + h * 256, OFF['mo'] + (h + 1) * 256)) +
                list(range(OFF['mz'] + h * 256, OFF['mz'] + (h + 1) * 256)))
        for pi, (g0, g1) in enumerate(pairs):
            c = []
            for jj in range(4):
                for g in (g0, g1):
                    hh = 4 * g + jj
                    c.extend(range(OFF['nq'] + hh * 64, OFF['nq'] + (hh + 1) * 64))
            add('nq', pi, c)
            c = []
            for nm in ('kc', 'ks', 'kw', 'vc'):
                for g in (g0, g1):
                    c.extend(range(OFF[nm] + g * 64, OFF[nm] + (g + 1) * 64))
            add('nk', pi, c)
            c = []
            for nm in ('vs', 'vw'):
                for g in (g0, g1):
                    c.extend(range(OFF[nm] + g * 64, OFF[nm] + (g + 1) * 64))
            for g in (g0, g1):
                c.extend(range(OFF['ng'] + g * 12, OFF['ng'] + (g + 1) * 12))
            add('nv', pi, c)
            c = []
            for g in (g0, g1):
                c.extend(range(OFF['nz'] + g * 256, OFF['nz'] + (g + 1) * 256))
            add('nz', pi, c)
        self.blocks = blocks
        self.tm_cols = cols
        fm = []
        for h in heads:
            fm.extend(range(OFF['mq'] + h * 256, OFF['mq'] + (h + 1) * 256))
        for h in heads:
            fm.extend(range(OFF['mk'] + h * 256, OFF['mk'] + (h + 1) * 256))
        self.fm_cols = fm
        self.if_cols = [OFF['mi'] + h for h in heads] + [OFF['mf'] + h for h in heads]
        mr = []
        for h in heads:
            mr.extend(range(h * 256, (h + 1) * 256))
        for (g0, g1) in pairs:
            for g in (g0, g1):
                mr.extend(range(1024 + g * 256, 1024 + (g + 1) * 256))
        self.mix_rows = mr


def host_consts():
    t = np.arange(SEQ)
    n = np.arange(64)
    cur = (t // 64)[:, None]
    valid = n[None, :] * 64 <= t[:, None]
    forced = (n[None, :] == 0) | (n[None, :] == cur) | (n[None, :] == cur - 1)
    sb = np.where(valid, np.where(forced, FORCE, 0.0), NEG).astype(np.float32)
    sb = sb.reshape(NT, 128, 64).transpose(1, 0, 2).copy()
    ov = np.zeros((256, 65), np.float32)
    for c in range(255):
        tok = c * 16 + np.arange(32)
        ov[c, 0] = 1.0
        for nn in np.unique(tok // 64):
            ov[c, 1 + nn] = np.mean(tok // 64 == nn)
    inv = (500000.0 ** (-(np.arange(8, dtype=np.float32) * 2.0 / 16))).astype(np.float32)
    return sb, ov, inv


def build(cfg):
    nc = bass.Bass("TRN2", target_bir_lowering=False)
    NH, NP, TF = cfg.NH, cfg.NP, cfg.TF
    NFC = 4 * NH
    WTM = len(cfg.tm_cols)
    NMIX = NH * 256 + NP * 512
    skind = "ExternalOutput" if cfg.debug else "Internal"

    def din(name, shape, dt=F32):
        return nc.dram_tensor(name, list(shape), dt, kind="ExternalInput").ap()

    def dscr(name, shape, dt):
        k = "ExternalOutput" if name in getattr(cfg, 'dump', ()) else "Internal"
        if name in getattr(cfg, 'feed', ()):
            k = "ExternalInput"
        return nc.dram_tensor(name, list(shape), dt, kind=k).ap()

    x_in = din("x", [SEQ, DM])
    xf_in = din("xf", [TF, DM])
    pf_in = din("pf", [TF, 256])
    pos_in = din("pos", [SEQ], I32)
    norm_g = din("norm_g", [DM])
    w_fm = din("w_fm", [DM, NFC * 128])
    w_if = din("w_if", [DM, 2 * NH])
    w_tm = din("w_tm", [DM, WTM])
    convw = din("convw", [128, NFC, 4])
    convb = din("convb", [128, NFC, 1])
    b_i = din("b_i", [NH, 1])
    b_f = din("b_f", [NH, 1])
    mng = din("mng", [NH * 256])
    posk = din("posk", [64, 32])
    posv = din("posv", [64, 32])
    ckw1 = din("ckw1", [64, 32, 128])
    ckw2 = din("ckw2", [128, 64])
    cvw1 = din("cvw1", [64, 32, 128])
    cvw2 = din("cvw2", [128, 64])
    w_out = din("w_out", [DM, DM])
    w_pg = din("w_pg", [DM, DM])
    w_pp = din("w_pp", [256, DM])
    ple_g = din("ple_g", [DM])
    fin_g = din("fin_g", [DM])
    selbias = din("selbias", [128, NT, 64])
    ovl = din("ovl", [256, 65])
    invf = din("invf", [8])
    out = nc.dram_tensor("out", [TF, DM], F32, kind="ExternalOutput").ap()

    qkT = dscr("qkT", [NFC * 128, SEQ], BF16)
    gi = dscr("gi", [NH, SEQ], F32)
    gf = dscr("gf", [NH, SEQ], F32)
    vm = dscr("vm", [SEQ, NH * 256], BF16)
    gm = dscr("gm", [SEQ, NH * 256], BF16)
    qTn = dscr("qTn", [NP, 512, SEQ], BF16)
    kT4 = dscr("kT4", [NP, 4, 128, SEQ], BF16)
    vsw = dscr("vsw", [NP, SEQ, 256], BF16)
    ngs = dscr("ngs", [NP, SEQ, 24], F32)
    szn = dscr("szn", [NP, SEQ, 512], BF16)
    mixT = dscr("mixT", [DM, SEQ], BF16)

    with contextlib.ExitStack() as st:
        S = Sched(nc, st)
        ident, identb = S.alloc([128, 128], BF16, 'ident')
        identf, identfb = S.alloc([128, 128], F32, 'identf')
        S.op('pool', lambda e: e.memset(identf, 1.0), writes=[identfb])
        S.op('pool', lambda e: e.affine_select(out=identf, in_=identf, pattern=[[-1, 128]], compare_op=ALU.is_equal,
                                                fill=0.0, base=0, channel_multiplier=1),
             reads=[identfb], writes=[identfb])
        S.op('dve', lambda e: e.tensor_copy(out=ident, in_=identf), reads=[identfb], writes=[identb])

        def transposes_to(pe_bank, srcs, src_bufs, dtype=BF16, idn=None, idb=None, width=128):
            pv = S.ps(pe_bank, dtype)
            idn = ident if idn is None else idn
            idb = identb if idb is None else idb
            for i, s_ap in enumerate(srcs):
                S.op('pe', lambda e, i=i, s_ap=s_ap: e.transpose(
                    out=pv[0:s_ap.shape[1], i * width:i * width + s_ap.shape[0]], in_=s_ap,
                    identity=idn[0:s_ap.shape[0], 0:s_ap.shape[0]]),
                    reads=list(src_bufs) + [idb], writes=[S.psbuf[pe_bank]])
            return pv

        if 'A' in cfg.phases:
            phase_A(S, cfg, locals())
        if 'B' in cfg.phases:
            phase_B(S, cfg, locals())
        if 'C' in cfg.phases:
            phase_C(S, cfg, locals())
        if 'D' in cfg.phases:
            phase_D(S, cfg, locals())
        S.barrier()
        S.emit()
    return nc


def phase_A(S, cfg, G):
    NH, NP = cfg.NH, cfg.NP
    NFC = 4 * NH
    ident, identb = G['ident'], G['identb']
    transposes_to = G['transposes_to']
    x_in, pos_in, norm_g = G['x_in'], G['pos_in'], G['norm_g']
    S.mark()
    hT, hTb = S.alloc([128, 16, SEQ], BF16, 'hT')
    cosq, cqb = S.alloc([128, NT, 8], F32, 'cosq')
    sinq, sqb = S.alloc([128, NT, 8], F32, 'sinq')
    cosk, ckb = S.alloc([128, NT, 8], F32, 'cosk')
    sink, skb = S.alloc([128, NT, 8], F32, 'sink')

    S.mark()
    posi, posib = S.alloc([128, NT], I32, 'posi')
    posf, posfb = S.alloc([128, NT], F32, 'posf')
    invt, invtb = S.alloc([128, 8], F32, 'invt')
    ang, angb = S.alloc([128, NT, 8], F32, 'ang')
    tq, tqb = S.alloc([128, NT, 8], F32, 'tq')
    tn, tnb = S.alloc([128, NT, 8], F32, 'tn')
    tr, trb = S.alloc([128, NT, 8], F32, 'tr')
    S.dma('sp', posi, G['pos_in'].rearrange("(c p) -> p c", p=128), writes=[posib], allow_slow_non_contiguous=True)
    S.dma('sp', invt, G['invf'].partition_broadcast(128), writes=[invtb])
    S.op('dve', lambda e: e.tensor_copy(out=posf, in_=posi), reads=[posib], writes=[posfb])
    S.op('dve', lambda e: e.tensor_tensor(out=ang, in0=posf.unsqueeze(2).to_broadcast([128, NT, 8]),
                                          in1=invt.unsqueeze(1).to_broadcast([128, NT, 8]), op=ALU.mult),
         reads=[posfb, invtb], writes=[angb])
    TWO_PI = 2.0 * math.pi
    C1 = 6.28125
    C2 = TWO_PI - C1
    MAGIC = 12582912.0

    def sin_of(dst, dstb, shift, scale):
        S.op('dve', lambda e: e.tensor_scalar(out=tq, in0=ang, scalar1=shift, scalar2=1.0 / TWO_PI, op0=ALU.add,
                                              op1=ALU.mult), reads=[angb], writes=[tqb])
        S.op('dve', lambda e: e.tensor_scalar(out=tn, in0=tq, scalar1=MAGIC, scalar2=None, op0=ALU.add),
             reads=[tqb], writes=[tnb])
        S.op('dve', lambda e: e.tensor_scalar(out=tn, in0=tn, scalar1=MAGIC, scalar2=None, op0=ALU.subtract),
             reads=[tnb], writes=[tnb])
        S.op('dve', lambda e: e.scalar_tensor_tensor(out=tr, in0=tn, scalar=-C1, in1=ang, op0=ALU.mult, op1=ALU.add),
             reads=[tnb, angb], writes=[trb])
        S.op('dve', lambda e: e.tensor_scalar(out=tr, in0=tr, scalar1=shift, scalar2=None, op0=ALU.add),
             reads=[trb], writes=[trb])
        S.op('dve', lambda e: e.scalar_tensor_tensor(out=tr, in0=tn, scalar=-C2, in1=tr, op0=ALU.mult, op1=ALU.add),
             reads=[tnb, trb], writes=[trb])
        S.op('dve', lambda e: e.tensor_scalar(out=tr, in0=tr, scalar1=3.1415925, scalar2=-3.1415925, op0=ALU.min,
                                              op1=ALU.max), reads=[trb], writes=[trb])
        S.op('act', lambda e: e.activation(out=dst, in_=tr, func=AF.Sin), reads=[trb], writes=[dstb])
        if scale != 1.0:
            pass

    sin_of(sink, skb, 0.0, 1.0)
    sin_of(cosk, ckb, math.pi / 2, 1.0)
    S.op('dve', lambda e: e.tensor_scalar(out=sinq, in0=sink, scalar1=0.125, scalar2=None, op0=ALU.mult),
         reads=[skb], writes=[sqb])
    S.op('dve', lambda e: e.tensor_scalar(out=cosq, in0=cosk, scalar1=0.125, scalar2=None, op0=ALU.mult),
         reads=[ckb], writes=[cqb])

    gt, gtb = S.alloc([128, DM], F32, 'gt')
    S.dma('sp', gt, norm_g.partition_broadcast(128), writes=[gtb])
    xb_ = [S.alloc([128, DM], F32, 'xt%d' % i) for i in range(2)]
    hb_ = [S.alloc([128, DM], BF16, 'hb%d' % i) for i in range(2)]
    junk, junkb = S.alloc([128, DM], BF16, 'junk')
    ss_ = [S.alloc([128, 1], F32, 'ss%d' % i) for i in range(2)]
    for tt in range(NT):
        xt, xtb = xb_[tt % 2]
        hb, hbb = hb_[tt % 2]
        ss, ssb = ss_[tt % 2]
        S.dma('sp', xt, x_in[tt * 128:(tt + 1) * 128, :], writes=[xtb])
        S.op('act', lambda e, xt=xt, ss=ss: e.activation(out=junk, in_=xt, func=AF.Square, accum_out=ss),
             reads=[xtb], writes=[junkb, ssb])
        S.op('dve', lambda e, ss=ss: e.tensor_scalar(out=ss, in0=ss, scalar1=1.0 / DM, scalar2=EPS, op0=ALU.mult,
                                                     op1=ALU.add), reads=[ssb], writes=[ssb])
        S.op('act', lambda e, ss=ss: e.activation(out=ss, in_=ss, func=AF.Sqrt), reads=[ssb], writes=[ssb])
        S.op('dve', lambda e, ss=ss: e.reciprocal(out=ss, in_=ss), reads=[ssb], writes=[ssb])
        S.op('dve', lambda e, xt=xt, ss=ss, hb=hb: e.scalar_tensor_tensor(out=hb, in0=xt, scalar=ss, in1=gt,
                                                                          op0=ALU.mult, op1=ALU.mult),
             reads=[xtb, ssb, gtb], writes=[hbb])
        for half in range(2):
            bank = 4 + half
            pv = transposes_to(bank, [hb[:, (half * 8 + i) * 128:(half * 8 + i + 1) * 128] for i in range(8)], [hbb])
            src = pv[:, 0:1024].rearrange("p (c t) -> p c t", c=8)
            dst = hT[:, half * 8:(half + 1) * 8, tt * 128:(tt + 1) * 128]
            if half == 0:
                S.op('act', lambda e, src=src, dst=dst: e.copy(out=dst, in_=src), reads=[S.psbuf[bank]], writes=[hTb])
            else:
                S.op('dve', lambda e, src=src, dst=dst: e.tensor_copy(out=dst, in_=src), reads=[S.psbuf[bank]],
                     writes=[hTb])
    S.barrier()
    S.release()
    if getattr(cfg, 'sub', '') == 'A0':
        S.release()
        return

    S.mark()
    qkT, w_fm = G['qkT'], G['w_fm']
    cw, cwb = S.alloc([128, NFC, 4], F32, 'cw')
    cb, cbb = S.alloc([128, NFC, 1], F32, 'cb')
    S.dma('sp', cw, G['convw'], writes=[cwb])
    S.dma('sp', cb, G['convb'], writes=[cbb])
    wfc_ = [S.alloc([128, 16, 128], BF16, 'wfc%d' % i) for i in range(2)]
    stage_ = [S.alloc([128, 515], F32, 'stage%d' % i) for i in range(2)]
    acc_ = [S.alloc([128, 512], F32, 'acc%d' % i) for i in range(2)]
    sg_ = [S.alloc([128, 512], F32, 'sg%d' % i) for i in range(2)]
    ob_ = [S.alloc([128, 512], BF16, 'ob%d' % i) for i in range(2)]
    it = 0
    for fc in range(NFC):
        wt, wtb = wfc_[fc % 2]
        S.dma('pool', wt, w_fm[:, fc * 128:(fc + 1) * 128].rearrange("(c p) n -> p c n", p=128), writes=[wtb])
        scale = 1.0 if fc < 2 * NH else 1.0 / 16.0
        s0, s0b = stage_[0]
        S.op('dve', lambda e, s0=s0: e.memset(s0[:, 0:3], 0.0), writes=[s0b])
        for tb in range(8):
            stg, stgb = stage_[tb % 2]
            acc, accb = acc_[it % 2]
            sg, sgb = sg_[it % 2]
            ob, obb = ob_[it % 2]
            bank = it % 2
            it += 1
            ps = S.ps(bank)
            for kc in range(16):
                S.op('pe', lambda e, ps=ps, wt=wt, kc=kc, tb=tb: e.matmul(
                    ps, lhsT=wt[:, kc, :], rhs=hT[:, kc, tb * 512:(tb + 1) * 512], start=(kc == 0), stop=(kc == 15)),
                    reads=[wtb, hTb], writes=[S.psbuf[bank]])
            S.op('act', lambda e, stg=stg, ps=ps: e.copy(out=stg[:, 3:515], in_=ps), reads=[S.psbuf[bank]],
                 writes=[stgb])
            S.op('dve', lambda e, acc=acc, stg=stg, fc=fc: e.tensor_scalar(
                out=acc, in0=stg[:, 3:515], scalar1=cw[:, fc, 3:4], scalar2=cb[:, fc, 0:1], op0=ALU.mult,
                op1=ALU.add), reads=[stgb, cwb, cbb], writes=[accb])
            for j in range(3):
                S.op('dve', lambda e, acc=acc, stg=stg, fc=fc, j=j: e.scalar_tensor_tensor(
                    out=acc, in0=stg[:, j:j + 512], scalar=cw[:, fc, j:j + 1], in1=acc, op0=ALU.mult, op1=ALU.add),
                    reads=[stgb, cwb, accb], writes=[accb])
            if tb < 7:
                nstg, nstgb = stage_[(tb + 1) % 2]
                S.op('act', lambda e, nstg=nstg, stg=stg: e.copy(out=nstg[:, 0:3], in_=stg[:, 512:515]),
                     reads=[stgb], writes=[nstgb])
            S.op('act', lambda e, sg=sg, acc=acc: e.activation(out=sg, in_=acc, func=AF.Sigmoid), reads=[accb],
                 writes=[sgb])
            S.op('dve', lambda e, ob=ob, acc=acc, sg=sg, scale=scale: e.scalar_tensor_tensor(
                out=ob, in0=acc, scalar=scale, in1=sg, op0=ALU.mult, op1=ALU.mult), reads=[accb, sgb], writes=[obb])
            S.dma('sp', qkT[fc * 128:(fc + 1) * 128, tb * 512:(tb + 1) * 512], ob, reads=[obb])
    wif, wifb = S.alloc([128, 16, 2 * NH], BF16, 'wif')
    S.dma('pool', wif, G['w_if'].rearrange("(c p) n -> p c n", p=128), writes=[wifb])
    rows_ = [S.alloc([NH, 2, 512], F32, 'rows%d' % i) for i in range(2)]
    for tb in range(8):
        rw, rwb = rows_[tb % 2]
        for k2 in range(2):
            bank = 2 + k2
            ps = S.ps(bank)
            for kc in range(16):
                S.op('pe', lambda e, ps=ps, kc=kc, tb=tb, k2=k2: e.matmul(
                    ps[0:NH, :], lhsT=wif[:, kc, k2 * NH:(k2 + 1) * NH], rhs=hT[:, kc, tb * 512:(tb + 1) * 512],
                    start=(kc == 0), stop=(kc == 15)), reads=[wifb, hTb], writes=[S.psbuf[bank]])
            S.op('act', lambda e, rw=rw, ps=ps, k2=k2: e.copy(out=rw[:, k2, :], in_=ps[0:NH, :]),
                 reads=[S.psbuf[bank]], writes=[rwb])
        S.dma('sp', G['gi'][:, tb * 512:(tb + 1) * 512], rw[:, 0, :], reads=[rwb])
        S.dma('sp', G['gf'][:, tb * 512:(tb + 1) * 512], rw[:, 1, :], reads=[rwb])
    S.barrier()
    S.release()
    if getattr(cfg, 'sub', '') == 'A1':
        S.release()
        return

    S.mark()
    w_tm = G['w_tm']
    wblk_ = [S.alloc([128, 16, 512], BF16, 'wblk%d' % i) for i in range(2)]
    sgt_ = [S.alloc([128, 512], F32, 'sgt%d' % i) for i in range(2)]
    t1_ = [S.alloc([128, 256], F32, 't1%d' % i) for i in range(2)]
    o_ = [S.alloc([128, 512], BF16, 'o%d' % i) for i in range(3)]
    ng_ = [S.alloc([128, 24], F32, 'ng%d' % i) for i in range(2)]
    rt_ = [S.alloc([128, 4, 8, 8], F32, 'rt%d' % i) for i in range(2)]
    stgT_ = [S.alloc([128, 4, 512], BF16, 'stgT%d' % i) for i in range(2)]
    it = 0
    for bi, (kind, idx, c0, W) in enumerate(cfg.blocks):
        if getattr(cfg, 'kinds', None) and kind not in cfg.kinds:
            continue
        wt, wtb = wblk_[bi % 2]
        S.dma('pool', wt[:, :, 0:W], w_tm[:, c0:c0 + W].rearrange("(c p) n -> p c n", p=128), writes=[wtb])
        for tt in range(NT):
            bank = it % 4
            ps = S.ps(bank)
            psb = S.psbuf[bank]
            o, ob = o_[it % 3]
            sgt, sgtb = sgt_[it % 2]
            t1, t1b = t1_[it % 2]
            it += 1
            rows = slice(tt * 128, (tt + 1) * 128)
            for kc in range(16):
                S.op('pe', lambda e, ps=ps, wt=wt, kc=kc, tt=tt, W=W: e.matmul(
                    ps[:, 0:W], lhsT=hT[:, kc, tt * 128:(tt + 1) * 128], rhs=wt[:, kc, 0:W], start=(kc == 0),
                    stop=(kc == 15)), reads=[wtb, hTb], writes=[psb])
            if kind == 'v_m':
                S.op('act', lambda e, o=o, ps=ps: e.copy(out=o, in_=ps), reads=[psb], writes=[ob])
                S.dma('sp', G['vm'][rows, idx * 512:(idx + 1) * 512], o, reads=[ob])
            elif kind == 'gate_m':
                S.op('act', lambda e, sgt=sgt, ps=ps: e.activation(out=sgt, in_=ps, func=AF.Sigmoid), reads=[psb],
                     writes=[sgtb])
                S.op('dve', lambda e, t1=t1, ps=ps, sgt=sgt: e.tensor_tensor(out=t1, in0=ps[:, 256:512],
                                                                             in1=sgt[:, 256:512], op=ALU.mult),
                     reads=[psb, sgtb], writes=[t1b])
                S.op('dve', lambda e, o=o, t1=t1, sgt=sgt: e.tensor_tensor(out=o[:, 0:256], in0=t1,
                                                                            in1=sgt[:, 0:256], op=ALU.mult),
                     reads=[t1b, sgtb], writes=[ob])
                S.dma('sp', G['gm'][rows, idx * 256:(idx + 1) * 256], o[:, 0:256], reads=[ob])
            elif kind in ('nq', 'nk'):
                nh = 8 if kind == 'nq' else 6
                ct, ctb, sn, snb = (cosk, ckb, sink, skb)
                sc = 0.125 if kind == 'nq' else 1.0
                rt, rtb = rt_[tt % 2]
                S.op('act', lambda e, sgt=sgt, ps=ps, sc=sc: e.activation(out=sgt, in_=ps, func=AF.Copy, scale=sc),
                     reads=[psb], writes=[sgtb])
                S.op('act', lambda e, o=o, sgt=sgt: e.copy(out=o, in_=sgt), reads=[sgtb], writes=[ob])
                ps3 = sgt.rearrange("p (h d) -> p h d", d=64)
                o3 = o.rearrange("p (h d) -> p h d", d=64)
                x1 = ps3[:, 0:nh, 0:8]
                x2 = ps3[:, 0:nh, 8:16]
                cbv = ct[:, tt:tt + 1, :].to_broadcast([128, nh, 8])
                sbv = sn[:, tt:tt + 1, :].to_broadcast([128, nh, 8])
                for k4, (a, bv) in enumerate([(x1, cbv), (x2, sbv), (x2, cbv), (x1, sbv)]):
                    S.op('dve', lambda e, rt=rt, k4=k4, a=a, bv=bv, nh=nh: e.tensor_tensor(
                        out=rt[:, k4, 0:nh, :], in0=a, in1=bv, op=ALU.mult), reads=[sgtb, ctb, snb], writes=[rtb])
                S.op('dve', lambda e, o3=o3, rt=rt, nh=nh: e.tensor_tensor(
                    out=o3[:, 0:nh, 0:8], in0=rt[:, 0, 0:nh, :], in1=rt[:, 1, 0:nh, :], op=ALU.subtract),
                    reads=[rtb], writes=[ob])
                S.op('dve', lambda e, o3=o3, rt=rt, nh=nh: e.tensor_tensor(
                    out=o3[:, 0:nh, 8:16], in0=rt[:, 2, 0:nh, :], in1=rt[:, 3, 0:nh, :], op=ALU.add),
                    reads=[rtb], writes=[ob])
                tbank = 4 + (tt % 2)
                pv = transposes_to(tbank, [o[:, j * 128:(j + 1) * 128] for j in range(4)], [ob])
                stg, stgb = stgT_[(tt // 4) % 2]
                q4 = tt % 4
                src = pv[:, 0:512].rearrange("p (j t) -> p j t", j=4)
                dst = stg[:, :, q4 * 128:(q4 + 1) * 128]
                if tt % 2 == 0:
                    S.op('act', lambda e, src=src, dst=dst: e.copy(out=dst, in_=src), reads=[S.psbuf[tbank]],
                         writes=[stgb])
                else:
                    S.op('dve', lambda e, src=src, dst=dst: e.tensor_copy(out=dst, in_=src), reads=[S.psbuf[tbank]],
                         writes=[stgb])
                if q4 == 3:
                    tb = tt // 4
                    if kind == 'nq':
                        dd = G['qTn'][idx].rearrange("(j p) t -> p j t", p=128)[:, :, tb * 512:(tb + 1) * 512]
                    else:
                        dd = G['kT4'][idx].rearrange("j p t -> p j t")[:, :, tb * 512:(tb + 1) * 512]
                    S.dma('sp', dd, stg, reads=[stgb])
            elif kind == 'nv':
                ngt, ngtb = ng_[tt % 2]
                S.op('act', lambda e, o=o, ps=ps: e.copy(out=o[:, 0:256], in_=ps[:, 0:256]), reads=[psb], writes=[ob])
                S.op('act', lambda e, ngt=ngt, ps=ps: e.activation(out=ngt, in_=ps[:, 256:280], func=AF.Sigmoid),
                     reads=[psb], writes=[ngtb])
                S.dma('sp', G['vsw'][idx][rows, :], o[:, 0:256], reads=[ob])
                S.dma('sp', G['ngs'][idx][rows, :], ngt, reads=[ngtb])
            elif kind == 'nz':
                S.op('act', lambda e, sgt=sgt, ps=ps: e.activation(out=sgt, in_=ps, func=AF.Sigmoid), reads=[psb],
                     writes=[sgtb])
                S.op('dve', lambda e, o=o, ps=ps, sgt=sgt: e.tensor_tensor(out=o, in0=ps, in1=sgt, op=ALU.mult),
                     reads=[psb, sgtb], writes=[ob])
                S.dma('sp', G['szn'][idx][rows, :], o, reads=[ob])
    S.barrier()
    S.release()
    S.release()


def rstd_inplace(S, ss, ssb, n, eps=EPS):
    S.op('dve', lambda e: e.tensor_scalar(out=ss, in0=ss, scalar1=1.0 / n, scalar2=eps, op0=ALU.mult, op1=ALU.add),
         reads=[ssb], writes=[ssb])
    S.op('act', lambda e: e.activation(out=ss, in_=ss, func=AF.Sqrt), reads=[ssb], writes=[ssb])
    S.op('dve', lambda e: e.reciprocal(out=ss, in_=ss), reads=[ssb], writes=[ssb])


def phase_B(S, cfg, G):
    NH = cfg.NH
    identf, identfb = G['identf'], G['identfb']
    transposes_to = G['transposes_to']
    qkT, vm, gm, mixT = G['qkT'], G['vm'], G['gm'], G['mixT']
    S.mark()
    cols, colsb = S.alloc([128, NT, 3, NH], F32, 'cols')
    decb, decbb = S.alloc([128, NH * NT], F32, 'decb')
    cm01, cm01b = S.alloc([128, 128], F32, 'cm01')
    S.op('pool', lambda e: e.memset(cm01, 1.0), writes=[cm01b])
    S.op('pool', lambda e: e.affine_select(out=cm01, in_=cm01, pattern=[[1, 128]], compare_op=ALU.is_ge, fill=0.0,
                                            base=0, channel_multiplier=-1), reads=[cm01b], writes=[cm01b])
    S.mark()
    T = [S.alloc([NH, SEQ], F32, 'T%d' % i) for i in range(8)]
    (T1, T1b), (T2, T2b), (T3, T3b), (T4, T4b), (T5, T5b), (T6, T6b), (T7, T7b), (T8, T8b) = T
    bi, bib = S.alloc([NH, 1], F32, 'bi')
    bf_, bfb = S.alloc([NH, 1], F32, 'bf')
    bs, bsb = S.alloc([NH, NT], F32, 'bs')
    be, beb = S.alloc([NH, NT], F32, 'be')
    dec, decrb = S.alloc([NH, NT], F32, 'dec')
    sel, selb = S.alloc([NH, NH, 128], F32, 'sel')
    S.dma('sp', T1, G['gi'], writes=[T1b])
    S.dma('sp', T2, G['gf'], writes=[T2b])
    S.dma('sp', bi, G['b_i'], writes=[bib])
    S.dma('sp', bf_, G['b_f'], writes=[bfb])
    S.op('dve', lambda e: e.tensor_scalar(out=bf_, in0=bf_, scalar1=-1.0, scalar2=None, op0=ALU.mult), reads=[bfb],
         writes=[bfb])
    S.op('dve', lambda e: e.tensor_scalar(out=T1, in0=T1, scalar1=bi, scalar2=None, op0=ALU.add), reads=[T1b, bib],
         writes=[T1b])
    S.op('act', lambda e: e.activation(out=T2, in_=T2, func=AF.Exp, scale=-1.0, bias=bf_), reads=[T2b, bfb],
         writes=[T2b])
    S.op('act', lambda e: e.activation(out=T2, in_=T2, func=AF.Ln, bias=1.0), reads=[T2b], writes=[T2b])
    S.op('pool', lambda e: e.memset(T8, 1.0), writes=[T8b])
    S.op('dve', lambda e: e.tensor_tensor_scan(out=T3, data0=T8, data1=T2, initial=0.0, op0=ALU.mult, op1=ALU.add),
         reads=[T8b, T2b], writes=[T3b])
    S.op('dve', lambda e: e.tensor_tensor(out=T1, in0=T1, in1=T3, op=ALU.add), reads=[T1b, T3b], writes=[T1b])
    S.op('dve', lambda e: e.tensor_tensor_scan(out=T4, data0=T1, data1=T1, initial=0.0, op0=ALU.max, op1=ALU.max),
         reads=[T1b], writes=[T4b])
    T3v = T3.rearrange("p (c l) -> p c l", l=128)
    T1v = T1.rearrange("p (c l) -> p c l", l=128)
    T4v = T4.rearrange("p (c l) -> p c l", l=128)
    S.op('dve', lambda e: e.memset(bs, 0.0), writes=[bsb])
    S.op('dve', lambda e: e.tensor_copy(out=bs[:, 1:NT], in_=T3v[:, 0:NT - 1, 127]), reads=[T3b, bsb], writes=[bsb])
    S.op('dve', lambda e: e.tensor_copy(out=be, in_=T3v[:, :, 127]), reads=[T3b], writes=[beb])
    bsB = bs.unsqueeze(2).to_broadcast([NH, NT, 128])
    beB = be.unsqueeze(2).to_broadcast([NH, NT, 128])
    for (dst, dstb, src, srcb, bb, bbb) in [(T5, T5b, T1v, T1b, bsB, bsb), (T6, T6b, T1v, T1b, beB, beb),
                                            (T7, T7b, T4v, T4b, bsB, bsb)]:
        dv = dst.rearrange("p (c l) -> p c l", l=128)
        S.op('dve', lambda e, dv=dv, src=src, bb=bb: e.tensor_tensor(out=dv, in0=src, in1=bb, op=ALU.subtract),
             reads=[srcb, bbb], writes=[dstb])
        S.op('act', lambda e, dst=dst: e.activation(out=dst, in_=dst, func=AF.Exp), reads=[dstb], writes=[dstb])
    S.op('dve', lambda e: e.tensor_tensor(out=dec, in0=bs, in1=be, op=ALU.subtract), reads=[bsb, beb], writes=[decrb])
    S.op('act', lambda e: e.activation(out=dec, in_=dec, func=AF.Exp), reads=[decrb], writes=[decrb])
    for c in range(NT):
        bank = 6 + (c % 2)
        pv = transposes_to(bank, [Tq[:, c * 128:(c + 1) * 128] for Tq in (T5, T6, T7)], [T5b, T6b, T7b], dtype=F32,
                           idn=identf, idb=identfb, width=NH)
        S.op('act' if c % 2 else 'dve',
             (lambda e, c=c, pv=pv: e.copy(out=cols[:, c, :, :], in_=pv[:, 0:3 * NH].rearrange("p (a h) -> p a h", a=3)))
             if c % 2 else
             (lambda e, c=c, pv=pv: e.tensor_copy(out=cols[:, c, :, :],
                                                  in_=pv[:, 0:3 * NH].rearrange("p (a h) -> p a h", a=3))),
             reads=[S.psbuf[bank]], writes=[colsb])
    S.op('pool', lambda e: e.memset(sel, 1.0), writes=[selb])
    S.op('pool', lambda e: e.affine_select(out=sel, in_=sel, pattern=[[-1, NH], [0, 128]], compare_op=ALU.is_equal,
                                            fill=0.0, base=0, channel_multiplier=1), reads=[selb], writes=[selb])
    pdec = S.ps(5)
    for h in range(NH):
        S.op('pe', lambda e, h=h: e.matmul(pdec[:, h * NT:(h + 1) * NT], lhsT=sel[:, h, :], rhs=dec, start=True,
                                           stop=True), reads=[selb, decrb], writes=[S.psbuf[5]])
    S.op('dve', lambda e: e.tensor_copy(out=decb, in_=pdec[:, 0:NH * NT]), reads=[S.psbuf[5]], writes=[decbb])
    S.barrier()
    S.release()

    S.mark()
    Sst, Sstb = S.alloc([128, 2, 257], F32, 'Sst')
    Sbf, Sbfb = S.alloc([128, 2, 257], BF16, 'Sbf')
    hd_ = []
    for i in range(2):
        hd_.append(dict(q=S.alloc([128, 2, SEQ], BF16, 'qTh%d' % i), k=S.alloc([128, 2, SEQ], BF16, 'kTh%d' % i),
                        v=S.alloc([128, NT, 257], BF16, 'vh%d' % i), g=S.alloc([128, NT, 256], BF16, 'gmh%d' % i),
                        n=S.alloc([128, 256], F32, 'mng%d' % i)))
    kt_ = [S.alloc([128, 256], BF16, 'kt%d' % i) for i in range(2)]
    wv_ = [S.alloc([128, 257], BF16, 'wv%d' % i) for i in range(2)]
    PT_ = [S.alloc([128, 128], BF16, 'PT%d' % i) for i in range(2)]
    dd_ = [S.alloc([128, 4], F32, 'dd%d' % i) for i in range(2)]
    hm_ = [S.alloc([128, 256], F32, 'hm%d' % i) for i in range(2)]
    st6_ = [S.alloc([128, 8], F32, 'st6%d' % i) for i in range(2)]
    y_ = [S.alloc([128, 256], F32, 'y%d' % i) for i in range(2)]
    mo_ = [S.alloc([128, 256], BF16, 'mo%d' % i) for i in range(2)]
    stgM_ = [S.alloc([128, 2, 512], BF16, 'stgM%d' % i) for i in range(2)]
    for hi in range(NH):
        H = hd_[hi % 2]
        (qT, qTb), (kT, kTb), (vh, vhb), (gh, ghb), (mn, mnb) = H['q'], H['k'], H['v'], H['g'], H['n']
        S.dma('sp', qT, qkT[hi * 256:(hi + 1) * 256, :].rearrange("(j p) t -> p j t", p=128), writes=[qTb])
        S.dma('sp', kT, qkT[(NH + hi) * 256:(NH + hi + 1) * 256, :].rearrange("(j p) t -> p j t", p=128), writes=[kTb])
        for c4 in range(4):
            S.dma('sp', vh[:, c4 * 8:(c4 + 1) * 8, 0:256],
                  vm[c4 * 1024:(c4 + 1) * 1024, hi * 256:(hi + 1) * 256].rearrange("(c p) d -> p c d", p=128),
                  writes=[vhb])
            S.dma('sp', gh[:, c4 * 8:(c4 + 1) * 8, :],
                  gm[c4 * 1024:(c4 + 1) * 1024, hi * 256:(hi + 1) * 256].rearrange("(c p) d -> p c d", p=128),
                  writes=[ghb])
        S.dma('sp', mn, G['mng'][hi * 256:(hi + 1) * 256].partition_broadcast(128), writes=[mnb])
        S.op('dve', lambda e, vh=vh: e.memset(vh[:, :, 256:257], 1.0), writes=[vhb])
        S.op('dve', lambda e: e.memset(Sst, 0.0), writes=[Sstb])
        S.op('dve', lambda e: e.memset(Sbf, 0.0), writes=[Sbfb])
        for c in range(NT):
            cs = slice(c * 128, (c + 1) * 128)
            kt, ktb = kt_[c % 2]
            wv, wvb = wv_[c % 2]
            PT, PTb = PT_[c % 2]
            dd, ddb = dd_[c % 2]
            hm, hmb = hm_[c % 2]
            st6, st6b = st6_[c % 2]
            y, yb = y_[c % 2]
            mo, mob = mo_[c % 2]
            stg, stgb = stgM_[(c // 4) % 2]
            pv = transposes_to(6, [kT[:, j, cs] for j in range(2)], [kTb])
            S.op('act', lambda e, kt=kt, pv=pv: e.copy(out=kt, in_=pv[:, 0:256]), reads=[S.psbuf[6]], writes=[ktb])
            S.op('dve', lambda e, wv=wv, vh=vh, c=c, hi=hi: e.tensor_scalar(
                out=wv, in0=vh[:, c, :], scalar1=cols[:, c, 1, hi:hi + 1], scalar2=None, op0=ALU.mult),
                reads=[vhb, colsb], writes=[wvb])
            b_s = c % 2
            psS = S.ps(b_s)
            for j in range(2):
                S.op('pe', lambda e, psS=psS, kT=kT, qT=qT, j=j, cs=cs: e.matmul(
                    psS[:, 0:128], lhsT=kT[:, j, cs], rhs=qT[:, j, cs], start=(j == 0), stop=(j == 1)),
                    reads=[kTb, qTb], writes=[S.psbuf[b_s]])
            S.op('dve', lambda e, PT=PT, psS=psS, c=c, hi=hi: e.scalar_tensor_tensor(
                out=PT, in0=psS[:, 0:128], scalar=cols[:, c, 0, hi:hi + 1], in1=cm01, op0=ALU.mult, op1=ALU.mult),
                reads=[S.psbuf[b_s], colsb, cm01b], writes=[PTb])
            b_a = 2 + (c % 2)
            pa = S.ps(b_a)
            S.op('pe', lambda e, pa=pa, PT=PT, vh=vh, c=c: e.matmul(pa[:, 0:257], lhsT=PT, rhs=vh[:, c, :], start=True,
                                                                   stop=False),
                 reads=[PTb, vhb], writes=[S.psbuf[b_a]])
            for j in range(2):
                S.op('pe', lambda e, pa=pa, qT=qT, j=j, cs=cs: e.matmul(pa[:, 0:257], lhsT=qT[:, j, cs], rhs=Sbf[:, j, :],
                                                                       start=False, stop=(j == 1)),
                     reads=[qTb, Sbfb], writes=[S.psbuf[b_a]])
            S.op('act', lambda e, dd=dd, pa=pa: e.activation(out=dd[:, 3:4], in_=pa[:, 256:257], func=AF.Abs),
                 reads=[S.psbuf[b_a]], writes=[ddb])
            S.op('dve', lambda e, dd=dd, c=c, hi=hi: e.tensor_scalar(
                out=dd[:, 0:1], in0=dd[:, 3:4], scalar1=cols[:, c, 2, hi:hi + 1], scalar2=None, op0=ALU.max),
                reads=[ddb, colsb], writes=[ddb])
            S.op('dve', lambda e, dd=dd: e.reciprocal(out=dd[:, 0:1], in_=dd[:, 0:1]), reads=[ddb], writes=[ddb])
            S.op('act', lambda e, hm=hm, pa=pa, dd=dd: e.activation(out=hm, in_=pa[:, 0:256], func=AF.Copy,
                                                                   scale=dd[:, 0:1]),
                 reads=[S.psbuf[b_a], ddb], writes=[hmb])
            S.op('dve', lambda e, st6=st6, hm=hm: e.bn_stats(out=st6[:, 0:6], in_=hm), reads=[hmb], writes=[st6b])
            S.op('dve', lambda e, st6=st6: e.bn_aggr(out=st6[:, 6:8], in_=st6[:, 0:6]), reads=[st6b], writes=[st6b])
            S.op('dve', lambda e, dd=dd, st6=st6: e.tensor_scalar(out=dd[:, 1:2], in0=st6[:, 7:8], scalar1=EPS,
                                                                 scalar2=None, op0=ALU.add),
                 reads=[st6b], writes=[ddb])
            S.op('act', lambda e, dd=dd: e.activation(out=dd[:, 1:2], in_=dd[:, 1:2], func=AF.Sqrt), reads=[ddb],
                 writes=[ddb])
            S.op('dve', lambda e, dd=dd: e.reciprocal(out=dd[:, 1:2], in_=dd[:, 1:2]), reads=[ddb], writes=[ddb])
            S.op('dve', lambda e, dd=dd, st6=st6: e.scalar_tensor_tensor(
                out=dd[:, 2:3], in0=st6[:, 6:7], scalar=-1.0, in1=dd[:, 1:2], op0=ALU.mult, op1=ALU.mult),
                reads=[st6b, ddb], writes=[ddb])
            S.op('act', lambda e, y=y, hm=hm, dd=dd: e.activation(out=y, in_=hm, func=AF.Identity, scale=dd[:, 1:2],
                                                                 bias=dd[:, 2:3]),
                 reads=[hmb, ddb], writes=[yb])
            S.op('dve', lambda e, y=y, mn=mn: e.tensor_tensor(out=y, in0=y, in1=mn, op=ALU.mult), reads=[yb, mnb],
                 writes=[yb])
            S.op('dve', lambda e, mo=mo, y=y, gh=gh, c=c: e.tensor_tensor(out=mo, in0=y, in1=gh[:, c, :], op=ALU.mult),
                 reads=[yb, ghb], writes=[mob])
            pv2 = transposes_to(7, [mo[:, j * 128:(j + 1) * 128] for j in range(2)], [mob])
            q4 = c % 4
            S.op('act', lambda e, stg=stg, pv2=pv2, q4=q4: e.copy(
                out=stg[:, :, q4 * 128:(q4 + 1) * 128], in_=pv2[:, 0:256].rearrange("p (j t) -> p j t", j=2)),
                reads=[S.psbuf[7]], writes=[stgb])
            if q4 == 3:
                c4 = c // 4
                S.dma('sp', mixT[hi * 256:(hi + 1) * 256, :].rearrange("(j p) t -> p j t", p=128)[
                    :, :, c4 * 512:(c4 + 1) * 512], stg, reads=[stgb])
            for j in range(2):
                pd = S.ps(4 + j)
                S.op('pe', lambda e, pd=pd, kt=kt, wv=wv, j=j: e.matmul(pd[:, 0:257], lhsT=kt[:, j * 128:(j + 1) * 128],
                                                                       rhs=wv, start=True, stop=True),
                     reads=[ktb, wvb], writes=[S.psbuf[4 + j]])
                S.op('dve', lambda e, pd=pd, j=j, c=c, hi=hi: e.scalar_tensor_tensor(
                    out=Sst[:, j, :], in0=Sst[:, j, :], scalar=decb[:, hi * NT + c:hi * NT + c + 1], in1=pd[:, 0:257],
                    op0=ALU.mult, op1=ALU.add), reads=[Sstb, decbb, S.psbuf[4 + j]], writes=[Sstb])
            S.op('act', lambda e: e.copy(out=Sbf, in_=Sst), reads=[Sstb], writes=[Sbfb])
    S.barrier()
    S.release()
    S.release()


def phase_C(S, cfg, G):
    NH, NP = cfg.NH, cfg.NP
    ident, identb = G['ident'], G['identb']
    transposes_to = G['transposes_to']
    mixT = G['mixT']
    S.mark()
    caus, causb = S.alloc([128, 128], F32, 'caus')
    onesf, onesfb = S.alloc([128, 128], F32, 'onesf')
    caus4, caus4b = S.alloc([128, 4, 128], BF16, 'caus4')
    anti4, anti4b = S.alloc([128, 4, 128], BF16, 'anti4')
    E, Eb = S.alloc([128, SEQ], BF16, 'E')
    S.op('pool', lambda e: e.memset(onesf, 1.0), writes=[onesfb])
    S.op('pool', lambda e: e.affine_select(out=caus, in_=onesf, pattern=[[1, 128]], compare_op=ALU.is_ge, fill=0.0,
                                            base=0, channel_multiplier=-1), reads=[onesfb], writes=[causb])
    S.op('dve', lambda e: e.tensor_copy(out=caus4, in_=caus.unsqueeze(1).to_broadcast([128, 4, 128])), reads=[causb],
         writes=[caus4b])
    S.op('dve', lambda e: e.tensor_scalar(out=anti4, in0=caus4, scalar1=-1.0, scalar2=1.0, op0=ALU.mult, op1=ALU.add),
         reads=[caus4b], writes=[anti4b])
    S.mark()
    Ef, Efb = S.alloc([128, SEQ], F32, 'Ef')
    Eg, Egb = S.alloc([128, SEQ], F32, 'Eg')
    for (Et, Etb, sh) in ((Ef, Efb, 0), (Eg, Egb, SEQ)):
        S.op('pool', lambda e, Et=Et: e.memset(Et, 1.0), writes=[Etb])
        S.op('pool', lambda e, Et=Et, sh=sh: e.affine_select(out=Et, in_=Et, pattern=[[1, SEQ]], compare_op=ALU.is_ge,
                                                             fill=0.0, base=sh, channel_multiplier=-64),
             reads=[Etb], writes=[Etb])
        S.op('pool', lambda e, Et=Et, sh=sh: e.affine_select(out=Et, in_=Et, pattern=[[-1, SEQ]], compare_op=ALU.is_ge,
                                                             fill=0.0, base=63 - sh, channel_multiplier=64),
             reads=[Etb], writes=[Etb])
    S.op('dve', lambda e: e.tensor_tensor(out=E, in0=Ef, in1=Eg, op=ALU.add), reads=[Efb, Egb], writes=[Eb])
    S.barrier()
    S.release()
    selb_t, selb_tb = S.alloc([128, NT, 64], F32, 'selb')
    S.dma('sp', selb_t, G['selbias'], writes=[selb_tb])
    ovt, ovtb = S.alloc([128, 2, 65], F32, 'ovt')
    S.dma('sp', ovt, G['ovl'].rearrange("(c p) n -> p c n", p=128), writes=[ovtb])
    w1 = {}
    w2 = {}
    pos = {}
    for nm, a1, a2, ap_ in (('k', G['ckw1'], G['ckw2'], G['posk']), ('v', G['cvw1'], G['cvw2'], G['posv'])):
        w1[nm] = S.alloc([128, 32, 128], BF16, 'w1' + nm)
        w2[nm] = S.alloc([128, 128], BF16, 'w2' + nm)
        pos[nm] = S.alloc([128, 32], BF16, 'pos' + nm)
        for hf in range(2):
            S.dma('pool', w1[nm][0][hf * 64:(hf + 1) * 64], a1, writes=[w1[nm][1]])
            S.dma('pool', pos[nm][0][hf * 64:(hf + 1) * 64], ap_, writes=[pos[nm][1]])
            S.dma('pool', w2[nm][0][:, hf * 64:(hf + 1) * 64], a2, writes=[w2[nm][1]])
    qTp, qTpb = S.alloc([128, 4, SEQ], BF16, 'qTp')
    k4, k4b = S.alloc([128, 4, SEQ], BF16, 'k4')
    vsw_t, vsw_tb = S.alloc([128, NT, 4, 65], BF16, 'vsw_t')
    ng_t, ng_tb = S.alloc([128, NT, 24], F32, 'ng_t')
    Rg, _ = S.alloc([128, 16384], BF16, 'Rg')
    kg1 = (Rg[:, 0:8192].rearrange("p (l c) -> p l c", l=32), Buf('kg'))
    kg = {'k': kg1, 'v': kg1}
    Pall = Rg.rearrange("p (k j t) -> p k j t", k=32, j=4)
    Pallb = [Buf('Pall%d' % i) for i in range(32)]
    Pw, _ = S.alloc([128, 5, 4, 128], BF16, 'Pw')
    Pwb = [Buf('Pw%d' % i) for i in range(5)]
    kcT, kcTb = S.alloc([128, 256], BF16, 'kcT')
    vca, vcab = S.alloc([128, 2, 2, 129], BF16, 'vca')
    biasc, biascb = S.alloc([128, 1], F32, 'biasc')
    xg, xgb = S.alloc([128, 256], F32, 'xg')
    x2, x2b = S.alloc([128, 256], F32, 'x2')
    gl, glb = S.alloc([128, 256], BF16, 'gl')
    sz_ = [S.alloc([128, 512], BF16, 'sz%d' % i) for i in range(2)]
    P_ = [S.alloc([128, 4, 128], BF16, 'P%d' % i) for i in range(4)]
    mk_ = [S.alloc([128, 128], F32, 'mk%d' % i) for i in range(2)]
    mkh_ = [S.alloc([128, 128], BF16, 'mkh%d' % i) for i in range(2)]
    rs_ = [S.alloc([128, 16], F32, 'rs%d' % i) for i in range(2)]
    imp_ = [S.alloc([128, 64], F32, 'imp%d' % i) for i in range(2)]
    sc2_ = [S.alloc([128, 64], F32, 'sc2%d' % i) for i in range(2)]
    m8_ = [S.alloc([128, 16], F32, 'm8%d' % i) for i in range(2)]
    nm_ = [S.alloc([128, 128], BF16, 'nm%d' % i) for i in range(2)]
    nmT1_ = [S.alloc([128, 128], BF16, 'nmT1%d' % i) for i in range(2)]
    nmT4_ = [S.alloc([128, 4, 128], BF16, 'qn%d' % i) for i in range(2)]
    ksE_ = [S.alloc([128, SEQ], BF16, 'ksE%d' % i) for i in range(2)]
    nacc_ = [S.alloc([128, 4, 64], F32, 'nacc%d' % i) for i in range(2)]
    no_ = [S.alloc([128, 256], BF16, 'no%d' % i) for i in range(2)]
    stgN_ = [S.alloc([128, 4, 512], BF16, 'stgN%d' % i) for i in range(2)]
    pit = 0
    for pi in range(NP):
        S.barrier()
        S.dma('sp', qTp, G['qTn'][pi].rearrange("(j p) t -> p j t", p=128), writes=[qTpb])
        S.dma('sp', k4, G['kT4'][pi].rearrange("j p t -> p j t"), writes=[k4b])
        for c4 in range(4):
            for a in range(4):
                S.dma('sp', vsw_t[:, c4 * 8:(c4 + 1) * 8, a, 0:64],
                      G['vsw'][pi][c4 * 1024:(c4 + 1) * 1024, a * 64:(a + 1) * 64].rearrange("(c p) d -> p c d", p=128),
                      writes=[vsw_tb])
        S.op('dve', lambda e: e.memset(vsw_t[:, :, :, 64:65], 1.0), writes=[vsw_tb])
        S.dma('sp', ng_t, G['ngs'][pi].rearrange("(c p) n -> p c n", p=128), writes=[ng_tb])
        for g2 in range(2):
            kst, kstb = ksE_[g2]
            ksl = slice(64 * g2, 64 * g2 + 64)
            esl = slice(64 * (1 - g2), 64 * (1 - g2) + 64)
            S.op('act', lambda e, kst=kst, ksl=ksl: e.copy(out=kst[ksl, :], in_=k4[ksl, 1, :]), reads=[k4b],
                 writes=[kstb])
            S.op('dve', lambda e, kst=kst, esl=esl: e.tensor_copy(out=kst[esl, :], in_=E[esl, :]), reads=[Eb],
                 writes=[kstb])

        for nm, srcidx in (('k', 0), ('v', 3)):
            k4v = k4[:, srcidx, :].rearrange("p (c s) -> p c s", s=16)
            for l in range(32):
                srcv = k4v[:, 0:255, l] if l < 16 else k4v[:, 1:256, l - 16]
                S.op('dve' if l % 2 else 'act',
                     (lambda e, l=l, srcv=srcv, nm=nm: e.tensor_copy(out=kg[nm][0][:, l, 0:255], in_=srcv)) if l % 2 else
                     (lambda e, l=l, srcv=srcv, nm=nm: e.copy(out=kg[nm][0][:, l, 0:255], in_=srcv)),
                     reads=[k4b], writes=[kg[nm][1]])
            for g2 in range(2):
                pb = 64 * g2
                w1t, w1b = w1[nm]
                w2t, w2b = w2[nm]
                pst, psb_ = pos[nm]
                pbias = S.ps(0)
                for l in range(32):
                    S.op('pe', lambda e, l=l, w1t=w1t, pst=pst, pb=pb: e.matmul(
                        pbias[:, 0:1], lhsT=w1t[pb:pb + 64, l, :], rhs=pst[pb:pb + 64, l:l + 1], start=(l == 0),
                        stop=(l == 31)), reads=[w1b, psb_], writes=[S.psbuf[0]])
                S.op('act', lambda e: e.copy(out=biasc, in_=pbias[:, 0:1]), reads=[S.psbuf[0]], writes=[biascb])
                phid = S.ps(1)
                for l in range(32):
                    S.op('pe', lambda e, l=l, w1t=w1t, pb=pb, srcidx=srcidx, nm=nm: e.matmul(
                        phid[:, 0:255], lhsT=w1t[pb:pb + 64, l, :], rhs=kg[nm][0][pb:pb + 64, l, 0:255],
                        start=(l == 0), stop=(l == 31)), reads=[w1b, kg[nm][1]], writes=[S.psbuf[1]])
                S.op('dve', lambda e: e.memset(xg, 0.0), writes=[xgb])
                S.op('act', lambda e: e.activation(out=xg[:, 0:255], in_=phid[:, 0:255], func=AF.Identity, bias=biasc),
                     reads=[S.psbuf[1], biascb, xgb], writes=[xgb])
                S.op('dve', lambda e: e.tensor_tensor(out=x2, in0=xg, in1=xg, op=ALU.mult), reads=[xgb], writes=[x2b])
                S.op('dve', lambda e: e.tensor_scalar(out=x2, in0=x2, scalar1=0.044715, scalar2=1.0, op0=ALU.mult,
                                                      op1=ALU.add), reads=[x2b], writes=[x2b])
                S.op('dve', lambda e: e.tensor_tensor(out=x2, in0=x2, in1=xg, op=ALU.mult), reads=[x2b, xgb], writes=[x2b])
                S.op('act', lambda e: e.activation(out=x2, in_=x2, func=AF.Sigmoid, scale=1.5957691216057308),
                     reads=[x2b], writes=[x2b])
                S.op('dve', lambda e: e.tensor_tensor(out=gl, in0=xg, in1=x2, op=ALU.mult), reads=[xgb, x2b], writes=[glb])
                if nm == 'k':
                    pk = S.ps(2)
                    S.op('pe', lambda e, w2t=w2t: e.matmul(pk[:, 0:256], lhsT=w2t, rhs=gl, start=True, stop=True),
                         reads=[w2b, glb], writes=[S.psbuf[2]])
                    S.op('act', lambda e, pb=pb: e.copy(out=kcT[pb:pb + 64, :], in_=pk[pb:pb + 64, 0:256]),
                         reads=[S.psbuf[2]], writes=[kcTb])
                else:
                    for ct in range(2):
                        pvv = S.ps(2)
                        S.op('pe', lambda e, w2t=w2t, ct=ct: e.matmul(pvv[:, 0:64], lhsT=gl[:, ct * 128:(ct + 1) * 128],
                                                                     rhs=w2t[:, 0:64], start=True, stop=True),
                             reads=[w2b, glb], writes=[S.psbuf[2]])
                        S.op('act', lambda e, g2=g2, ct=ct: e.copy(out=vca[:, g2, ct, 0:64], in_=pvv[:, 0:64]),
                             reads=[S.psbuf[2]], writes=[vcab])
                        S.op('dve', lambda e, g2=g2, ct=ct: e.tensor_copy(out=vca[:, g2, ct, 64:129], in_=ovt[:, ct, :]),
                             reads=[ovtb, vcab], writes=[vcab])
        base_row = NH * 256 + pi * 512
        lvl = getattr(cfg, 'clevel', 9)
        S.barrier()
        for qt in getattr(cfg, 'qts', range(NT)):
            szt, sztb = sz_[qt % 2]
            S.dma('sp', szt, G['szn'][pi][qt * 128:(qt + 1) * 128, :], writes=[sztb])
            stg, stgb = stgN_[(qt // 4) % 2]
            for g2 in range(2):
                pb = 64 * g2
                u = (qt * 2 + g2) % 2
                rs, rsb = rs_[u]
                imp, impb = imp_[u]
                sc2, sc2b = sc2_[u]
                m8, m8b = m8_[u]
                nmt, nmtb = nm_[u]
                nmT1, nmT1b = nmT1_[u]
                nmT4, nmT4b = nmT4_[u]
                nacc, naccb = nacc_[u]
                no, nob = no_[u]
                q4 = qTp[pb:pb + 64, :, qt * 128:(qt + 1) * 128]
                ngv = ng_t[:, qt, g2 * 12:(g2 + 1) * 12].rearrange("p (j b) -> p j b", b=3)
                nct = 1 if qt < 16 else 2
                pO = [S.ps(1), S.ps(2)]
                cslot = []
                for ct in range(nct):
                    P, Pb = P_[pit % 4]
                    cslot.append((P, Pb))
                    mk, mkb = mk_[pit % 2]
                    pit += 1
                    psc = S.ps(0)
                    S.op('pe', lambda e, psc=psc, ct=ct, pb=pb, q4=q4: e.matmul(
                        psc, lhsT=kcT[pb:pb + 64, ct * 128:(ct + 1) * 128], rhs=q4, start=True, stop=True),
                        reads=[kcTb, qTpb], writes=[S.psbuf[0]])
                    S.op('act', lambda e, P=P, psc=psc: e.activation(out=P.rearrange("p j t -> p (j t)"), in_=psc,
                                                                    func=AF.Exp), reads=[S.psbuf[0]], writes=[Pb])
                    S.op('pool', lambda e, mk=mk, ct=ct, qt=qt: e.affine_select(
                        out=mk, in_=onesf, pattern=[[1, 128]], compare_op=ALU.is_ge, fill=0.0,
                        base=-(2048 * ct - 128 * qt + 31), channel_multiplier=-16), reads=[onesfb], writes=[mkb])
                    mkh, mkhb = mkh_[pit % 2]
                    S.op('dve', lambda e, mkh=mkh, mk=mk: e.tensor_copy(out=mkh, in_=mk), reads=[mkb], writes=[mkhb])
                    S.op('dve', lambda e, P=P, mkh=mkh: e.tensor_tensor(
                        out=P, in0=P, in1=mkh.unsqueeze(1).to_broadcast([128, 4, 128]), op=ALU.mult), reads=[Pb, mkhb],
                        writes=[Pb])
                if lvl < 2:
                    continue
                for j in range(4):
                    for ct in range(nct):
                        P, Pb = cslot[ct]
                        S.op('pe', lambda e, j=j, P=P, ct=ct, g2=g2, nct=nct: e.matmul(
                            pO[j // 2][:, (j % 2) * 256:(j % 2) * 256 + 129], lhsT=P[:, j, :], rhs=vca[:, g2, ct, :],
                            start=(ct == 0), stop=(ct == nct - 1)), reads=[Pb, vcab], writes=[S.psbuf[1 + j // 2]])
                if lvl < 2:
                    continue
                for j in range(4):
                    S.op('act', lambda e, rs=rs, j=j: e.copy(
                        out=rs[:, j:j + 1], in_=pO[j // 2][:, (j % 2) * 256 + 64:(j % 2) * 256 + 65]),
                        reads=[S.psbuf[1 + j // 2]], writes=[rsb])
                S.op('dve', lambda e, rs=rs: e.tensor_scalar(out=rs[:, 0:4], in0=rs[:, 0:4], scalar1=1e-30, scalar2=None,
                                                             op0=ALU.max), reads=[rsb], writes=[rsb])
                S.op('dve', lambda e, rs=rs: e.reciprocal(out=rs[:, 0:4], in_=rs[:, 0:4]), reads=[rsb], writes=[rsb])
                S.op('act', lambda e, imp=imp, rs=rs: e.activation(out=imp, in_=pO[0][:, 65:129], func=AF.Copy,
                                                                  scale=rs[:, 0:1]),
                     reads=[S.psbuf[1], rsb], writes=[impb])
                for j in range(1, 4):
                    S.op('dve', lambda e, imp=imp, rs=rs, j=j: e.scalar_tensor_tensor(
                        out=imp, in0=pO[j // 2][:, (j % 2) * 256 + 65:(j % 2) * 256 + 129], scalar=rs[:, j:j + 1],
                        in1=imp, op0=ALU.mult, op1=ALU.add), reads=[S.psbuf[1 + j // 2], rsb, impb], writes=[impb])
                S.op('dve', lambda e, imp=imp, qt=qt: e.tensor_tensor(out=imp, in0=imp, in1=selb_t[:, qt, :], op=ALU.add),
                     reads=[impb, selb_tb], writes=[impb])
                if lvl < 3:
                    continue
                S.op('dve', lambda e, m8=m8, imp=imp: e.max(out=m8[:, 0:8], in_=imp), reads=[impb], writes=[m8b])
                S.op('dve', lambda e, sc2=sc2, m8=m8, imp=imp: e.match_replace(
                    out=sc2, in_to_replace=m8[:, 0:8], in_values=imp, imm_value=-3e38), reads=[impb, m8b], writes=[sc2b])
                S.op('dve', lambda e, m8=m8, sc2=sc2: e.max(out=m8[:, 8:16], in_=sc2), reads=[sc2b, m8b], writes=[m8b])
                for hf in range(2):
                    S.op('dve', lambda e, nmt=nmt, imp=imp, m8=m8, hf=hf: e.tensor_scalar(
                        out=nmt[:, hf * 64:(hf + 1) * 64], in0=imp, scalar1=m8[:, 15:16], scalar2=-30000.0,
                        op0=ALU.is_lt, op1=ALU.mult), reads=[impb, m8b], writes=[nmtb])
                pvt = transposes_to(7, [nmt], [nmtb])
                S.op('act', lambda e, nmT1=nmT1, pvt=pvt: e.copy(out=nmT1, in_=pvt[:, 0:128]), reads=[S.psbuf[7]],
                     writes=[nmT1b])
                qsl = slice(64 * g2, 64 * g2 + 64)
                msl = slice(64 * (1 - g2), 64 * (1 - g2) + 64)
                S.op('dve', lambda e, nmT4=nmT4, nmT1=nmT1, msl=msl: e.tensor_copy(
                    out=nmT4[msl], in_=nmT1[msl].unsqueeze(1).to_broadcast([64, 4, 128])), reads=[nmT1b],
                    writes=[nmT4b])
                S.op('act', lambda e, nmT4=nmT4, qsl=qsl, qt=qt: e.copy(
                    out=nmT4[qsl], in_=qTp[qsl, :, qt * 128:(qt + 1) * 128]), reads=[qTpb], writes=[nmT4b])
                if lvl < 4:
                    continue
                S.op('dve', lambda e, rs=rs, ngv=ngv: e.tensor_tensor(out=rs[:, 4:8], in0=rs[:, 0:4], in1=ngv[:, :, 0],
                                                                     op=ALU.mult), reads=[rsb, ng_tb], writes=[rsb])
                for j in range(4):
                    S.op('act', lambda e, nacc=nacc, rs=rs, j=j: e.activation(
                        out=nacc[:, j, :], in_=pO[j // 2][:, (j % 2) * 256:(j % 2) * 256 + 64], func=AF.Copy,
                        scale=rs[:, 4 + j:5 + j]), reads=[S.psbuf[1 + j // 2], rsb], writes=[naccb])
                for br in (1, 2):
                    if lvl < 5 or (br == 2 and lvl < 6):
                        continue
                    kts = list(range(qt + 1)) if br == 1 else list(range(max(0, qt - 4), qt + 1))
                    bankO = 5 if br == 1 else 6
                    pA = S.ps(bankO)
                    va = g2 if br == 1 else 2 + g2
                    slot = {}
                    for ki, kt in enumerate(kts):
                        if br == 1:
                            P, Pb = Pall[:, kt], Pallb[kt]
                        else:
                            P, Pb = Pw[:, ki], Pwb[ki]
                        slot[kt] = (P, Pb)
                        bsc = 3 + (pit % 2)
                        pit += 1
                        pss = S.ps(bsc)
                        if br == 1:
                            kst, kstb = ksE_[g2]
                            S.op('pe', lambda e, pss=pss, kt=kt, nmT4=nmT4, kst=kst: e.matmul(
                                pss, lhsT=kst[:, kt * 128:(kt + 1) * 128], rhs=nmT4, start=True, stop=True),
                                reads=[kstb, nmT4b], writes=[S.psbuf[bsc]])
                        else:
                            S.op('pe', lambda e, pss=pss, br=br, kt=kt, pb=pb, q4=q4: e.matmul(
                                pss, lhsT=k4[pb:pb + 64, br, kt * 128:(kt + 1) * 128], rhs=q4, start=True, stop=True),
                                reads=[k4b, qTpb], writes=[S.psbuf[bsc]])
                        S.op('act', lambda e, P=P, pss=pss: e.activation(out=P.rearrange("p j t -> p (j t)"), in_=pss,
                                                                        func=AF.Exp), reads=[S.psbuf[bsc]], writes=[Pb])
                        if kt == qt:
                            S.op('dve', lambda e, P=P: e.tensor_tensor(out=P, in0=P, in1=caus4, op=ALU.mult),
                                 reads=[Pb, caus4b], writes=[Pb])
                        if br == 2 and kt == qt - 4:
                            S.op('dve', lambda e, P=P: e.tensor_tensor(out=P, in0=P, in1=anti4, op=ALU.mult),
                                 reads=[Pb, anti4b], writes=[Pb])
                    for j in range(4):
                        for kt in kts:
                            P, Pb = slot[kt]
                            S.op('pe', lambda e, pA=pA, j=j, P=P, kt=kt, va=va, kts=kts: e.matmul(
                                pA[:, j * 128:j * 128 + 65], lhsT=P[:, j, :], rhs=vsw_t[:, kt, va, :],
                                start=(kt == kts[0]), stop=(kt == kts[-1])), reads=[Pb, vsw_tb], writes=[S.psbuf[bankO]])
                    o8 = 8 if br == 1 else 12
                    for j in range(4):
                        S.op('act', lambda e, rs=rs, pA=pA, o8=o8, j=j: e.copy(
                            out=rs[:, o8 + j:o8 + j + 1], in_=pA[:, j * 128 + 64:j * 128 + 65]),
                            reads=[S.psbuf[bankO]], writes=[rsb])
                    S.op('dve', lambda e, rs=rs, o8=o8: e.tensor_scalar(
                        out=rs[:, o8:o8 + 4], in0=rs[:, o8:o8 + 4], scalar1=1e-30, scalar2=None, op0=ALU.max),
                        reads=[rsb], writes=[rsb])
                    S.op('dve', lambda e, rs=rs, o8=o8: e.reciprocal(out=rs[:, o8:o8 + 4], in_=rs[:, o8:o8 + 4]),
                         reads=[rsb], writes=[rsb])
                    S.op('dve', lambda e, rs=rs, ngv=ngv, o8=o8, br=br: e.tensor_tensor(
                        out=rs[:, o8:o8 + 4], in0=rs[:, o8:o8 + 4], in1=ngv[:, :, br], op=ALU.mult),
                        reads=[rsb, ng_tb], writes=[rsb])
                    for j in range(4):
                        S.op('dve', lambda e, nacc=nacc, pA=pA, rs=rs, j=j, o8=o8: e.scalar_tensor_tensor(
                            out=nacc[:, j, :], in0=pA[:, j * 128:j * 128 + 64], scalar=rs[:, o8 + j:o8 + j + 1],
                            in1=nacc[:, j, :], op0=ALU.mult, op1=ALU.add), reads=[S.psbuf[bankO], rsb, naccb],
                            writes=[naccb])
                if lvl < 7:
                    continue
                S.op('dve', lambda e, no=no, nacc=nacc, szt=szt, g2=g2: e.tensor_tensor(
                    out=no, in0=nacc.rearrange("p j d -> p (j d)"), in1=szt[:, g2 * 256:(g2 + 1) * 256], op=ALU.mult),
                    reads=[naccb, sztb], writes=[nob])
                pv2 = transposes_to(7, [no[:, j * 128:(j + 1) * 128] for j in range(2)], [nob])
                q4i = qt % 4
                S.op('act', lambda e, stg=stg, pv2=pv2, q4i=q4i, g2=g2: e.copy(
                    out=stg[:, g2 * 2:g2 * 2 + 2, q4i * 128:(q4i + 1) * 128],
                    in_=pv2[:, 0:256].rearrange("p (j t) -> p j t", j=2)), reads=[S.psbuf[7]], writes=[stgb])
            if qt % 4 == 3 and lvl >= 7:
                tb = qt // 4
                S.dma('sp', mixT[base_row:base_row + 512, :].rearrange("(j p) t -> p j t", p=128)[
                    :, :, tb * 512:(tb + 1) * 512], stg, reads=[stgb])
    S.barrier()
    S.release()


def phase_D(S, cfg, G):
    TF = cfg.TF
    ident, identb = G['ident'], G['identb']
    transposes_to = G['transposes_to']
    mixT, xf_in, pf_in, out = G['mixT'], G['xf_in'], G['pf_in'], G['out']
    S.mark()
    wo, wob = S.alloc([128, 16, DM], BF16, 'wo')
    wg, wgb = S.alloc([128, 16, DM], BF16, 'wg')
    wp, wpb = S.alloc([128, 2, DM], BF16, 'wp')
    pgt, pgtb = S.alloc([128, DM], F32, 'pgt')
    fgt, fgtb = S.alloc([128, DM], F32, 'fgt')
    for cb in range(4):
        S.dma('pool', wo[:, :, cb * 512:(cb + 1) * 512],
              G['w_out'][:, cb * 512:(cb + 1) * 512].rearrange("(c p) n -> p c n", p=128), writes=[wob])
    for cb in range(4):
        S.dma('pool', wg[:, :, cb * 512:(cb + 1) * 512],
              G['w_pg'][:, cb * 512:(cb + 1) * 512].rearrange("(c p) n -> p c n", p=128), writes=[wgb])
    S.dma('pool', wp, G['w_pp'].rearrange("(c p) n -> p c n", p=128), writes=[wpb])
    S.dma('sp', pgt, G['ple_g'].partition_broadcast(128), writes=[pgtb])
    S.dma('sp', fgt, G['fin_g'].partition_broadcast(128), writes=[fgtb])
    mT, mTb = S.alloc([128, 16, 128], BF16, 'mT')
    xt, xtb = S.alloc([128, DM], F32, 'xt')
    x1, x1b = S.alloc([128, DM], F32, 'x1')
    x1h, x1hb = S.alloc([128, DM], BF16, 'x1h')
    x1T, x1Tb = S.alloc([128, 16, 128], BF16, 'x1T')
    pl, plb = S.alloc([128, DM], F32, 'pl')
    junk, junkb = S.alloc([128, DM], BF16, 'junkD')
    pt, ptb = S.alloc([128, 256], F32, 'pt')
    ph, phb = S.alloc([128, 256], BF16, 'ph')
    pT, pTb = S.alloc([128, 2, 128], BF16, 'pT')
    sgd_ = [S.alloc([128, 512], F32, 'sgd%d' % i) for i in range(2)]
    ss_ = [S.alloc([128, 1], F32, 'ssD%d' % i) for i in range(2)]
    mixv = mixT.rearrange("(c p) t -> p c t", p=128)
    it = 0
    for tt in range(TF // 128):
        rows = slice(tt * 128, (tt + 1) * 128)
        S.dma('sp', mT, mixv[:, :, rows], writes=[mTb])
        S.dma('sp', xt, xf_in[rows, :], writes=[xtb])
        S.dma('sp', pt, pf_in[rows, :], writes=[ptb])
        S.op('act', lambda e: e.copy(out=ph, in_=pt), reads=[ptb], writes=[phb])
        pv = transposes_to(7, [ph[:, j * 128:(j + 1) * 128] for j in range(2)], [phb])
        S.op('act', lambda e, pv=pv: e.copy(out=pT, in_=pv[:, 0:256].rearrange("p (j t) -> p j t", j=2)),
             reads=[S.psbuf[7]], writes=[pTb])
        for cb in range(4):
            bank = it % 4
            it += 1
            ps = S.ps(bank)
            cs = slice(cb * 512, (cb + 1) * 512)
            for kc in range(16):
                S.op('pe', lambda e, ps=ps, kc=kc, cs=cs: e.matmul(ps, lhsT=mT[:, kc, :], rhs=wo[:, kc, cs],
                                                                  start=(kc == 0), stop=(kc == 15)),
                     reads=[mTb, wob], writes=[S.psbuf[bank]])
            S.op('dve', lambda e, ps=ps, cs=cs: e.tensor_tensor(out=x1[:, cs], in0=ps, in1=xt[:, cs], op=ALU.add),
                 reads=[S.psbuf[bank], xtb], writes=[x1b])
        S.op('act', lambda e: e.copy(out=x1h, in_=x1), reads=[x1b], writes=[x1hb])
        for half in range(2):
            bank = 4 + half
            pv = transposes_to(bank, [x1h[:, (half * 8 + i) * 128:(half * 8 + i + 1) * 128] for i in range(8)], [x1hb])
            src = pv[:, 0:1024].rearrange("p (c t) -> p c t", c=8)
            dst = x1T[:, half * 8:(half + 1) * 8, :]
            if half == 0:
                S.op('act', lambda e, src=src, dst=dst: e.copy(out=dst, in_=src), reads=[S.psbuf[bank]], writes=[x1Tb])
            else:
                S.op('dve', lambda e, src=src, dst=dst: e.tensor_copy(out=dst, in_=src), reads=[S.psbuf[bank]],
                     writes=[x1Tb])
        for cb in range(4):
            bank = it % 4
            it += 1
            ps = S.ps(bank)
            cs = slice(cb * 512, (cb + 1) * 512)
            for k in range(2):
                S.op('pe', lambda e, ps=ps, k=k, cs=cs: e.matmul(ps, lhsT=pT[:, k, :], rhs=wp[:, k, cs], start=(k == 0),
                                                                stop=(k == 1)),
                     reads=[pTb, wpb], writes=[S.psbuf[bank]])
            S.op('act', lambda e, ps=ps, cs=cs: e.copy(out=pl[:, cs], in_=ps), reads=[S.psbuf[bank]], writes=[plb])
        ss, ssb = ss_[0]
        S.op('act', lambda e, ss=ss: e.activation(out=junk, in_=pl, func=AF.Square, accum_out=ss), reads=[plb],
             writes=[junkb, ssb])
        rstd_inplace(S, ss, ssb, DM)
        S.op('dve', lambda e, ss=ss: e.scalar_tensor_tensor(out=pl, in0=pl, scalar=ss, in1=pgt, op0=ALU.mult,
                                                            op1=ALU.mult), reads=[plb, ssb, pgtb], writes=[plb])
        for cb in range(4):
            bank = it % 4
            it += 1
            ps = S.ps(bank)
            cs = slice(cb * 512, (cb + 1) * 512)
            sgd, sgdb = sgd_[cb % 2]
            for kc in range(16):
                S.op('pe', lambda e, ps=ps, kc=kc, cs=cs: e.matmul(ps, lhsT=x1T[:, kc, :], rhs=wg[:, kc, cs],
                                                                  start=(kc == 0), stop=(kc == 15)),
                     reads=[x1Tb, wgb], writes=[S.psbuf[bank]])
            S.op('act', lambda e, ps=ps, sgd=sgd: e.activation(out=sgd, in_=ps, func=AF.Sigmoid), reads=[S.psbuf[bank]],
                 writes=[sgdb])
            S.op('dve', lambda e, sgd=sgd, cs=cs: e.tensor_tensor(out=sgd, in0=sgd, in1=pl[:, cs], op=ALU.mult),
                 reads=[sgdb, plb], writes=[sgdb])
            S.op('dve', lambda e, sgd=sgd, cs=cs: e.tensor_tensor(out=x1[:, cs], in0=x1[:, cs], in1=sgd, op=ALU.add),
                 reads=[sgdb, x1b], writes=[x1b])
        ss2, ss2b = ss_[1]
        S.op('act', lambda e, ss2=ss2: e.activation(out=junk, in_=x1, func=AF.Square, accum_out=ss2), reads=[x1b],
             writes=[junkb, ss2b])
        rstd_inplace(S, ss2, ss2b, DM)
        S.op('dve', lambda e, ss2=ss2: e.scalar_tensor_tensor(out=xt, in0=x1, scalar=ss2, in1=fgt, op0=ALU.mult,
                                                              op1=ALU.mult), reads=[x1b, ss2b, fgtb, xtb], writes=[xtb])
        S.dma('sp', out[rows, :], xt, reads=[xtb])
    S.barrier()
    S.release()


def core_inputs(inp, cfg, b, tok0, wout_rows):
    f = np.float32
    w_in = inp['w_in'][0]
    sb, ov, inv = host_consts()
    heads = cfg.heads
    fm = cfg.fm_cols
    cw = inp['conv_w'][0]
    cbias = inp['conv_b'][0]
    ch = [c - OFF['mq'] for c in fm]
    d = {
        'x': np.ascontiguousarray(inp['x'][b]),
        'xf': np.ascontiguousarray(inp['x'][b, tok0:tok0 + cfg.TF]),
        'pf': np.ascontiguousarray(inp['p'][0, b, tok0:tok0 + cfg.TF]),
        'pos': np.ascontiguousarray(inp['positions'][b]).astype(np.int32),
        'norm_g': np.ascontiguousarray(inp['norm_g'][0]),
        'w_fm': np.ascontiguousarray(w_in[:, fm]),
        'w_if': np.ascontiguousarray(w_in[:, cfg.if_cols]),
        'w_tm': np.ascontiguousarray(w_in[:, cfg.tm_cols]),
        'convw': np.ascontiguousarray(cw[:, ch].T.reshape(-1, 128, 4).transpose(1, 0, 2)),
        'convb': np.ascontiguousarray(cbias[ch].reshape(-1, 128).T[:, :, None]),
        'b_i': np.ascontiguousarray(inp['b_igate'][0][heads][:, None]),
        'b_f': np.ascontiguousarray(inp['b_fgate'][0][heads][:, None]),
        'mng': np.ascontiguousarray(np.concatenate([inp['m_norm_g'][0][h * 256:(h + 1) * 256] for h in heads])),
        'posk': np.ascontiguousarray(inp['cmp_pos_k'][0].T),
        'posv': np.ascontiguousarray(inp['cmp_pos_v'][0].T),
        'ckw1': np.ascontiguousarray(inp['cmp_k_w1'][0].reshape(32, 64, 128).transpose(1, 0, 2)),
        'ckw2': np.ascontiguousarray(inp['cmp_k_w2'][0]),
        'cvw1': np.ascontiguousarray(inp['cmp_v_w1'][0].reshape(32, 64, 128).transpose(1, 0, 2)),
        'cvw2': np.ascontiguousarray(inp['cmp_v_w2'][0]),
        'w_out': np.ascontiguousarray(inp['w_out'][0][wout_rows]),
        'w_pg': np.ascontiguousarray(inp['ple_gate_w'][0]),
        'w_pp': np.ascontiguousarray(inp['ple_proj_w'][0]),
        'ple_g': np.ascontiguousarray(inp['ple_norm_g'][0]),
        'fin_g': np.ascontiguousarray(inp['final_norm_g']),
        'selbias': sb, 'ovl': ov, 'invf': inv,
    }
    return {k: (v if v.dtype == np.int32 else v.astype(f)) for k, v in d.items()}


_NC_CACHE = {}


def kernel(**inputs):
    inp = {k: np.asarray(v) for k, v in inputs.items()}
    cfg = Cfg([0, 1, 2, 3], [(0, 1), (2, 3)], SEQ)
    if 'nc' not in _NC_CACHE:
        _NC_CACHE['nc'] = build(cfg)
    nc = _NC_CACHE['nc']
    rows = cfg.mix_rows
    in_maps = []
    for c in range(8):
        in_maps.append(core_inputs(inp, cfg, c // 2, 0, rows))
    res = run_bass_kernel_spmd(nc, in_maps, core_ids=list(range(8)))
    out = np.stack([np.asarray(res.results[2 * b]["out"]) for b in range(4)], axis=0)
    return out.astype(np.float32)
```

```python
import contextlib
import math
import numpy as np
import concourse.bass as bass
import concourse.mybir as mybir
from concourse.bass_utils import run_bass_kernel_spmd

F32 = mybir.dt.float32
BF16 = mybir.dt.bfloat16
I32 = mybir.dt.int32
U8 = mybir.dt.uint8
AF = mybir.ActivationFunctionType
ALU = mybir.AluOpType
AX = mybir.AxisListType
ENG = ['pe', 'act', 'dve', 'pool', 'sp']
DT_SIZE = {F32: 4, BF16: 2, I32: 4, U8: 1}

SEQ = 4096
DM = 2048
NT = SEQ // 128
EPS = 1e-6
FORCE = 1e3
NEG = -1e30


class Buf:
    __slots__ = ('name', 'w', 'r')

    def __init__(self, name=''):
        self.name = name
        self.w = None
        self.r = {}


class Sched:
    def __init__(self, nc, stack, n_dma_sems=8):
        self.nc = nc
        self.prog = {e: [] for e in ENG}
        self.cnt = {e: 0 for e in ENG}
        self.seen = {e: {} for e in ENG}
        self.esem = {e: stack.enter_context(nc.semaphore('es_' + e)) for e in ENG if e != 'sp'}
        self.dq = {}
        self.dsem = []
        self.dcnt = []
        for q in ('sp', 'act', 'pool'):
            ids = []
            for i in range(n_dma_sems):
                self.dsem.append(stack.enter_context(nc.semaphore('ds_%s%d' % (q, i))))
                self.dcnt.append(0)
                ids.append(len(self.dsem) - 1)
            self.dq[q] = [ids, 0]
        self.ccsem = stack.enter_context(nc.semaphore('cc_sem'))
        self.arena_bytes = 204 * 1024
        self.arena = stack.enter_context(nc.sbuf_tensor('arena', [128, self.arena_bytes], U8))
        self.arena_off = 0
        self.marks = []
        self.psum = [stack.enter_context(nc.psum_tensor('ps%d' % i, [128, 512], F32)) for i in range(8)]
        self.psbuf = [Buf('ps%d' % i) for i in range(8)]

    def alloc(self, shape, dtype, name=''):
        n = 1
        for s in shape[1:]:
            n *= s
        nbytes = (n * DT_SIZE[dtype] + 63) // 64 * 64
        off = self.arena_off
        assert off + nbytes <= self.arena_bytes, ('SBUF arena overflow', name, off, nbytes)
        self.arena_off += nbytes
        v = self.arena[0:shape[0], off:off + n * DT_SIZE[dtype]]
        if dtype != U8:
            v = v.bitcast(dtype)
        if len(shape) > 2:
            names = ' '.join('d%d' % i for i in range(len(shape) - 1))
            kw = {'d%d' % i: shape[i + 1] for i in range(len(shape) - 1)}
            v = v.rearrange('p (%s) -> p %s' % (names, names), **kw)
        return v, Buf(name)

    def mark(self):
        self.marks.append(self.arena_off)

    def release(self):
        self.arena_off = self.marks.pop()

    def ps(self, i, dtype=F32):
        v = self.psum[i][:, :]
        if dtype != F32:
            v = v.bitcast(dtype)
        return v

    @staticmethod
    def _key(tok):
        if tok[0] == 'e':
            return ('e', tok[1]), tok[2]
        return ('d', tok[1]), 16 * tok[2]

    def _waits(self, eng, need):
        out = []
        seen = self.seen[eng]
        for k, v in need.items():
            if k == ('e', 'pe') and eng == 'pe':
                continue
            if seen.get(k, 0) >= v:
                continue
            seen[k] = v
            sem = self.esem[k[1]] if k[0] == 'e' else (self.dsem[k[1]] if k[0] == 'd' else self.ccsem)
            out.append((sem, v))
        return out

    @staticmethod
    def _need(reads, writes, extra=()):
        need = {}

        def add(k, v):
            if need.get(k, 0) < v:
                need[k] = v
        for b in reads:
            if b.w is not None:
                add(*b.w)
        for b in writes:
            if b.w is not None:
                add(*b.w)
            for k, v in b.r.items():
                add(k, v)
        for k, v in extra:
            add(k, v)
        return need

    @staticmethod
    def _commit(kv, reads, writes):
        k, v = kv
        for b in reads:
            if b.r.get(k, 0) < v:
                b.r[k] = v
        for b in writes:
            b.w = kv
            b.r = {}

    def op(self, eng, fn, reads=(), writes=()):
        waits = self._waits(eng, self._need(reads, writes))
        self.cnt[eng] += 1
        self._commit((('e', eng), self.cnt[eng]), reads, writes)
        self.prog[eng].append((waits, fn, (self.esem[eng], 1)))

    def dma(self, q, out, in_, reads=(), writes=(), **kw):
        ids, rr = self.dq[q]
        j = ids[rr % len(ids)]
        self.dq[q][1] = rr + 1
        extra = [(('d', j), 16 * self.dcnt[j])] if self.dcnt[j] > 0 else []
        waits = self._waits(q, self._need(reads, writes, extra))
        self.dcnt[j] += 1
        self._commit((('d', j), 16 * self.dcnt[j]), reads, writes)
        self.prog[q].append((waits, (lambda e, o=out, i=in_, kw=kw: e.dma_start(out=o, in_=i, **kw)),
                             (self.dsem[j], 16)))

    def collective(self, fn, reads=(), writes=()):
        waits = self._waits('pool', self._need(reads, writes))
        self.cccnt = getattr(self, 'cccnt', 0) + 1
        self._commit((('c', 0), self.cccnt), reads, writes)
        self.prog['pool'].append((waits, fn, (self.ccsem, None)))
        self.seen['pool'][('c', 0)] = self.cccnt
        self.prog['pool'].append(([(self.ccsem, self.cccnt)], None, None))

    def barrier(self):
        need = {}
        for e in ENG:
            if e != 'sp' and self.cnt[e] > 0:
                need[('e', e)] = self.cnt[e]
        for j, c in enumerate(self.dcnt):
            if c > 0:
                need[('d', j)] = 16 * c
        for e in ENG:
            waits = self._waits(e, need)
            if waits:
                self.prog[e].append((waits, None, None))

    def emit(self):
        nc = self.nc
        with nc.Block() as block:
            def replay(lst, eng):
                for waits, fn, inc in lst:
                    for sem, v in waits:
                        eng.wait_ge(sem, v)
                    if fn is not None:
                        if inc[1] is None:
                            fn(eng).then_inc(inc[0])
                        else:
                            fn(eng).then_inc(inc[0], inc[1])

            @block.sync
            def _(e):
                replay(self.prog['sp'], e)

            @block.scalar
            def _(e):
                replay(self.prog['act'], e)

            @block.vector
            def _(e):
                replay(self.prog['dve'], e)

            @block.gpsimd
            def _(e):
                replay(self.prog['pool'], e)

            @block.tensor
            def _(e):
                replay(self.prog['pe'], e)


OFF = {}
_o = 0
for _n, _w in [('mq', 1024), ('mk', 1024), ('mv', 1024), ('mo', 1024), ('mz', 1024), ('mi', 4), ('mf', 4),
               ('nq', 1024), ('kc', 256), ('vc', 256), ('ks', 256), ('vs', 256), ('kw', 256), ('vw', 256),
               ('ng', 48), ('nz', 1024)]:
    OFF[_n] = _o
    _o += _w
assert _o == 8760


class Cfg:
    def __init__(self, heads, pairs, tf, debug=False, phases='ABCD', gather=False):
        self.gather = gather
        self.heads = heads
        self.pairs = pairs
        self.NH = len(heads)
        self.NP = len(pairs)
        self.TF = tf
        self.debug = debug
        self.phases = phases
        blocks = []
        cols = []

        def add(kind, idx, c):
            blocks.append((kind, idx, len(cols), len(c)))
            cols.extend(c)
        hv = []
        for h in heads:
            hv.extend(range(OFF['mv'] + h * 256, OFF['mv'] + (h + 1) * 256))
        for i in range(0, len(hv), 512):
            add('v_m', i // 512, hv[i:i + 512])
        for i, h in enumerate(heads):
            add('gate_m', i, list(range(OFF['mo'] + h * 256, OFF['mo'] + (h + 1) * 256)) +
                list(range(OFF['mz'] + h * 256, OFF['mz'] + (h + 1) * 256)))
        for pi, (g0, g1) in enumerate(pairs):
            c = []
            for jj in range(4):
                for g in (g0, g1):
                    hh = 4 * g + jj
                    c.extend(range(OFF['nq'] + hh * 64, OFF['nq'] + (hh + 1) * 64))
            add('nq', pi, c)
            c = []
            for nm in ('kc', 'ks', 'kw', 'vc'):
                for g in (g0, g1):
                    c.extend(range(OFF[nm] + g * 64, OFF[nm] + (g + 1) * 64))
            add('nk', pi, c)
            c = []
            for nm in ('vs', 'vw'):
                for g in (g0, g1):
                    c.extend(range(OFF[nm] + g * 64, OFF[nm] + (g + 1) * 64))
            for g in (g0, g1):
                c.extend(range(OFF['ng'] + g * 12, OFF['ng'] + (g + 1) * 12))
            add('nv', pi, c)
            c = []
            for g in (g0, g1):
                c.extend(range(OFF['nz'] + g * 256, OFF['nz'] + (g + 1) * 256))
            add('nz', pi, c)
        self.blocks = blocks
        self.tm_cols = cols
        fm = []
        for h in heads:
            fm.extend(range(OFF['mq'] + h * 256, OFF['mq'] + (h + 1) * 256))
        for h in heads:
            fm.extend(range(OFF['mk'] + h * 256, OFF['mk'] + (h + 1) * 256))
        self.fm_cols = fm
        self.if_cols = [OFF['mi'] + h for h in heads] + [OFF['mf'] + h for h in heads]
        mr = []
        for h in heads:
            mr.extend(range(h * 256, (h + 1) * 256))
        for (g0, g1) in pairs:
            for g in (g0, g1):
                mr.extend(range(1024 + g * 256, 1024 + (g + 1) * 256))
        self.mix_rows = mr


def host_consts():
    t = np.arange(SEQ)
    n = np.arange(64)
    cur = (t // 64)[:, None]
    valid = n[None, :] * 64 <= t[:, None]
    forced = (n[None, :] == 0) | (n[None, :] == cur) | (n[None, :] == cur - 1)
    sb = np.where(valid, np.where(forced, FORCE, 0.0), NEG).astype(np.float32)
    sb = sb.reshape(NT, 128, 64).transpose(1, 0, 2).copy()
    ov = np.zeros((256, 65), np.float32)
    for c in range(255):
        tok = c * 16 + np.arange(32)
        ov[c, 0] = 1.0
        for nn in np.unique(tok // 64):
            ov[c, 1 + nn] = np.mean(tok // 64 == nn)
    inv = (500000.0 ** (-(np.arange(8, dtype=np.float32) * 2.0 / 16))).astype(np.float32)
    return sb, ov, inv


def build(cfg):
    nc = bass.Bass("TRN2", target_bir_lowering=False)
    NH, NP, TF = cfg.NH, cfg.NP, cfg.TF
    NFC = 4 * NH
    WTM = len(cfg.tm_cols)
    NMIX = NH * 256 + NP * 512
    skind = "ExternalOutput" if cfg.debug else "Internal"

    def din(name, shape, dt=F32):
        return nc.dram_tensor(name, list(shape), dt, kind="ExternalInput").ap()

    def dscr(name, shape, dt):
        k = "ExternalOutput" if name in getattr(cfg, 'dump', ()) else "Internal"
        if name in getattr(cfg, 'feed', ()):
            k = "ExternalInput"
        return nc.dram_tensor(name, list(shape), dt, kind=k).ap()

    x_in = din("x", [SEQ, DM])
    xf_in = din("xf", [TF, DM])
    pf_in = din("pf", [TF, 256])
    pos_in = din("pos", [SEQ], I32)
    norm_g = din("norm_g", [DM])
    w_fm = din("w_fm", [DM, NFC * 128])
    w_if = din("w_if", [DM, 2 * NH])
    w_tm = din("w_tm", [DM, WTM])
    convw = din("convw", [128, NFC, 4])
    convb = din("convb", [128, NFC, 1])
    b_i = din("b_i", [NH, 1])
    b_f = din("b_f", [NH, 1])
    mng = din("mng", [NH * 256])
    posk = din("posk", [64, 32])
    posv = din("posv", [64, 32])
    ckw1 = din("ckw1", [64, 32, 128])
    ckw2 = din("ckw2", [128, 64])
    cvw1 = din("cvw1", [64, 32, 128])
    cvw2 = din("cvw2", [128, 64])
    w_out = din("w_out", [DM, DM])
    w_pg = din("w_pg", [DM, DM])
    w_pp = din("w_pp", [256, DM])
    ple_g = din("ple_g", [DM])
    fin_g = din("fin_g", [DM])
    selbias = din("selbias", [128, NT, 64])
    ovl = din("ovl", [256, 65])
    invf = din("invf", [8])
    out = nc.dram_tensor("out", [TF, DM], F32, kind="ExternalOutput").ap()

    qkT = dscr("qkT", [NFC * 128, SEQ], BF16)
    gi = dscr("gi", [NH, SEQ], F32)
    gf = dscr("gf", [NH, SEQ], F32)
    vm = dscr("vm", [SEQ, NH * 256], BF16)
    gm = dscr("gm", [SEQ, NH * 256], BF16)
    qTn = dscr("qTn", [NP, 512, SEQ], BF16)
    kT4 = dscr("kT4", [NP, 4, 128, SEQ], BF16)
    vsw = dscr("vsw", [NP, SEQ, 256], BF16)
    ngs = dscr("ngs", [NP, SEQ, 24], F32)
    szn = dscr("szn", [NP, SEQ, 512], BF16)
    mixT = dscr("mixT", [NMIX if cfg.gather else DM, SEQ], BF16)
    CCH = 128
    mixG = dscr("mixG", [DM, SEQ], BF16) if cfg.gather else mixT
    mixGb = Buf('mixG')

    with contextlib.ExitStack() as st:
        S = Sched(nc, st)
        ident, identb = S.alloc([128, 128], BF16, 'ident')
        identf, identfb = S.alloc([128, 128], F32, 'identf')
        S.op('pool', lambda e: e.memset(identf, 1.0), writes=[identfb])
        S.op('pool', lambda e: e.affine_select(out=identf, in_=identf, pattern=[[-1, 128]], compare_op=ALU.is_equal,
                                                fill=0.0, base=0, channel_multiplier=1),
             reads=[identfb], writes=[identfb])
        S.op('dve', lambda e: e.tensor_copy(out=ident, in_=identf), reads=[identfb], writes=[identb])

        def transposes_to(pe_bank, srcs, src_bufs, dtype=BF16, idn=None, idb=None, width=128):
            pv = S.ps(pe_bank, dtype)
            idn = ident if idn is None else idn
            idb = identb if idb is None else idb
            for i, s_ap in enumerate(srcs):
                S.op('pe', lambda e, i=i, s_ap=s_ap: e.transpose(
                    out=pv[0:s_ap.shape[1], i * width:i * width + s_ap.shape[0]], in_=s_ap,
                    identity=idn[0:s_ap.shape[0], 0:s_ap.shape[0]]),
                    reads=list(src_bufs) + [idb], writes=[S.psbuf[pe_bank]])
            return pv

        if 'A' in cfg.phases:
            phase_A(S, cfg, locals())
        if 'B' in cfg.phases:
            phase_B(S, cfg, locals())
        if 'C' in cfg.phases:
            phase_C(S, cfg, locals())
        if cfg.gather:
            S.barrier()
            for ci in range(NMIX // CCH):
                S.collective(lambda e, ci=ci: e.collective_compute(
                    "AllGather", ALU.bypass, replica_groups=[[0, 1], [2, 3], [4, 5], [6, 7]],
                    ins=[mixT[ci * CCH:(ci + 1) * CCH, :].opt()],
                    outs=[mixG[ci * 2 * CCH:(ci + 1) * 2 * CCH, :].opt()]), writes=[mixGb])
        if 'D' in cfg.phases:
            phase_D(S, cfg, locals())
        S.barrier()
        S.emit()
    return nc


def phase_A(S, cfg, G):
    NH, NP = cfg.NH, cfg.NP
    NFC = 4 * NH
    ident, identb = G['ident'], G['identb']
    transposes_to = G['transposes_to']
    x_in, pos_in, norm_g = G['x_in'], G['pos_in'], G['norm_g']
    S.mark()
    hT, hTb = S.alloc([128, 16, SEQ], BF16, 'hT')
    cosq, cqb = S.alloc([128, NT, 8], F32, 'cosq')
    sinq, sqb = S.alloc([128, NT, 8], F32, 'sinq')
    cosk, ckb = S.alloc([128, NT, 8], F32, 'cosk')
    sink, skb = S.alloc([128, NT, 8], F32, 'sink')

    S.mark()
    posi, posib = S.alloc([128, NT], I32, 'posi')
    posf, posfb = S.alloc([128, NT], F32, 'posf')
    invt, invtb = S.alloc([128, 8], F32, 'invt')
    ang, angb = S.alloc([128, NT, 8], F32, 'ang')
    tq, tqb = S.alloc([128, NT, 8], F32, 'tq')
    tn, tnb = S.alloc([128, NT, 8], F32, 'tn')
    tr, trb = S.alloc([128, NT, 8], F32, 'tr')
    S.dma('sp', posi, G['pos_in'].rearrange("(c p) -> p c", p=128), writes=[posib], allow_slow_non_contiguous=True)
    S.dma('sp', invt, G['invf'].partition_broadcast(128), writes=[invtb])
    S.op('dve', lambda e: e.tensor_copy(out=posf, in_=posi), reads=[posib], writes=[posfb])
    S.op('dve', lambda e: e.tensor_tensor(out=ang, in0=posf.unsqueeze(2).to_broadcast([128, NT, 8]),
                                          in1=invt.unsqueeze(1).to_broadcast([128, NT, 8]), op=ALU.mult),
         reads=[posfb, invtb], writes=[angb])
    TWO_PI = 2.0 * math.pi
    C1 = 6.28125
    C2 = TWO_PI - C1
    MAGIC = 12582912.0

    def sin_of(dst, dstb, shift, scale):
        S.op('dve', lambda e: e.tensor_scalar(out=tq, in0=ang, scalar1=shift, scalar2=1.0 / TWO_PI, op0=ALU.add,
                                              op1=ALU.mult), reads=[angb], writes=[tqb])
        S.op('dve', lambda e: e.tensor_scalar(out=tn, in0=tq, scalar1=MAGIC, scalar2=None, op0=ALU.add),
             reads=[tqb], writes=[tnb])
        S.op('dve', lambda e: e.tensor_scalar(out=tn, in0=tn, scalar1=MAGIC, scalar2=None, op0=ALU.subtract),
             reads=[tnb], writes=[tnb])
        S.op('dve', lambda e: e.scalar_tensor_tensor(out=tr, in0=tn, scalar=-C1, in1=ang, op0=ALU.mult, op1=ALU.add),
             reads=[tnb, angb], writes=[trb])
        S.op('dve', lambda e: e.tensor_scalar(out=tr, in0=tr, scalar1=shift, scalar2=None, op0=ALU.add),
             reads=[trb], writes=[trb])
        S.op('dve', lambda e: e.scalar_tensor_tensor(out=tr, in0=tn, scalar=-C2, in1=tr, op0=ALU.mult, op1=ALU.add),
             reads=[tnb, trb], writes=[trb])
        S.op('dve', lambda e: e.tensor_scalar(out=tr, in0=tr, scalar1=3.1415925, scalar2=-3.1415925, op0=ALU.min,
                                              op1=ALU.max), reads=[trb], writes=[trb])
        S.op('act', lambda e: e.activation(out=dst, in_=tr, func=AF.Sin), reads=[trb], writes=[dstb])
        if scale != 1.0:
            pass

    sin_of(sink, skb, 0.0, 1.0)
    sin_of(cosk, ckb, math.pi / 2, 1.0)
    S.op('dve', lambda e: e.tensor_scalar(out=sinq, in0=sink, scalar1=0.125, scalar2=None, op0=ALU.mult),
         reads=[skb], writes=[sqb])
    S.op('dve', lambda e: e.tensor_scalar(out=cosq, in0=cosk, scalar1=0.125, scalar2=None, op0=ALU.mult),
         reads=[ckb], writes=[cqb])

    gt, gtb = S.alloc([128, DM], F32, 'gt')
    S.dma('sp', gt, norm_g.partition_broadcast(128), writes=[gtb])
    xb_ = [S.alloc([128, DM], F32, 'xt%d' % i) for i in range(2)]
    hb_ = [S.alloc([128, DM], BF16, 'hb%d' % i) for i in range(2)]
    junk, junkb = S.alloc([128, DM], BF16, 'junk')
    ss_ = [S.alloc([128, 1], F32, 'ss%d' % i) for i in range(2)]
    for tt in range(NT):
        xt, xtb = xb_[tt % 2]
        hb, hbb = hb_[tt % 2]
        ss, ssb = ss_[tt % 2]
        S.dma('sp', xt, x_in[tt * 128:(tt + 1) * 128, :], writes=[xtb])
        S.op('act', lambda e, xt=xt, ss=ss: e.activation(out=junk, in_=xt, func=AF.Square, accum_out=ss),
             reads=[xtb], writes=[junkb, ssb])
        S.op('dve', lambda e, ss=ss: e.tensor_scalar(out=ss, in0=ss, scalar1=1.0 / DM, scalar2=EPS, op0=ALU.mult,
                                                     op1=ALU.add), reads=[ssb], writes=[ssb])
        S.op('act', lambda e, ss=ss: e.activation(out=ss, in_=ss, func=AF.Sqrt), reads=[ssb], writes=[ssb])
        S.op('dve', lambda e, ss=ss: e.reciprocal(out=ss, in_=ss), reads=[ssb], writes=[ssb])
        S.op('dve', lambda e, xt=xt, ss=ss, hb=hb: e.scalar_tensor_tensor(out=hb, in0=xt, scalar=ss, in1=gt,
                                                                          op0=ALU.mult, op1=ALU.mult),
             reads=[xtb, ssb, gtb], writes=[hbb])
        for half in range(2):
            bank = 4 + half
            pv = transposes_to(bank, [hb[:, (half * 8 + i) * 128:(half * 8 + i + 1) * 128] for i in range(8)], [hbb])
            src = pv[:, 0:1024].rearrange("p (c t) -> p c t", c=8)
            dst = hT[:, half * 8:(half + 1) * 8, tt * 128:(tt + 1) * 128]
            if half == 0:
                S.op('act', lambda e, src=src, dst=dst: e.copy(out=dst, in_=src), reads=[S.psbuf[bank]], writes=[hTb])
            else:
                S.op('dve', lambda e, src=src, dst=dst: e.tensor_copy(out=dst, in_=src), reads=[S.psbuf[bank]],
                     writes=[hTb])
    S.barrier()
    S.release()
    if getattr(cfg, 'sub', '') == 'A0':
        S.release()
        return

    S.mark()
    qkT, w_fm = G['qkT'], G['w_fm']
    cw, cwb = S.alloc([128, NFC, 4], F32, 'cw')
    cb, cbb = S.alloc([128, NFC, 1], F32, 'cb')
    S.dma('sp', cw, G['convw'], writes=[cwb])
    S.dma('sp', cb, G['convb'], writes=[cbb])
    wfc_ = [S.alloc([128, 16, 128], BF16, 'wfc%d' % i) for i in range(2)]
    stage_ = [S.alloc([128, 515], F32, 'stage%d' % i) for i in range(2)]
    acc_ = [S.alloc([128, 512], F32, 'acc%d' % i) for i in range(2)]
    sg_ = [S.alloc([128, 512], F32, 'sg%d' % i) for i in range(2)]
    ob_ = [S.alloc([128, 512], BF16, 'ob%d' % i) for i in range(2)]
    it = 0
    for fc in range(NFC):
        wt, wtb = wfc_[fc % 2]
        S.dma('pool', wt, w_fm[:, fc * 128:(fc + 1) * 128].rearrange("(c p) n -> p c n", p=128), writes=[wtb])
        scale = 1.0 if fc < 2 * NH else 1.0 / 16.0
        s0, s0b = stage_[0]
        S.op('dve', lambda e, s0=s0: e.memset(s0[:, 0:3], 0.0), writes=[s0b])
        for tb in range(8):
            stg, stgb = stage_[tb % 2]
            acc, accb = acc_[it % 2]
            sg, sgb = sg_[it % 2]
            ob, obb = ob_[it % 2]
            bank = it % 2
            it += 1
            ps = S.ps(bank)
            for kc in range(16):
                S.op('pe', lambda e, ps=ps, wt=wt, kc=kc, tb=tb: e.matmul(
                    ps, lhsT=wt[:, kc, :], rhs=hT[:, kc, tb * 512:(tb + 1) * 512], start=(kc == 0), stop=(kc == 15)),
                    reads=[wtb, hTb], writes=[S.psbuf[bank]])
            S.op('act', lambda e, stg=stg, ps=ps: e.copy(out=stg[:, 3:515], in_=ps), reads=[S.psbuf[bank]],
                 writes=[stgb])
            S.op('dve', lambda e, acc=acc, stg=stg, fc=fc: e.tensor_scalar(
                out=acc, in0=stg[:, 3:515], scalar1=cw[:, fc, 3:4], scalar2=cb[:, fc, 0:1], op0=ALU.mult,
                op1=ALU.add), reads=[stgb, cwb, cbb], writes=[accb])
            for j in range(3):
                S.op('dve', lambda e, acc=acc, stg=stg, fc=fc, j=j: e.scalar_tensor_tensor(
                    out=acc, in0=stg[:, j:j + 512], scalar=cw[:, fc, j:j + 1], in1=acc, op0=ALU.mult, op1=ALU.add),
                    reads=[stgb, cwb, accb], writes=[accb])
            if tb < 7:
                nstg, nstgb = stage_[(tb + 1) % 2]
                S.op('act', lambda e, nstg=nstg, stg=stg: e.copy(out=nstg[:, 0:3], in_=stg[:, 512:515]),
                     reads=[stgb], writes=[nstgb])
            S.op('act', lambda e, sg=sg, acc=acc: e.activation(out=sg, in_=acc, func=AF.Sigmoid), reads=[accb],
                 writes=[sgb])
            S.op('dve', lambda e, ob=ob, acc=acc, sg=sg, scale=scale: e.scalar_tensor_tensor(
                out=ob, in0=acc, scalar=scale, in1=sg, op0=ALU.mult, op1=ALU.mult), reads=[accb, sgb], writes=[obb])
            S.dma('sp', qkT[fc * 128:(fc + 1) * 128, tb * 512:(tb + 1) * 512], ob, reads=[obb])
    wif, wifb = S.alloc([128, 16, 2 * NH], BF16, 'wif')
    S.dma('pool', wif, G['w_if'].rearrange("(c p) n -> p c n", p=128), writes=[wifb])
    rows_ = [S.alloc([NH, 2, 512], F32, 'rows%d' % i) for i in range(2)]
    for tb in range(8):
        rw, rwb = rows_[tb % 2]
        for k2 in range(2):
            bank = 2 + k2
            ps = S.ps(bank)
            for kc in range(16):
                S.op('pe', lambda e, ps=ps, kc=kc, tb=tb, k2=k2: e.matmul(
                    ps[0:NH, :], lhsT=wif[:, kc, k2 * NH:(k2 + 1) * NH], rhs=hT[:, kc, tb * 512:(tb + 1) * 512],
                    start=(kc == 0), stop=(kc == 15)), reads=[wifb, hTb], writes=[S.psbuf[bank]])
            S.op('act', lambda e, rw=rw, ps=ps, k2=k2: e.copy(out=rw[:, k2, :], in_=ps[0:NH, :]),
                 reads=[S.psbuf[bank]], writes=[rwb])
        S.dma('sp', G['gi'][:, tb * 512:(tb + 1) * 512], rw[:, 0, :], reads=[rwb])
        S.dma('sp', G['gf'][:, tb * 512:(tb + 1) * 512], rw[:, 1, :], reads=[rwb])
    S.barrier()
    S.release()
    if getattr(cfg, 'sub', '') == 'A1':
        S.release()
        return

    S.mark()
    w_tm = G['w_tm']
    wblk_ = [S.alloc([128, 16, 512], BF16, 'wblk%d' % i) for i in range(2)]
    sgt_ = [S.alloc([128, 512], F32, 'sgt%d' % i) for i in range(2)]
    t1_ = [S.alloc([128, 256], F32, 't1%d' % i) for i in range(2)]
    o_ = [S.alloc([128, 512], BF16, 'o%d' % i) for i in range(3)]
    ng_ = [S.alloc([128, 24], F32, 'ng%d' % i) for i in range(2)]
    rt_ = [S.alloc([128, 4, 8, 8], F32, 'rt%d' % i) for i in range(2)]
    stgT_ = [S.alloc([128, 4, 512], BF16, 'stgT%d' % i) for i in range(2)]
    it = 0
    for bi, (kind, idx, c0, W) in enumerate(cfg.blocks):
        if getattr(cfg, 'kinds', None) and kind not in cfg.kinds:
            continue
        wt, wtb = wblk_[bi % 2]
        S.dma('pool', wt[:, :, 0:W], w_tm[:, c0:c0 + W].rearrange("(c p) n -> p c n", p=128), writes=[wtb])
        for tt in range(NT):
            bank = it % 4
            ps = S.ps(bank)
            psb = S.psbuf[bank]
            o, ob = o_[it % 3]
            sgt, sgtb = sgt_[it % 2]
            t1, t1b = t1_[it % 2]
            it += 1
            rows = slice(tt * 128, (tt + 1) * 128)
            for kc in range(16):
                S.op('pe', lambda e, ps=ps, wt=wt, kc=kc, tt=tt, W=W: e.matmul(
                    ps[:, 0:W], lhsT=hT[:, kc, tt * 128:(tt + 1) * 128], rhs=wt[:, kc, 0:W], start=(kc == 0),
                    stop=(kc == 15)), reads=[wtb, hTb], writes=[psb])
            if kind == 'v_m':
                S.op('act', lambda e, o=o, ps=ps: e.copy(out=o, in_=ps), reads=[psb], writes=[ob])
                S.dma('sp', G['vm'][rows, idx * 512:(idx + 1) * 512], o, reads=[ob])
            elif kind == 'gate_m':
                S.op('act', lambda e, sgt=sgt, ps=ps: e.activation(out=sgt, in_=ps, func=AF.Sigmoid), reads=[psb],
                     writes=[sgtb])
                S.op('dve', lambda e, t1=t1, ps=ps, sgt=sgt: e.tensor_tensor(out=t1, in0=ps[:, 256:512],
                                                                             in1=sgt[:, 256:512], op=ALU.mult),
                     reads=[psb, sgtb], writes=[t1b])
                S.op('dve', lambda e, o=o, t1=t1, sgt=sgt: e.tensor_tensor(out=o[:, 0:256], in0=t1,
                                                                            in1=sgt[:, 0:256], op=ALU.mult),
                     reads=[t1b, sgtb], writes=[ob])
                S.dma('sp', G['gm'][rows, idx * 256:(idx + 1) * 256], o[:, 0:256], reads=[ob])
            elif kind in ('nq', 'nk'):
                nh = 8 if kind == 'nq' else 6
                ct, ctb, sn, snb = (cosk, ckb, sink, skb)
                sc = 0.125 if kind == 'nq' else 1.0
                rt, rtb = rt_[tt % 2]
                S.op('act', lambda e, sgt=sgt, ps=ps, sc=sc: e.activation(out=sgt, in_=ps, func=AF.Copy, scale=sc),
                     reads=[psb], writes=[sgtb])
                S.op('act', lambda e, o=o, sgt=sgt: e.copy(out=o, in_=sgt), reads=[sgtb], writes=[ob])
                ps3 = sgt.rearrange("p (h d) -> p h d", d=64)
                o3 = o.rearrange("p (h d) -> p h d", d=64)
                x1 = ps3[:, 0:nh, 0:8]
                x2 = ps3[:, 0:nh, 8:16]
                cbv = ct[:, tt:tt + 1, :].to_broadcast([128, nh, 8])
                sbv = sn[:, tt:tt + 1, :].to_broadcast([128, nh, 8])
                for k4, (a, bv) in enumerate([(x1, cbv), (x2, sbv), (x2, cbv), (x1, sbv)]):
                    S.op('dve', lambda e, rt=rt, k4=k4, a=a, bv=bv, nh=nh: e.tensor_tensor(
                        out=rt[:, k4, 0:nh, :], in0=a, in1=bv, op=ALU.mult), reads=[sgtb, ctb, snb], writes=[rtb])
                S.op('dve', lambda e, o3=o3, rt=rt, nh=nh: e.tensor_tensor(
                    out=o3[:, 0:nh, 0:8], in0=rt[:, 0, 0:nh, :], in1=rt[:, 1, 0:nh, :], op=ALU.subtract),
                    reads=[rtb], writes=[ob])
                S.op('dve', lambda e, o3=o3, rt=rt, nh=nh: e.tensor_tensor(
                    out=o3[:, 0:nh, 8:16], in0=rt[:, 2, 0:nh, :], in1=rt[:, 3, 0:nh, :], op=ALU.add),
                    reads=[rtb], writes=[ob])
                tbank = 4 + (tt % 2)
                pv = transposes_to(tbank, [o[:, j * 128:(j + 1) * 128] for j in range(4)], [ob])
                stg, stgb = stgT_[(tt // 4) % 2]
                q4 = tt % 4
                src = pv[:, 0:512].rearrange("p (j t) -> p j t", j=4)
                dst = stg[:, :, q4 * 128:(q4 + 1) * 128]
                if tt % 2 == 0:
                    S.op('act', lambda e, src=src, dst=dst: e.copy(out=dst, in_=src), reads=[S.psbuf[tbank]],
                         writes=[stgb])
                else:
                    S.op('dve', lambda e, src=src, dst=dst: e.tensor_copy(out=dst, in_=src), reads=[S.psbuf[tbank]],
                         writes=[stgb])
                if q4 == 3:
                    tb = tt // 4
                    if kind == 'nq':
                        dd = G['qTn'][idx].rearrange("(j p) t -> p j t", p=128)[:, :, tb * 512:(tb + 1) * 512]
                    else:
                        dd = G['kT4'][idx].rearrange("j p t -> p j t")[:, :, tb * 512:(tb + 1) * 512]
                    S.dma('sp', dd, stg, reads=[stgb])
            elif kind == 'nv':
                ngt, ngtb = ng_[tt % 2]
                S.op('act', lambda e, o=o, ps=ps: e.copy(out=o[:, 0:256], in_=ps[:, 0:256]), reads=[psb], writes=[ob])
                S.op('act', lambda e, ngt=ngt, ps=ps: e.activation(out=ngt, in_=ps[:, 256:280], func=AF.Sigmoid),
                     reads=[psb], writes=[ngtb])
                S.dma('sp', G['vsw'][idx][rows, :], o[:, 0:256], reads=[ob])
                S.dma('sp', G['ngs'][idx][rows, :], ngt, reads=[ngtb])
            elif kind == 'nz':
                S.op('act', lambda e, sgt=sgt, ps=ps: e.activation(out=sgt, in_=ps, func=AF.Sigmoid), reads=[psb],
                     writes=[sgtb])
                S.op('dve', lambda e, o=o, ps=ps, sgt=sgt: e.tensor_tensor(out=o, in0=ps, in1=sgt, op=ALU.mult),
                     reads=[psb, sgtb], writes=[ob])
                S.dma('sp', G['szn'][idx][rows, :], o, reads=[ob])
    S.barrier()
    S.release()
    S.release()


def rstd_inplace(S, ss, ssb, n, eps=EPS):
    S.op('dve', lambda e: e.tensor_scalar(out=ss, in0=ss, scalar1=1.0 / n, scalar2=eps, op0=ALU.mult, op1=ALU.add),
         reads=[ssb], writes=[ssb])
    S.op('act', lambda e: e.activation(out=ss, in_=ss, func=AF.Sqrt), reads=[ssb], writes=[ssb])
    S.op('dve', lambda e: e.reciprocal(out=ss, in_=ss), reads=[ssb], writes=[ssb])


def phase_B(S, cfg, G):
    NH = cfg.NH
    identf, identfb = G['identf'], G['identfb']
    transposes_to = G['transposes_to']
    qkT, vm, gm, mixT = G['qkT'], G['vm'], G['gm'], G['mixT']
    S.mark()
    cols, colsb = S.alloc([128, NT, 3, NH], F32, 'cols')
    decb, decbb = S.alloc([128, NH * NT], F32, 'decb')
    cm01, cm01b = S.alloc([128, 128], F32, 'cm01')
    S.op('pool', lambda e: e.memset(cm01, 1.0), writes=[cm01b])
    S.op('pool', lambda e: e.affine_select(out=cm01, in_=cm01, pattern=[[1, 128]], compare_op=ALU.is_ge, fill=0.0,
                                            base=0, channel_multiplier=-1), reads=[cm01b], writes=[cm01b])
    S.mark()
    T = [S.alloc([NH, SEQ], F32, 'T%d' % i) for i in range(8)]
    (T1, T1b), (T2, T2b), (T3, T3b), (T4, T4b), (T5, T5b), (T6, T6b), (T7, T7b), (T8, T8b) = T
    bi, bib = S.alloc([NH, 1], F32, 'bi')
    bf_, bfb = S.alloc([NH, 1], F32, 'bf')
    bs, bsb = S.alloc([NH, NT], F32, 'bs')
    be, beb = S.alloc([NH, NT], F32, 'be')
    dec, decrb = S.alloc([NH, NT], F32, 'dec')
    sel, selb = S.alloc([NH, NH, 128], F32, 'sel')
    S.dma('sp', T1, G['gi'], writes=[T1b])
    S.dma('sp', T2, G['gf'], writes=[T2b])
    S.dma('sp', bi, G['b_i'], writes=[bib])
    S.dma('sp', bf_, G['b_f'], writes=[bfb])
    S.op('dve', lambda e: e.tensor_scalar(out=bf_, in0=bf_, scalar1=-1.0, scalar2=None, op0=ALU.mult), reads=[bfb],
         writes=[bfb])
    S.op('dve', lambda e: e.tensor_scalar(out=T1, in0=T1, scalar1=bi, scalar2=None, op0=ALU.add), reads=[T1b, bib],
         writes=[T1b])
    S.op('act', lambda e: e.activation(out=T2, in_=T2, func=AF.Exp, scale=-1.0, bias=bf_), reads=[T2b, bfb],
         writes=[T2b])
    S.op('act', lambda e: e.activation(out=T2, in_=T2, func=AF.Ln, bias=1.0), reads=[T2b], writes=[T2b])
    S.op('pool', lambda e: e.memset(T8, 1.0), writes=[T8b])
    S.op('dve', lambda e: e.tensor_tensor_scan(out=T3, data0=T8, data1=T2, initial=0.0, op0=ALU.mult, op1=ALU.add),
         reads=[T8b, T2b], writes=[T3b])
    S.op('dve', lambda e: e.tensor_tensor(out=T1, in0=T1, in1=T3, op=ALU.add), reads=[T1b, T3b], writes=[T1b])
    S.op('dve', lambda e: e.tensor_tensor_scan(out=T4, data0=T1, data1=T1, initial=0.0, op0=ALU.max, op1=ALU.max),
         reads=[T1b], writes=[T4b])
    T3v = T3.rearrange("p (c l) -> p c l", l=128)
    T1v = T1.rearrange("p (c l) -> p c l", l=128)
    T4v = T4.rearrange("p (c l) -> p c l", l=128)
    S.op('dve', lambda e: e.memset(bs, 0.0), writes=[bsb])
    S.op('dve', lambda e: e.tensor_copy(out=bs[:, 1:NT], in_=T3v[:, 0:NT - 1, 127]), reads=[T3b, bsb], writes=[bsb])
    S.op('dve', lambda e: e.tensor_copy(out=be, in_=T3v[:, :, 127]), reads=[T3b], writes=[beb])
    bsB = bs.unsqueeze(2).to_broadcast([NH, NT, 128])
    beB = be.unsqueeze(2).to_broadcast([NH, NT, 128])
    for (dst, dstb, src, srcb, bb, bbb) in [(T5, T5b, T1v, T1b, bsB, bsb), (T6, T6b, T1v, T1b, beB, beb),
                                            (T7, T7b, T4v, T4b, bsB, bsb)]:
        dv = dst.rearrange("p (c l) -> p c l", l=128)
        S.op('dve', lambda e, dv=dv, src=src, bb=bb: e.tensor_tensor(out=dv, in0=src, in1=bb, op=ALU.subtract),
             reads=[srcb, bbb], writes=[dstb])
        S.op('act', lambda e, dst=dst: e.activation(out=dst, in_=dst, func=AF.Exp), reads=[dstb], writes=[dstb])
    S.op('dve', lambda e: e.tensor_tensor(out=dec, in0=bs, in1=be, op=ALU.subtract), reads=[bsb, beb], writes=[decrb])
    S.op('act', lambda e: e.activation(out=dec, in_=dec, func=AF.Exp), reads=[decrb], writes=[decrb])
    for c in range(NT):
        bank = 6 + (c % 2)
        pv = transposes_to(bank, [Tq[:, c * 128:(c + 1) * 128] for Tq in (T5, T6, T7)], [T5b, T6b, T7b], dtype=F32,
                           idn=identf, idb=identfb, width=NH)
        S.op('act' if c % 2 else 'dve',
             (lambda e, c=c, pv=pv: e.copy(out=cols[:, c, :, :], in_=pv[:, 0:3 * NH].rearrange("p (a h) -> p a h", a=3)))
             if c % 2 else
             (lambda e, c=c, pv=pv: e.tensor_copy(out=cols[:, c, :, :],
                                                  in_=pv[:, 0:3 * NH].rearrange("p (a h) -> p a h", a=3))),
             reads=[S.psbuf[bank]], writes=[colsb])
    S.op('pool', lambda e: e.memset(sel, 1.0), writes=[selb])
    S.op('pool', lambda e: e.affine_select(out=sel, in_=sel, pattern=[[-1, NH], [0, 128]], compare_op=ALU.is_equal,
                                            fill=0.0, base=0, channel_multiplier=1), reads=[selb], writes=[selb])
    pdec = S.ps(5)
    for h in range(NH):
        S.op('pe', lambda e, h=h: e.matmul(pdec[:, h * NT:(h + 1) * NT], lhsT=sel[:, h, :], rhs=dec, start=True,
                                           stop=True), reads=[selb, decrb], writes=[S.psbuf[5]])
    S.op('dve', lambda e: e.tensor_copy(out=decb, in_=pdec[:, 0:NH * NT]), reads=[S.psbuf[5]], writes=[decbb])
    S.barrier()
    S.release()

    S.mark()
    Sst, Sstb = S.alloc([128, 2, 257], F32, 'Sst')
    Sbf, Sbfb = S.alloc([128, 2, 257], BF16, 'Sbf')
    hd_ = []
    for i in range(2):
        hd_.append(dict(q=S.alloc([128, 2, SEQ], BF16, 'qTh%d' % i), k=S.alloc([128, 2, SEQ], BF16, 'kTh%d' % i),
                        v=S.alloc([128, NT, 257], BF16, 'vh%d' % i), g=S.alloc([128, NT, 256], BF16, 'gmh%d' % i),
                        n=S.alloc([128, 256], F32, 'mng%d' % i)))
    kt_ = [S.alloc([128, 256], BF16, 'kt%d' % i) for i in range(2)]
    wv_ = [S.alloc([128, 257], BF16, 'wv%d' % i) for i in range(2)]
    PT_ = [S.alloc([128, 128], BF16, 'PT%d' % i) for i in range(2)]
    dd_ = [S.alloc([128, 4], F32, 'dd%d' % i) for i in range(2)]
    hm_ = [S.alloc([128, 256], F32, 'hm%d' % i) for i in range(2)]
    st6_ = [S.alloc([128, 8], F32, 'st6%d' % i) for i in range(2)]
    y_ = [S.alloc([128, 256], F32, 'y%d' % i) for i in range(2)]
    mo_ = [S.alloc([128, 256], BF16, 'mo%d' % i) for i in range(2)]
    stgM_ = [S.alloc([128, 2, 512], BF16, 'stgM%d' % i) for i in range(2)]
    for hi in range(NH):
        H = hd_[hi % 2]
        (qT, qTb), (kT, kTb), (vh, vhb), (gh, ghb), (mn, mnb) = H['q'], H['k'], H['v'], H['g'], H['n']
        S.dma('sp', qT, qkT[hi * 256:(hi + 1) * 256, :].rearrange("(j p) t -> p j t", p=128), writes=[qTb])
        S.dma('sp', kT, qkT[(NH + hi) * 256:(NH + hi + 1) * 256, :].rearrange("(j p) t -> p j t", p=128), writes=[kTb])
        for c4 in range(4):
            S.dma('sp', vh[:, c4 * 8:(c4 + 1) * 8, 0:256],
                  vm[c4 * 1024:(c4 + 1) * 1024, hi * 256:(hi + 1) * 256].rearrange("(c p) d -> p c d", p=128),
                  writes=[vhb])
            S.dma('sp', gh[:, c4 * 8:(c4 + 1) * 8, :],
                  gm[c4 * 1024:(c4 + 1) * 1024, hi * 256:(hi + 1) * 256].rearrange("(c p) d -> p c d", p=128),
                  writes=[ghb])
        S.dma('sp', mn, G['mng'][hi * 256:(hi + 1) * 256].partition_broadcast(128), writes=[mnb])
        S.op('dve', lambda e, vh=vh: e.memset(vh[:, :, 256:257], 1.0), writes=[vhb])
        S.op('dve', lambda e: e.memset(Sst, 0.0), writes=[Sstb])
        S.op('dve', lambda e: e.memset(Sbf, 0.0), writes=[Sbfb])
        for c in range(NT):
            cs = slice(c * 128, (c + 1) * 128)
            kt, ktb = kt_[c % 2]
            wv, wvb = wv_[c % 2]
            PT, PTb = PT_[c % 2]
            dd, ddb = dd_[c % 2]
            hm, hmb = hm_[c % 2]
            st6, st6b = st6_[c % 2]
            y, yb = y_[c % 2]
            mo, mob = mo_[c % 2]
            stg, stgb = stgM_[(c // 4) % 2]
            pv = transposes_to(6, [kT[:, j, cs] for j in range(2)], [kTb])
            S.op('act', lambda e, kt=kt, pv=pv: e.copy(out=kt, in_=pv[:, 0:256]), reads=[S.psbuf[6]], writes=[ktb])
            S.op('dve', lambda e, wv=wv, vh=vh, c=c, hi=hi: e.tensor_scalar(
                out=wv, in0=vh[:, c, :], scalar1=cols[:, c, 1, hi:hi + 1], scalar2=None, op0=ALU.mult),
                reads=[vhb, colsb], writes=[wvb])
            b_s = c % 2
            psS = S.ps(b_s)
            for j in range(2):
                S.op('pe', lambda e, psS=psS, kT=kT, qT=qT, j=j, cs=cs: e.matmul(
                    psS[:, 0:128], lhsT=kT[:, j, cs], rhs=qT[:, j, cs], start=(j == 0), stop=(j == 1)),
                    reads=[kTb, qTb], writes=[S.psbuf[b_s]])
            S.op('dve', lambda e, PT=PT, psS=psS, c=c, hi=hi: e.scalar_tensor_tensor(
                out=PT, in0=psS[:, 0:128], scalar=cols[:, c, 0, hi:hi + 1], in1=cm01, op0=ALU.mult, op1=ALU.mult),
                reads=[S.psbuf[b_s], colsb, cm01b], writes=[PTb])
            b_a = 2 + (c % 2)
            pa = S.ps(b_a)
            S.op('pe', lambda e, pa=pa, PT=PT, vh=vh, c=c: e.matmul(pa[:, 0:257], lhsT=PT, rhs=vh[:, c, :], start=True,
                                                                   stop=False),
                 reads=[PTb, vhb], writes=[S.psbuf[b_a]])
            for j in range(2):
                S.op('pe', lambda e, pa=pa, qT=qT, j=j, cs=cs: e.matmul(pa[:, 0:257], lhsT=qT[:, j, cs], rhs=Sbf[:, j, :],
                                                                       start=False, stop=(j == 1)),
                     reads=[qTb, Sbfb], writes=[S.psbuf[b_a]])
            S.op('act', lambda e, dd=dd, pa=pa: e.activation(out=dd[:, 3:4], in_=pa[:, 256:257], func=AF.Abs),
                 reads=[S.psbuf[b_a]], writes=[ddb])
            S.op('dve', lambda e, dd=dd, c=c, hi=hi: e.tensor_scalar(
                out=dd[:, 0:1], in0=dd[:, 3:4], scalar1=cols[:, c, 2, hi:hi + 1], scalar2=None, op0=ALU.max),
                reads=[ddb, colsb], writes=[ddb])
            S.op('dve', lambda e, dd=dd: e.reciprocal(out=dd[:, 0:1], in_=dd[:, 0:1]), reads=[ddb], writes=[ddb])
            S.op('act', lambda e, hm=hm, pa=pa, dd=dd: e.activation(out=hm, in_=pa[:, 0:256], func=AF.Copy,
                                                                   scale=dd[:, 0:1]),
                 reads=[S.psbuf[b_a], ddb], writes=[hmb])
            S.op('dve', lambda e, st6=st6, hm=hm: e.bn_stats(out=st6[:, 0:6], in_=hm), reads=[hmb], writes=[st6b])
            S.op('dve', lambda e, st6=st6: e.bn_aggr(out=st6[:, 6:8], in_=st6[:, 0:6]), reads=[st6b], writes=[st6b])
            S.op('dve', lambda e, dd=dd, st6=st6: e.tensor_scalar(out=dd[:, 1:2], in0=st6[:, 7:8], scalar1=EPS,
                                                                 scalar2=None, op0=ALU.add),
                 reads=[st6b], writes=[ddb])
            S.op('act', lambda e, dd=dd: e.activation(out=dd[:, 1:2], in_=dd[:, 1:2], func=AF.Sqrt), reads=[ddb],
                 writes=[ddb])
            S.op('dve', lambda e, dd=dd: e.reciprocal(out=dd[:, 1:2], in_=dd[:, 1:2]), reads=[ddb], writes=[ddb])
            S.op('dve', lambda e, dd=dd, st6=st6: e.scalar_tensor_tensor(
                out=dd[:, 2:3], in0=st6[:, 6:7], scalar=-1.0, in1=dd[:, 1:2], op0=ALU.mult, op1=ALU.mult),
                reads=[st6b, ddb], writes=[ddb])
            S.op('act', lambda e, y=y, hm=hm, dd=dd: e.activation(out=y, in_=hm, func=AF.Identity, scale=dd[:, 1:2],
                                                                 bias=dd[:, 2:3]),
                 reads=[hmb, ddb], writes=[yb])
            S.op('dve', lambda e, y=y, mn=mn: e.tensor_tensor(out=y, in0=y, in1=mn, op=ALU.mult), reads=[yb, mnb],
                 writes=[yb])
            S.op('dve', lambda e, mo=mo, y=y, gh=gh, c=c: e.tensor_tensor(out=mo, in0=y, in1=gh[:, c, :], op=ALU.mult),
                 reads=[yb, ghb], writes=[mob])
            pv2 = transposes_to(7, [mo[:, j * 128:(j + 1) * 128] for j in range(2)], [mob])
            q4 = c % 4
            S.op('act', lambda e, stg=stg, pv2=pv2, q4=q4: e.copy(
                out=stg[:, :, q4 * 128:(q4 + 1) * 128], in_=pv2[:, 0:256].rearrange("p (j t) -> p j t", j=2)),
                reads=[S.psbuf[7]], writes=[stgb])
            if q4 == 3:
                c4 = c // 4
                S.dma('sp', mixT[hi * 256:(hi + 1) * 256, :].rearrange("(j p) t -> p j t", p=128)[
                    :, :, c4 * 512:(c4 + 1) * 512], stg, reads=[stgb])
            for j in range(2):
                pd = S.ps(4 + j)
                S.op('pe', lambda e, pd=pd, kt=kt, wv=wv, j=j: e.matmul(pd[:, 0:257], lhsT=kt[:, j * 128:(j + 1) * 128],
                                                                       rhs=wv, start=True, stop=True),
                     reads=[ktb, wvb], writes=[S.psbuf[4 + j]])
                S.op('dve', lambda e, pd=pd, j=j, c=c, hi=hi: e.scalar_tensor_tensor(
                    out=Sst[:, j, :], in0=Sst[:, j, :], scalar=decb[:, hi * NT + c:hi * NT + c + 1], in1=pd[:, 0:257],
                    op0=ALU.mult, op1=ALU.add), reads=[Sstb, decbb, S.psbuf[4 + j]], writes=[Sstb])
            S.op('act', lambda e: e.copy(out=Sbf, in_=Sst), reads=[Sstb], writes=[Sbfb])
    S.barrier()
    S.release()
    S.release()


def phase_C(S, cfg, G):
    NH, NP = cfg.NH, cfg.NP
    ident, identb = G['ident'], G['identb']
    transposes_to = G['transposes_to']
    mixT = G['mixT']
    S.mark()
    caus, causb = S.alloc([128, 128], F32, 'caus')
    onesf, onesfb = S.alloc([128, 128], F32, 'onesf')
    caus4, caus4b = S.alloc([128, 4, 128], BF16, 'caus4')
    anti4, anti4b = S.alloc([128, 4, 128], BF16, 'anti4')
    E, Eb = S.alloc([128, SEQ], BF16, 'E')
    S.op('pool', lambda e: e.memset(onesf, 1.0), writes=[onesfb])
    S.op('pool', lambda e: e.affine_select(out=caus, in_=onesf, pattern=[[1, 128]], compare_op=ALU.is_ge, fill=0.0,
                                            base=0, channel_multiplier=-1), reads=[onesfb], writes=[causb])
    S.op('dve', lambda e: e.tensor_copy(out=caus4, in_=caus.unsqueeze(1).to_broadcast([128, 4, 128])), reads=[causb],
         writes=[caus4b])
    S.op('dve', lambda e: e.tensor_scalar(out=anti4, in0=caus4, scalar1=-1.0, scalar2=1.0, op0=ALU.mult, op1=ALU.add),
         reads=[caus4b], writes=[anti4b])
    S.mark()
    Ef, Efb = S.alloc([128, SEQ], F32, 'Ef')
    Eg, Egb = S.alloc([128, SEQ], F32, 'Eg')
    for (Et, Etb, sh) in ((Ef, Efb, 0), (Eg, Egb, SEQ)):
        S.op('pool', lambda e, Et=Et: e.memset(Et, 1.0), writes=[Etb])
        S.op('pool', lambda e, Et=Et, sh=sh: e.affine_select(out=Et, in_=Et, pattern=[[1, SEQ]], compare_op=ALU.is_ge,
                                                             fill=0.0, base=sh, channel_multiplier=-64),
             reads=[Etb], writes=[Etb])
        S.op('pool', lambda e, Et=Et, sh=sh: e.affine_select(out=Et, in_=Et, pattern=[[-1, SEQ]], compare_op=ALU.is_ge,
                                                             fill=0.0, base=63 - sh, channel_multiplier=64),
             reads=[Etb], writes=[Etb])
    S.op('dve', lambda e: e.tensor_tensor(out=E, in0=Ef, in1=Eg, op=ALU.add), reads=[Efb, Egb], writes=[Eb])
    S.barrier()
    S.release()
    selb_t, selb_tb = S.alloc([128, NT, 64], F32, 'selb')
    S.dma('sp', selb_t, G['selbias'], writes=[selb_tb])
    ovt, ovtb = S.alloc([128, 2, 65], F32, 'ovt')
    S.dma('sp', ovt, G['ovl'].rearrange("(c p) n -> p c n", p=128), writes=[ovtb])
    w1 = {}
    w2 = {}
    pos = {}
    for nm, a1, a2, ap_ in (('k', G['ckw1'], G['ckw2'], G['posk']), ('v', G['cvw1'], G['cvw2'], G['posv'])):
        w1[nm] = S.alloc([128, 32, 128], BF16, 'w1' + nm)
        w2[nm] = S.alloc([128, 128], BF16, 'w2' + nm)
        pos[nm] = S.alloc([128, 32], BF16, 'pos' + nm)
        for hf in range(2):
            S.dma('pool', w1[nm][0][hf * 64:(hf + 1) * 64], a1, writes=[w1[nm][1]])
            S.dma('pool', pos[nm][0][hf * 64:(hf + 1) * 64], ap_, writes=[pos[nm][1]])
            S.dma('pool', w2[nm][0][:, hf * 64:(hf + 1) * 64], a2, writes=[w2[nm][1]])
    qTp, qTpb = S.alloc([128, 4, SEQ], BF16, 'qTp')
    k4, k4b = S.alloc([128, 4, SEQ], BF16, 'k4')
    vsw_t, vsw_tb = S.alloc([128, NT, 4, 65], BF16, 'vsw_t')
    ng_t, ng_tb = S.alloc([128, NT, 24], F32, 'ng_t')
    Rg, _ = S.alloc([128, 16384], BF16, 'Rg')
    kg1 = (Rg[:, 0:8192].rearrange("p (l c) -> p l c", l=32), Buf('kg'))
    kg = {'k': kg1, 'v': kg1}
    Pall = Rg.rearrange("p (k j t) -> p k j t", k=32, j=4)
    Pallb = [Buf('Pall%d' % i) for i in range(32)]
    Pw, _ = S.alloc([128, 5, 4, 128], BF16, 'Pw')
    Pwb = [Buf('Pw%d' % i) for i in range(5)]
    kcT, kcTb = S.alloc([128, 256], BF16, 'kcT')
    vca, vcab = S.alloc([128, 2, 2, 129], BF16, 'vca')
    biasc, biascb = S.alloc([128, 1], F32, 'biasc')
    xg, xgb = S.alloc([128, 256], F32, 'xg')
    x2, x2b = S.alloc([128, 256], F32, 'x2')
    gl, glb = S.alloc([128, 256], BF16, 'gl')
    sz_ = [S.alloc([128, 512], BF16, 'sz%d' % i) for i in range(2)]
    P_ = [S.alloc([128, 4, 128], BF16, 'P%d' % i) for i in range(4)]
    mk_ = [S.alloc([128, 128], F32, 'mk%d' % i) for i in range(2)]
    mkh_ = [S.alloc([128, 128], BF16, 'mkh%d' % i) for i in range(2)]
    rs_ = [S.alloc([128, 16], F32, 'rs%d' % i) for i in range(2)]
    imp_ = [S.alloc([128, 64], F32, 'imp%d' % i) for i in range(2)]
    sc2_ = [S.alloc([128, 64], F32, 'sc2%d' % i) for i in range(2)]
    m8_ = [S.alloc([128, 16], F32, 'm8%d' % i) for i in range(2)]
    nm_ = [S.alloc([128, 128], BF16, 'nm%d' % i) for i in range(2)]
    nmT1_ = [S.alloc([128, 128], BF16, 'nmT1%d' % i) for i in range(2)]
    nmT4_ = [S.alloc([128, 4, 128], BF16, 'qn%d' % i) for i in range(2)]
    ksE_ = [S.alloc([128, SEQ], BF16, 'ksE%d' % i) for i in range(2)]
    nacc_ = [S.alloc([128, 4, 64], F32, 'nacc%d' % i) for i in range(2)]
    no_ = [S.alloc([128, 256], BF16, 'no%d' % i) for i in range(2)]
    stgN_ = [S.alloc([128, 4, 512], BF16, 'stgN%d' % i) for i in range(2)]
    pit = 0
    for pi in range(NP):
        S.barrier()
        S.dma('sp', qTp, G['qTn'][pi].rearrange("(j p) t -> p j t", p=128), writes=[qTpb])
        S.dma('sp', k4, G['kT4'][pi].rearrange("j p t -> p j t"), writes=[k4b])
        for c4 in range(4):
            for a in range(4):
                S.dma('sp', vsw_t[:, c4 * 8:(c4 + 1) * 8, a, 0:64],
                      G['vsw'][pi][c4 * 1024:(c4 + 1) * 1024, a * 64:(a + 1) * 64].rearrange("(c p) d -> p c d", p=128),
                      writes=[vsw_tb])
        S.op('dve', lambda e: e.memset(vsw_t[:, :, :, 64:65], 1.0), writes=[vsw_tb])
        S.dma('sp', ng_t, G['ngs'][pi].rearrange("(c p) n -> p c n", p=128), writes=[ng_tb])
        for g2 in range(2):
            kst, kstb = ksE_[g2]
            ksl = slice(64 * g2, 64 * g2 + 64)
            esl = slice(64 * (1 - g2), 64 * (1 - g2) + 64)
            S.op('act', lambda e, kst=kst, ksl=ksl: e.copy(out=kst[ksl, :], in_=k4[ksl, 1, :]), reads=[k4b],
                 writes=[kstb])
            S.op('dve', lambda e, kst=kst, esl=esl: e.tensor_copy(out=kst[esl, :], in_=E[esl, :]), reads=[Eb],
                 writes=[kstb])

        for nm, srcidx in (('k', 0), ('v', 3)):
            k4v = k4[:, srcidx, :].rearrange("p (c s) -> p c s", s=16)
            for l in range(32):
                srcv = k4v[:, 0:255, l] if l < 16 else k4v[:, 1:256, l - 16]
                S.op('dve' if l % 2 else 'act',
                     (lambda e, l=l, srcv=srcv, nm=nm: e.tensor_copy(out=kg[nm][0][:, l, 0:255], in_=srcv)) if l % 2 else
                     (lambda e, l=l, srcv=srcv, nm=nm: e.copy(out=kg[nm][0][:, l, 0:255], in_=srcv)),
                     reads=[k4b], writes=[kg[nm][1]])
            for g2 in range(2):
                pb = 64 * g2
                w1t, w1b = w1[nm]
                w2t, w2b = w2[nm]
                pst, psb_ = pos[nm]
                pbias = S.ps(0)
                for l in range(32):
                    S.op('pe', lambda e, l=l, w1t=w1t, pst=pst, pb=pb: e.matmul(
                        pbias[:, 0:1], lhsT=w1t[pb:pb + 64, l, :], rhs=pst[pb:pb + 64, l:l + 1], start=(l == 0),
                        stop=(l == 31)), reads=[w1b, psb_], writes=[S.psbuf[0]])
                S.op('act', lambda e: e.copy(out=biasc, in_=pbias[:, 0:1]), reads=[S.psbuf[0]], writes=[biascb])
                phid = S.ps(1)
                for l in range(32):
                    S.op('pe', lambda e, l=l, w1t=w1t, pb=pb, srcidx=srcidx, nm=nm: e.matmul(
                        phid[:, 0:255], lhsT=w1t[pb:pb + 64, l, :], rhs=kg[nm][0][pb:pb + 64, l, 0:255],
                        start=(l == 0), stop=(l == 31)), reads=[w1b, kg[nm][1]], writes=[S.psbuf[1]])
                S.op('dve', lambda e: e.memset(xg, 0.0), writes=[xgb])
                S.op('act', lambda e: e.activation(out=xg[:, 0:255], in_=phid[:, 0:255], func=AF.Identity, bias=biasc),
                     reads=[S.psbuf[1], biascb, xgb], writes=[xgb])
                S.op('dve', lambda e: e.tensor_tensor(out=x2, in0=xg, in1=xg, op=ALU.mult), reads=[xgb], writes=[x2b])
                S.op('dve', lambda e: e.tensor_scalar(out=x2, in0=x2, scalar1=0.044715, scalar2=1.0, op0=ALU.mult,
                                                      op1=ALU.add), reads=[x2b], writes=[x2b])
                S.op('dve', lambda e: e.tensor_tensor(out=x2, in0=x2, in1=xg, op=ALU.mult), reads=[x2b, xgb], writes=[x2b])
                S.op('act', lambda e: e.activation(out=x2, in_=x2, func=AF.Sigmoid, scale=1.5957691216057308),
                     reads=[x2b], writes=[x2b])
                S.op('dve', lambda e: e.tensor_tensor(out=gl, in0=xg, in1=x2, op=ALU.mult), reads=[xgb, x2b], writes=[glb])
                if nm == 'k':
                    pk = S.ps(2)
                    S.op('pe', lambda e, w2t=w2t: e.matmul(pk[:, 0:256], lhsT=w2t, rhs=gl, start=True, stop=True),
                         reads=[w2b, glb], writes=[S.psbuf[2]])
                    S.op('act', lambda e, pb=pb: e.copy(out=kcT[pb:pb + 64, :], in_=pk[pb:pb + 64, 0:256]),
                         reads=[S.psbuf[2]], writes=[kcTb])
                else:
                    for ct in range(2):
                        pvv = S.ps(2)
                        S.op('pe', lambda e, w2t=w2t, ct=ct: e.matmul(pvv[:, 0:64], lhsT=gl[:, ct * 128:(ct + 1) * 128],
                                                                     rhs=w2t[:, 0:64], start=True, stop=True),
                             reads=[w2b, glb], writes=[S.psbuf[2]])
                        S.op('act', lambda e, g2=g2, ct=ct: e.copy(out=vca[:, g2, ct, 0:64], in_=pvv[:, 0:64]),
                             reads=[S.psbuf[2]], writes=[vcab])
                        S.op('dve', lambda e, g2=g2, ct=ct: e.tensor_copy(out=vca[:, g2, ct, 64:129], in_=ovt[:, ct, :]),
                             reads=[ovtb, vcab], writes=[vcab])
        base_row = NH * 256 + pi * 512
        lvl = getattr(cfg, 'clevel', 9)
        S.barrier()
        for qt in getattr(cfg, 'qts', range(NT)):
            szt, sztb = sz_[qt % 2]
            S.dma('sp', szt, G['szn'][pi][qt * 128:(qt + 1) * 128, :], writes=[sztb])
            stg, stgb = stgN_[(qt // 4) % 2]
            for g2 in range(2):
                pb = 64 * g2
                u = (qt * 2 + g2) % 2
                rs, rsb = rs_[u]
                imp, impb = imp_[u]
                sc2, sc2b = sc2_[u]
                m8, m8b = m8_[u]
                nmt, nmtb = nm_[u]
                nmT1, nmT1b = nmT1_[u]
                nmT4, nmT4b = nmT4_[u]
                nacc, naccb = nacc_[u]
                no, nob = no_[u]
                q4 = qTp[pb:pb + 64, :, qt * 128:(qt + 1) * 128]
                ngv = ng_t[:, qt, g2 * 12:(g2 + 1) * 12].rearrange("p (j b) -> p j b", b=3)
                nct = 1 if qt < 16 else 2
                pO = [S.ps(1), S.ps(2)]
                cslot = []
                for ct in range(nct):
                    P, Pb = P_[pit % 4]
                    cslot.append((P, Pb))
                    mk, mkb = mk_[pit % 2]
                    pit += 1
                    psc = S.ps(0)
                    S.op('pe', lambda e, psc=psc, ct=ct, pb=pb, q4=q4: e.matmul(
                        psc, lhsT=kcT[pb:pb + 64, ct * 128:(ct + 1) * 128], rhs=q4, start=True, stop=True),
                        reads=[kcTb, qTpb], writes=[S.psbuf[0]])
                    S.op('act', lambda e, P=P, psc=psc: e.activation(out=P.rearrange("p j t -> p (j t)"), in_=psc,
                                                                    func=AF.Exp), reads=[S.psbuf[0]], writes=[Pb])
                    S.op('pool', lambda e, mk=mk, ct=ct, qt=qt: e.affine_select(
                        out=mk, in_=onesf, pattern=[[1, 128]], compare_op=ALU.is_ge, fill=0.0,
                        base=-(2048 * ct - 128 * qt + 31), channel_multiplier=-16), reads=[onesfb], writes=[mkb])
                    mkh, mkhb = mkh_[pit % 2]
                    S.op('dve', lambda e, mkh=mkh, mk=mk: e.tensor_copy(out=mkh, in_=mk), reads=[mkb], writes=[mkhb])
                    S.op('dve', lambda e, P=P, mkh=mkh: e.tensor_tensor(
                        out=P, in0=P, in1=mkh.unsqueeze(1).to_broadcast([128, 4, 128]), op=ALU.mult), reads=[Pb, mkhb],
                        writes=[Pb])
                if lvl < 2:
                    continue
                for j in range(4):
                    for ct in range(nct):
                        P, Pb = cslot[ct]
                        S.op('pe', lambda e, j=j, P=P, ct=ct, g2=g2, nct=nct: e.matmul(
                            pO[j // 2][:, (j % 2) * 256:(j % 2) * 256 + 129], lhsT=P[:, j, :], rhs=vca[:, g2, ct, :],
                            start=(ct == 0), stop=(ct == nct - 1)), reads=[Pb, vcab], writes=[S.psbuf[1 + j // 2]])
                if lvl < 2:
                    continue
                for j in range(4):
                    S.op('act', lambda e, rs=rs, j=j: e.copy(
                        out=rs[:, j:j + 1], in_=pO[j // 2][:, (j % 2) * 256 + 64:(j % 2) * 256 + 65]),
                        reads=[S.psbuf[1 + j // 2]], writes=[rsb])
                S.op('dve', lambda e, rs=rs: e.tensor_scalar(out=rs[:, 0:4], in0=rs[:, 0:4], scalar1=1e-30, scalar2=None,
                                                             op0=ALU.max), reads=[rsb], writes=[rsb])
                S.op('dve', lambda e, rs=rs: e.reciprocal(out=rs[:, 0:4], in_=rs[:, 0:4]), reads=[rsb], writes=[rsb])
                S.op('act', lambda e, imp=imp, rs=rs: e.activation(out=imp, in_=pO[0][:, 65:129], func=AF.Copy,
                                                                  scale=rs[:, 0:1]),
                     reads=[S.psbuf[1], rsb], writes=[impb])
                for j in range(1, 4):
                    S.op('dve', lambda e, imp=imp, rs=rs, j=j: e.scalar_tensor_tensor(
                        out=imp, in0=pO[j // 2][:, (j % 2) * 256 + 65:(j % 2) * 256 + 129], scalar=rs[:, j:j + 1],
                        in1=imp, op0=ALU.mult, op1=ALU.add), reads=[S.psbuf[1 + j // 2], rsb, impb], writes=[impb])
                S.op('dve', lambda e, imp=imp, qt=qt: e.tensor_tensor(out=imp, in0=imp, in1=selb_t[:, qt, :], op=ALU.add),
                     reads=[impb, selb_tb], writes=[impb])
                if lvl < 3:
                    continue
                S.op('dve', lambda e, m8=m8, imp=imp: e.max(out=m8[:, 0:8], in_=imp), reads=[impb], writes=[m8b])
                S.op('dve', lambda e, sc2=sc2, m8=m8, imp=imp: e.match_replace(
                    out=sc2, in_to_replace=m8[:, 0:8], in_values=imp, imm_value=-3e38), reads=[impb, m8b], writes=[sc2b])
                S.op('dve', lambda e, m8=m8, sc2=sc2: e.max(out=m8[:, 8:16], in_=sc2), reads=[sc2b, m8b], writes=[m8b])
                for hf in range(2):
                    S.op('dve', lambda e, nmt=nmt, imp=imp, m8=m8, hf=hf: e.tensor_scalar(
                        out=nmt[:, hf * 64:(hf + 1) * 64], in0=imp, scalar1=m8[:, 15:16], scalar2=-30000.0,
                        op0=ALU.is_lt, op1=ALU.mult), reads=[impb, m8b], writes=[nmtb])
                pvt = transposes_to(7, [nmt], [nmtb])
                S.op('act', lambda e, nmT1=nmT1, pvt=pvt: e.copy(out=nmT1, in_=pvt[:, 0:128]), reads=[S.psbuf[7]],
                     writes=[nmT1b])
                qsl = slice(64 * g2, 64 * g2 + 64)
                msl = slice(64 * (1 - g2), 64 * (1 - g2) + 64)
                S.op('dve', lambda e, nmT4=nmT4, nmT1=nmT1, msl=msl: e.tensor_copy(
                    out=nmT4[msl], in_=nmT1[msl].unsqueeze(1).to_broadcast([64, 4, 128])), reads=[nmT1b],
                    writes=[nmT4b])
                S.op('act', lambda e, nmT4=nmT4, qsl=qsl, qt=qt: e.copy(
                    out=nmT4[qsl], in_=qTp[qsl, :, qt * 128:(qt + 1) * 128]), reads=[qTpb], writes=[nmT4b])
                if lvl < 4:
                    continue
                S.op('dve', lambda e, rs=rs, ngv=ngv: e.tensor_tensor(out=rs[:, 4:8], in0=rs[:, 0:4], in1=ngv[:, :, 0],
                                                                     op=ALU.mult), reads=[rsb, ng_tb], writes=[rsb])
                for j in range(4):
                    S.op('act', lambda e, nacc=nacc, rs=rs, j=j: e.activation(
                        out=nacc[:, j, :], in_=pO[j // 2][:, (j % 2) * 256:(j % 2) * 256 + 64], func=AF.Copy,
                        scale=rs[:, 4 + j:5 + j]), reads=[S.psbuf[1 + j // 2], rsb], writes=[naccb])
                for br in (1, 2):
                    if lvl < 5 or (br == 2 and lvl < 6):
                        continue
                    kts = list(range(qt + 1)) if br == 1 else list(range(max(0, qt - 4), qt + 1))
                    bankO = 5 if br == 1 else 6
                    pA = S.ps(bankO)
                    va = g2 if br == 1 else 2 + g2
                    slot = {}
                    for ki, kt in enumerate(kts):
                        if br == 1:
                            P, Pb = Pall[:, kt], Pallb[kt]
                        else:
                            P, Pb = Pw[:, ki], Pwb[ki]
                        slot[kt] = (P, Pb)
                        bsc = 3 + (pit % 2)
                        pit += 1
                        pss = S.ps(bsc)
                        if br == 1:
                            kst, kstb = ksE_[g2]
                            S.op('pe', lambda e, pss=pss, kt=kt, nmT4=nmT4, kst=kst: e.matmul(
                                pss, lhsT=kst[:, kt * 128:(kt + 1) * 128], rhs=nmT4, start=True, stop=True),
                                reads=[kstb, nmT4b], writes=[S.psbuf[bsc]])
                        else:
                            S.op('pe', lambda e, pss=pss, br=br, kt=kt, pb=pb, q4=q4: e.matmul(
                                pss, lhsT=k4[pb:pb + 64, br, kt * 128:(kt + 1) * 128], rhs=q4, start=True, stop=True),
                                reads=[k4b, qTpb], writes=[S.psbuf[bsc]])
                        S.op('act', lambda e, P=P, pss=pss: e.activation(out=P.rearrange("p j t -> p (j t)"), in_=pss,
                                                                        func=AF.Exp), reads=[S.psbuf[bsc]], writes=[Pb])
                        if kt == qt:
                            S.op('dve', lambda e, P=P: e.tensor_tensor(out=P, in0=P, in1=caus4, op=ALU.mult),
                                 reads=[Pb, caus4b], writes=[Pb])
                        if br == 2 and kt == qt - 4:
                            S.op('dve', lambda e, P=P: e.tensor_tensor(out=P, in0=P, in1=anti4, op=ALU.mult),
                                 reads=[Pb, anti4b], writes=[Pb])
                    for j in range(4):
                        for kt in kts:
                            P, Pb = slot[kt]
                            S.op('pe', lambda e, pA=pA, j=j, P=P, kt=kt, va=va, kts=kts: e.matmul(
                                pA[:, j * 128:j * 128 + 65], lhsT=P[:, j, :], rhs=vsw_t[:, kt, va, :],
                                start=(kt == kts[0]), stop=(kt == kts[-1])), reads=[Pb, vsw_tb], writes=[S.psbuf[bankO]])
                    o8 = 8 if br == 1 else 12
                    for j in range(4):
                        S.op('act', lambda e, rs=rs, pA=pA, o8=o8, j=j: e.copy(
                            out=rs[:, o8 + j:o8 + j + 1], in_=pA[:, j * 128 + 64:j * 128 + 65]),
                            reads=[S.psbuf[bankO]], writes=[rsb])
                    S.op('dve', lambda e, rs=rs, o8=o8: e.tensor_scalar(
                        out=rs[:, o8:o8 + 4], in0=rs[:, o8:o8 + 4], scalar1=1e-30, scalar2=None, op0=ALU.max),
                        reads=[rsb], writes=[rsb])
                    S.op('dve', lambda e, rs=rs, o8=o8: e.reciprocal(out=rs[:, o8:o8 + 4], in_=rs[:, o8:o8 + 4]),
                         reads=[rsb], writes=[rsb])
                    S.op('dve', lambda e, rs=rs, ngv=ngv, o8=o8, br=br: e.tensor_tensor(
                        out=rs[:, o8:o8 + 4], in0=rs[:, o8:o8 + 4], in1=ngv[:, :, br], op=ALU.mult),
                        reads=[rsb, ng_tb], writes=[rsb])
                    for j in range(4):
                        S.op('dve', lambda e, nacc=nacc, pA=pA, rs=rs, j=j, o8=o8: e.scalar_tensor_tensor(
                            out=nacc[:, j, :], in0=pA[:, j * 128:j * 128 + 64], scalar=rs[:, o8 + j:o8 + j + 1],
                            in1=nacc[:, j, :], op0=ALU.mult, op1=ALU.add), reads=[S.psbuf[bankO], rsb, naccb],
                            writes=[naccb])
                if lvl < 7:
                    continue
                S.op('dve', lambda e, no=no, nacc=nacc, szt=szt, g2=g2: e.tensor_tensor(
                    out=no, in0=nacc.rearrange("p j d -> p (j d)"), in1=szt[:, g2 * 256:(g2 + 1) * 256], op=ALU.mult),
                    reads=[naccb, sztb], writes=[nob])
                pv2 = transposes_to(7, [no[:, j * 128:(j + 1) * 128] for j in range(2)], [nob])
                q4i = qt % 4
                S.op('act', lambda e, stg=stg, pv2=pv2, q4i=q4i, g2=g2: e.copy(
                    out=stg[:, g2 * 2:g2 * 2 + 2, q4i * 128:(q4i + 1) * 128],
                    in_=pv2[:, 0:256].rearrange("p (j t) -> p j t", j=2)), reads=[S.psbuf[7]], writes=[stgb])
            if qt % 4 == 3 and lvl >= 7:
                tb = qt // 4
                S.dma('sp', mixT[base_row:base_row + 512, :].rearrange("(j p) t -> p j t", p=128)[
                    :, :, tb * 512:(tb + 1) * 512], stg, reads=[stgb])
    S.barrier()
    S.release()


def phase_D(S, cfg, G):
    TF = cfg.TF
    ident, identb = G['ident'], G['identb']
    transposes_to = G['transposes_to']
    mixT, xf_in, pf_in, out = G['mixG'], G['xf_in'], G['pf_in'], G['out']
    mixGb = G['mixGb']
    S.mark()
    wo, wob = S.alloc([128, 16, DM], BF16, 'wo')
    wg, wgb = S.alloc([128, 16, DM], BF16, 'wg')
    wp, wpb = S.alloc([128, 2, DM], BF16, 'wp')
    pgt, pgtb = S.alloc([128, DM], F32, 'pgt')
    fgt, fgtb = S.alloc([128, DM], F32, 'fgt')
    for cb in range(4):
        S.dma('pool', wo[:, :, cb * 512:(cb + 1) * 512],
              G['w_out'][:, cb * 512:(cb + 1) * 512].rearrange("(c p) n -> p c n", p=128), writes=[wob])
    for cb in range(4):
        S.dma('pool', wg[:, :, cb * 512:(cb + 1) * 512],
              G['w_pg'][:, cb * 512:(cb + 1) * 512].rearrange("(c p) n -> p c n", p=128), writes=[wgb])
    S.dma('pool', wp, G['w_pp'].rearrange("(c p) n -> p c n", p=128), writes=[wpb])
    S.dma('sp', pgt, G['ple_g'].partition_broadcast(128), writes=[pgtb])
    S.dma('sp', fgt, G['fin_g'].partition_broadcast(128), writes=[fgtb])
    mT, mTb = S.alloc([128, 16, 128], BF16, 'mT')
    xt, xtb = S.alloc([128, DM], F32, 'xt')
    x1, x1b = S.alloc([128, DM], F32, 'x1')
    x1h, x1hb = S.alloc([128, DM], BF16, 'x1h')
    x1T, x1Tb = S.alloc([128, 16, 128], BF16, 'x1T')
    pl, plb = S.alloc([128, DM], F32, 'pl')
    junk, junkb = S.alloc([128, DM], BF16, 'junkD')
    pt, ptb = S.alloc([128, 256], F32, 'pt')
    ph, phb = S.alloc([128, 256], BF16, 'ph')
    pT, pTb = S.alloc([128, 2, 128], BF16, 'pT')
    sgd_ = [S.alloc([128, 512], F32, 'sgd%d' % i) for i in range(2)]
    ss_ = [S.alloc([128, 1], F32, 'ssD%d' % i) for i in range(2)]
    mixv = mixT.rearrange("(c p) t -> p c t", p=128)
    it = 0
    for tt in range(TF // 128):
        rows = slice(tt * 128, (tt + 1) * 128)
        S.dma('sp', mT, mixv[:, :, rows], reads=[mixGb], writes=[mTb])
        S.dma('sp', xt, xf_in[rows, :], writes=[xtb])
        S.dma('sp', pt, pf_in[rows, :], writes=[ptb])
        S.op('act', lambda e: e.copy(out=ph, in_=pt), reads=[ptb], writes=[phb])
        pv = transposes_to(7, [ph[:, j * 128:(j + 1) * 128] for j in range(2)], [phb])
        S.op('act', lambda e, pv=pv: e.copy(out=pT, in_=pv[:, 0:256].rearrange("p (j t) -> p j t", j=2)),
             reads=[S.psbuf[7]], writes=[pTb])
        for cb in range(4):
            bank = it % 4
            it += 1
            ps = S.ps(bank)
            cs = slice(cb * 512, (cb + 1) * 512)
            for kc in range(16):
                S.op('pe', lambda e, ps=ps, kc=kc, cs=cs: e.matmul(ps, lhsT=mT[:, kc, :], rhs=wo[:, kc, cs],
                                                                  start=(kc == 0), stop=(kc == 15)),
                     reads=[mTb, wob], writes=[S.psbuf[bank]])
            S.op('dve', lambda e, ps=ps, cs=cs: e.tensor_tensor(out=x1[:, cs], in0=ps, in1=xt[:, cs], op=ALU.add),
                 reads=[S.psbuf[bank], xtb], writes=[x1b])
        S.op('act', lambda e: e.copy(out=x1h, in_=x1), reads=[x1b], writes=[x1hb])
        for half in range(2):
            bank = 4 + half
            pv = transposes_to(bank, [x1h[:, (half * 8 + i) * 128:(half * 8 + i + 1) * 128] for i in range(8)], [x1hb])
            src = pv[:, 0:1024].rearrange("p (c t) -> p c t", c=8)
            dst = x1T[:, half * 8:(half + 1) * 8, :]
            if half == 0:
                S.op('act', lambda e, src=src, dst=dst: e.copy(out=dst, in_=src), reads=[S.psbuf[bank]], writes=[x1Tb])
            else:
                S.op('dve', lambda e, src=src, dst=dst: e.tensor_copy(out=dst, in_=src), reads=[S.psbuf[bank]],
                     writes=[x1Tb])
        for cb in range(4):
            bank = it % 4
            it += 1
            ps = S.ps(bank)
            cs = slice(cb * 512, (cb + 1) * 512)
            for k in range(2):
                S.op('pe', lambda e, ps=ps, k=k, cs=cs: e.matmul(ps, lhsT=pT[:, k, :], rhs=wp[:, k, cs], start=(k == 0),
                                                                stop=(k == 1)),
                     reads=[pTb, wpb], writes=[S.psbuf[bank]])
            S.op('act', lambda e, ps=ps, cs=cs: e.copy(out=pl[:, cs], in_=ps), reads=[S.psbuf[bank]], writes=[plb])
        ss, ssb = ss_[0]
        S.op('act', lambda e, ss=ss: e.activation(out=junk, in_=pl, func=AF.Square, accum_out=ss), reads=[plb],
             writes=[junkb, ssb])
        rstd_inplace(S, ss, ssb, DM)
        S.op('dve', lambda e, ss=ss: e.scalar_tensor_tensor(out=pl, in0=pl, scalar=ss, in1=pgt, op0=ALU.mult,
                                                            op1=ALU.mult), reads=[plb, ssb, pgtb], writes=[plb])
        for cb in range(4):
            bank = it % 4
            it += 1
            ps = S.ps(bank)
            cs = slice(cb * 512, (cb + 1) * 512)
            sgd, sgdb = sgd_[cb % 2]
            for kc in range(16):
                S.op('pe', lambda e, ps=ps, kc=kc, cs=cs: e.matmul(ps, lhsT=x1T[:, kc, :], rhs=wg[:, kc, cs],
                                                                  start=(kc == 0), stop=(kc == 15)),
                     reads=[x1Tb, wgb], writes=[S.psbuf[bank]])
            S.op('act', lambda e, ps=ps, sgd=sgd: e.activation(out=sgd, in_=ps, func=AF.Sigmoid), reads=[S.psbuf[bank]],
                 writes=[sgdb])
            S.op('dve', lambda e, sgd=sgd, cs=cs: e.tensor_tensor(out=sgd, in0=sgd, in1=pl[:, cs], op=ALU.mult),
                 reads=[sgdb, plb], writes=[sgdb])
            S.op('dve', lambda e, sgd=sgd, cs=cs: e.tensor_tensor(out=x1[:, cs], in0=x1[:, cs], in1=sgd, op=ALU.add),
                 reads=[sgdb, x1b], writes=[x1b])
        ss2, ss2b = ss_[1]
        S.op('act', lambda e, ss2=ss2: e.activation(out=junk, in_=x1, func=AF.Square, accum_out=ss2), reads=[x1b],
             writes=[junkb, ss2b])
        rstd_inplace(S, ss2, ss2b, DM)
        S.op('dve', lambda e, ss2=ss2: e.scalar_tensor_tensor(out=xt, in0=x1, scalar=ss2, in1=fgt, op0=ALU.mult,
                                                              op1=ALU.mult), reads=[x1b, ss2b, fgtb, xtb], writes=[xtb])
        S.dma('sp', out[rows, :], xt, reads=[xtb])
    S.barrier()
    S.release()


def core_inputs(inp, cfg, b, tok0, wout_rows):
    f = np.float32
    w_in = inp['w_in'][0]
    sb, ov, inv = host_consts()
    heads = cfg.heads
    fm = cfg.fm_cols
    cw = inp['conv_w'][0]
    cbias = inp['conv_b'][0]
    ch = [c - OFF['mq'] for c in fm]
    d = {
        'x': np.ascontiguousarray(inp['x'][b]),
        'xf': np.ascontiguousarray(inp['x'][b, tok0:tok0 + cfg.TF]),
        'pf': np.ascontiguousarray(inp['p'][0, b, tok0:tok0 + cfg.TF]),
        'pos': np.ascontiguousarray(inp['positions'][b]).astype(np.int32),
        'norm_g': np.ascontiguousarray(inp['norm_g'][0]),
        'w_fm': np.ascontiguousarray(w_in[:, fm]),
        'w_if': np.ascontiguousarray(w_in[:, cfg.if_cols]),
        'w_tm': np.ascontiguousarray(w_in[:, cfg.tm_cols]),
        'convw': np.ascontiguousarray(cw[:, ch].T.reshape(-1, 128, 4).transpose(1, 0, 2)),
        'convb': np.ascontiguousarray(cbias[ch].reshape(-1, 128).T[:, :, None]),
        'b_i': np.ascontiguousarray(inp['b_igate'][0][heads][:, None]),
        'b_f': np.ascontiguousarray(inp['b_fgate'][0][heads][:, None]),
        'mng': np.ascontiguousarray(np.concatenate([inp['m_norm_g'][0][h * 256:(h + 1) * 256] for h in heads])),
        'posk': np.ascontiguousarray(inp['cmp_pos_k'][0].T),
        'posv': np.ascontiguousarray(inp['cmp_pos_v'][0].T),
        'ckw1': np.ascontiguousarray(inp['cmp_k_w1'][0].reshape(32, 64, 128).transpose(1, 0, 2)),
        'ckw2': np.ascontiguousarray(inp['cmp_k_w2'][0]),
        'cvw1': np.ascontiguousarray(inp['cmp_v_w1'][0].reshape(32, 64, 128).transpose(1, 0, 2)),
        'cvw2': np.ascontiguousarray(inp['cmp_v_w2'][0]),
        'w_out': np.ascontiguousarray(inp['w_out'][0][wout_rows]),
        'w_pg': np.ascontiguousarray(inp['ple_gate_w'][0]),
        'w_pp': np.ascontiguousarray(inp['ple_proj_w'][0]),
        'ple_g': np.ascontiguousarray(inp['ple_norm_g'][0]),
        'fin_g': np.ascontiguousarray(inp['final_norm_g']),
        'selbias': sb, 'ovl': ov, 'invf': inv,
    }
    return {k: (v if v.dtype == np.int32 else v.astype(f)) for k, v in d.items()}


_NC_CACHE = {}


def kernel(**inputs):
    inp = {k: np.asarray(v) for k, v in inputs.items()}
    cfgs = [Cfg([2 * hh, 2 * hh + 1], [(2 * hh, 2 * hh + 1)], SEQ, gather=True) for hh in range(2)]
    if 'nc' not in _NC_CACHE:
        _NC_CACHE['nc'] = build(cfgs[0])
    nc = _NC_CACHE['nc']
    rows = []
    for ci in range(len(cfgs[0].mix_rows) // 128):
        for hh in range(2):
            rows.extend(cfgs[hh].mix_rows[ci * 128:(ci + 1) * 128])
    in_maps = []
    for c in range(8):
        in_maps.append(core_inputs(inp, cfgs[c % 2], c // 2, 0, rows))
    res = run_bass_kernel_spmd(nc, in_maps, core_ids=list(range(8)))
    out = np.stack([np.asarray(res.results[2 * b]["out"]) for b in range(4)], axis=0)
    return out.astype(np.float32)
```

```python
import contextlib
import math
import numpy as np
import concourse.bass as bass
import concourse.mybir as mybir
from concourse.bass_utils import run_bass_kernel_spmd

F32 = mybir.dt.float32
BF16 = mybir.dt.bfloat16
I32 = mybir.dt.int32
U8 = mybir.dt.uint8
AF = mybir.ActivationFunctionType
ALU = mybir.AluOpType
AX = mybir.AxisListType
ENG = ['pe', 'act', 'dve', 'pool', 'sp']
DT_SIZE = {F32: 4, BF16: 2, I32: 4, U8: 1}

SEQ = 4096
DM = 2048
NT = SEQ // 128
EPS = 1e-6
FORCE = 1e3
NEG = -1e30


class Buf:
    __slots__ = ('name', 'w', 'r')

    def __init__(self, name=''):
        self.name = name
        self.w = None
        self.r = {}


class Sched:
    def __init__(self, nc, stack, n_dma_sems=8):
        self.nc = nc
        self.prog = {e: [] for e in ENG}
        self.cnt = {e: 0 for e in ENG}
        self.seen = {e: {} for e in ENG}
        self.esem = {e: stack.enter_context(nc.semaphore('es_' + e)) for e in ENG if e != 'sp'}
        self.dq = {}
        self.dsem = []
        self.dcnt = []
        for q in ('sp', 'act', 'pool'):
            ids = []
            for i in range(n_dma_sems):
                self.dsem.append(stack.enter_context(nc.semaphore('ds_%s%d' % (q, i))))
                self.dcnt.append(0)
                ids.append(len(self.dsem) - 1)
            self.dq[q] = [ids, 0]
        self.ccsem = stack.enter_context(nc.semaphore('cc_sem'))
        self.arena_bytes = 204 * 1024
        self.arena = stack.enter_context(nc.sbuf_tensor('arena', [128, self.arena_bytes], U8))
        self.arena_off = 0
        self.marks = []
        self.psum = [stack.enter_context(nc.psum_tensor('ps%d' % i, [128, 512], F32)) for i in range(8)]
        self.psbuf = [Buf('ps%d' % i) for i in range(8)]

    def alloc(self, shape, dtype, name=''):
        n = 1
        for s in shape[1:]:
            n *= s
        nbytes = (n * DT_SIZE[dtype] + 63) // 64 * 64
        off = self.arena_off
        assert off + nbytes <= self.arena_bytes, ('SBUF arena overflow', name, off, nbytes)
        self.arena_off += nbytes
        v = self.arena[0:shape[0], off:off + n * DT_SIZE[dtype]]
        if dtype != U8:
            v = v.bitcast(dtype)
        if len(shape) > 2:
            names = ' '.join('d%d' % i for i in range(len(shape) - 1))
            kw = {'d%d' % i: shape[i + 1] for i in range(len(shape) - 1)}
            v = v.rearrange('p (%s) -> p %s' % (names, names), **kw)
        return v, Buf(name)

    def mark(self):
        self.marks.append(self.arena_off)

    def release(self):
        self.arena_off = self.marks.pop()

    def ps(self, i, dtype=F32):
        v = self.psum[i][:, :]
        if dtype != F32:
            v = v.bitcast(dtype)
        return v

    @staticmethod
    def _key(tok):
        if tok[0] == 'e':
            return ('e', tok[1]), tok[2]
        return ('d', tok[1]), 16 * tok[2]

    def _waits(self, eng, need):
        out = []
        seen = self.seen[eng]
        for k, v in need.items():
            if k == ('e', 'pe') and eng == 'pe':
                continue
            if seen.get(k, 0) >= v:
                continue
            seen[k] = v
            sem = self.esem[k[1]] if k[0] == 'e' else (self.dsem[k[1]] if k[0] == 'd' else self.ccsem)
            out.append((sem, v))
        return out

    @staticmethod
    def _need(reads, writes, extra=()):
        need = {}

        def add(k, v):
            if need.get(k, 0) < v:
                need[k] = v
        for b in reads:
            if b.w is not None:
                add(*b.w)
        for b in writes:
            if b.w is not None:
                add(*b.w)
            for k, v in b.r.items():
                add(k, v)
        for k, v in extra:
            add(k, v)
        return need

    @staticmethod
    def _commit(kv, reads, writes):
        k, v = kv
        for b in reads:
            if b.r.get(k, 0) < v:
                b.r[k] = v
        for b in writes:
            b.w = kv
            b.r = {}

    def op(self, eng, fn, reads=(), writes=()):
        waits = self._waits(eng, self._need(reads, writes))
        self.cnt[eng] += 1
        self._commit((('e', eng), self.cnt[eng]), reads, writes)
        self.prog[eng].append((waits, fn, (self.esem[eng], 1)))

    def dma(self, q, out, in_, reads=(), writes=(), **kw):
        ids, rr = self.dq[q]
        j = ids[rr % len(ids)]
        self.dq[q][1] = rr + 1
        extra = [(('d', j), 16 * self.dcnt[j])] if self.dcnt[j] > 0 else []
        waits = self._waits(q, self._need(reads, writes, extra))
        self.dcnt[j] += 1
        self._commit((('d', j), 16 * self.dcnt[j]), reads, writes)
        self.prog[q].append((waits, (lambda e, o=out, i=in_, kw=kw: e.dma_start(out=o, in_=i, **kw)),
                             (self.dsem[j], 16)))

    def collective(self, fn, reads=(), writes=()):
        waits = self._waits('pool', self._need(reads, writes))
        self.cccnt = getattr(self, 'cccnt', 0) + 1
        self._commit((('c', 0), self.cccnt), reads, writes)
        self.prog['pool'].append((waits, fn, (self.ccsem, None)))
        self.seen['pool'][('c', 0)] = self.cccnt
        self.prog['pool'].append(([(self.ccsem, self.cccnt)], None, None))

    def barrier(self):
        need = {}
        for e in ENG:
            if e != 'sp' and self.cnt[e] > 0:
                need[('e', e)] = self.cnt[e]
        for j, c in enumerate(self.dcnt):
            if c > 0:
                need[('d', j)] = 16 * c
        for e in ENG:
            waits = self._waits(e, need)
            if waits:
                self.prog[e].append((waits, None, None))

    def emit(self):
        nc = self.nc
        with nc.Block() as block:
            def replay(lst, eng):
                for waits, fn, inc in lst:
                    for sem, v in waits:
                        eng.wait_ge(sem, v)
                    if fn is not None:
                        if inc[1] is None:
                            fn(eng).then_inc(inc[0])
                        else:
                            fn(eng).then_inc(inc[0], inc[1])

            @block.sync
            def _(e):
                replay(self.prog['sp'], e)

            @block.scalar
            def _(e):
                replay(self.prog['act'], e)

            @block.vector
            def _(e):
                replay(self.prog['dve'], e)

            @block.gpsimd
            def _(e):
                replay(self.prog['pool'], e)

            @block.tensor
            def _(e):
                replay(self.prog['pe'], e)


OFF = {}
_o = 0
for _n, _w in [('mq', 1024), ('mk', 1024), ('mv', 1024), ('mo', 1024), ('mz', 1024), ('mi', 4), ('mf', 4),
               ('nq', 1024), ('kc', 256), ('vc', 256), ('ks', 256), ('vs', 256), ('kw', 256), ('vw', 256),
               ('ng', 48), ('nz', 1024)]:
    OFF[_n] = _o
    _o += _w
assert _o == 8760


class Cfg:
    def __init__(self, heads, pairs, tf, debug=False, phases='ABCD', gather=False):
        self.gather = gather
        self.heads = heads
        self.pairs = pairs
        self.NH = len(heads)
        self.NP = len(pairs)
        self.TF = tf
        self.debug = debug
        self.phases = phases
        blocks = []
        cols = []

        def add(kind, idx, c):
            blocks.append((kind, idx, len(cols), len(c)))
            cols.extend(c)
        hv = []
        for h in heads:
            hv.extend(range(OFF['mv'] + h * 256, OFF['mv'] + (h + 1) * 256))
        for i in range(0, len(hv), 512):
            add('v_m', i // 512, hv[i:i + 512])
        for i, h in enumerate(heads):
            add('gate_m', i, list(range(OFF['mo'] + h * 256, OFF['mo'] + (h + 1) * 256)) +
                list(range(OFF['mz'] + h * 256, OFF['mz'] + (h + 1) * 256)))
        for pi, (g0, g1) in enumerate(pairs):
            c = []
            for jj in range(4):
                for g in (g0, g1):
                    hh = 4 * g + jj
                    c.extend(range(OFF['nq'] + hh * 64, OFF['nq'] + (hh + 1) * 64))
            add('nq', pi, c)
            c = []
            for nm in ('kc', 'ks', 'kw', 'vc'):
                for g in (g0, g1):
                    c.extend(range(OFF[nm] + g * 64, OFF[nm] + (g + 1) * 64))
            add('nk', pi, c)
            c = []
            for nm in ('vs', 'vw'):
                for g in (g0, g1):
                    c.extend(range(OFF[nm] + g * 64, OFF[nm] + (g + 1) * 64))
            for g in (g0, g1):
                c.extend(range(OFF['ng'] + g * 12, OFF['ng'] + (g + 1) * 12))
            add('nv', pi, c)
            c = []
            for g in (g0, g1):
                c.extend(range(OFF['nz'] + g * 256, OFF['nz'] + (g + 1) * 256))
            add('nz', pi, c)
        self.blocks = blocks
        self.tm_cols = cols
        fm = []
        for h in heads:
            fm.extend(range(OFF['mq'] + h * 256, OFF['mq'] + (h + 1) * 256))
        for h in heads:
            fm.extend(range(OFF['mk'] + h * 256, OFF['mk'] + (h + 1) * 256))
        self.fm_cols = fm
        self.if_cols = [OFF['mi'] + h for h in heads] + [OFF['mf'] + h for h in heads]
        mr = []
        for h in heads:
            mr.extend(range(h * 256, (h + 1) * 256))
        for (g0, g1) in pairs:
            for g in (g0, g1):
                mr.extend(range(1024 + g * 256, 1024 + (g + 1) * 256))
        self.mix_rows = mr


def host_consts():
    t = np.arange(SEQ)
    n = np.arange(64)
    cur = (t // 64)[:, None]
    valid = n[None, :] * 64 <= t[:, None]
    forced = (n[None, :] == 0) | (n[None, :] == cur) | (n[None, :] == cur - 1)
    sb = np.where(valid, np.where(forced, FORCE, 0.0), NEG).astype(np.float32)
    sb = sb.reshape(NT, 128, 64).transpose(1, 0, 2).copy()
    ov = np.zeros((256, 65), np.float32)
    for c in range(255):
        tok = c * 16 + np.arange(32)
        ov[c, 0] = 1.0
        for nn in np.unique(tok // 64):
            ov[c, 1 + nn] = np.mean(tok // 64 == nn)
    inv = (500000.0 ** (-(np.arange(8, dtype=np.float32) * 2.0 / 16))).astype(np.float32)
    return sb, ov, inv


def build(cfg):
    nc = bass.Bass("TRN2", target_bir_lowering=False)
    NH, NP, TF = cfg.NH, cfg.NP, cfg.TF
    NFC = 4 * NH
    WTM = len(cfg.tm_cols)
    NMIX = NH * 256 + NP * 512
    skind = "ExternalOutput" if cfg.debug else "Internal"

    def din(name, shape, dt=F32):
        return nc.dram_tensor(name, list(shape), dt, kind="ExternalInput").ap()

    def dscr(name, shape, dt):
        k = "ExternalOutput" if name in getattr(cfg, 'dump', ()) else "Internal"
        if name in getattr(cfg, 'feed', ()):
            k = "ExternalInput"
        return nc.dram_tensor(name, list(shape), dt, kind=k).ap()

    x_in = din("x", [SEQ, DM])
    xf_in = din("xf", [TF, DM])
    pf_in = din("pf", [TF, 256])
    pos_in = din("pos", [SEQ], I32)
    norm_g = din("norm_g", [DM])
    w_fm = din("w_fm", [DM, NFC * 128])
    w_if = din("w_if", [DM, 2 * NH])
    w_tm = din("w_tm", [DM, WTM])
    convw = din("convw", [128, NFC, 4])
    convb = din("convb", [128, NFC, 1])
    b_i = din("b_i", [NH, 1])
    b_f = din("b_f", [NH, 1])
    mng = din("mng", [NH * 256])
    posk = din("posk", [64, 32])
    posv = din("posv", [64, 32])
    ckw1 = din("ckw1", [64, 32, 128])
    ckw2 = din("ckw2", [128, 64])
    cvw1 = din("cvw1", [64, 32, 128])
    cvw2 = din("cvw2", [128, 64])
    w_out = din("w_out", [DM, DM])
    w_pg = din("w_pg", [DM, DM])
    w_pp = din("w_pp", [256, DM])
    ple_g = din("ple_g", [DM])
    fin_g = din("fin_g", [DM])
    selbias = din("selbias", [128, NT, 64])
    ovl = din("ovl", [256, 65])
    invf = din("invf", [8])
    out = nc.dram_tensor("out", [TF, DM], F32, kind="ExternalOutput").ap()

    qkT = dscr("qkT", [NFC * 128, SEQ], BF16)
    gi = dscr("gi", [NH, SEQ], F32)
    gf = dscr("gf", [NH, SEQ], F32)
    vm = dscr("vm", [SEQ, NH * 256], BF16)
    gm = dscr("gm", [SEQ, NH * 256], BF16)
    qTn = dscr("qTn", [NP, 512, SEQ], BF16)
    kT4 = dscr("kT4", [NP, 4, 128, SEQ], BF16)
    vsw = dscr("vsw", [NP, SEQ, 256], BF16)
    ngs = dscr("ngs", [NP, SEQ, 24], F32)
    szn = dscr("szn", [NP, SEQ, 512], BF16)
    mixT = dscr("mixT", [NMIX if cfg.gather else DM, SEQ], BF16)
    CCH = 128
    mixG = dscr("mixG", [DM, SEQ], BF16) if cfg.gather else mixT
    mixGb = Buf('mixG')

    with contextlib.ExitStack() as st:
        S = Sched(nc, st)
        ident, identb = S.alloc([128, 128], BF16, 'ident')
        identf, identfb = S.alloc([128, 128], F32, 'identf')
        S.op('pool', lambda e: e.memset(identf, 1.0), writes=[identfb])
        S.op('pool', lambda e: e.affine_select(out=identf, in_=identf, pattern=[[-1, 128]], compare_op=ALU.is_equal,
                                                fill=0.0, base=0, channel_multiplier=1),
             reads=[identfb], writes=[identfb])
        S.op('dve', lambda e: e.tensor_copy(out=ident, in_=identf), reads=[identfb], writes=[identb])

        def transposes_to(pe_bank, srcs, src_bufs, dtype=BF16, idn=None, idb=None, width=128):
            pv = S.ps(pe_bank, dtype)
            idn = ident if idn is None else idn
            idb = identb if idb is None else idb
            for i, s_ap in enumerate(srcs):
                S.op('pe', lambda e, i=i, s_ap=s_ap: e.transpose(
                    out=pv[0:s_ap.shape[1], i * width:i * width + s_ap.shape[0]], in_=s_ap,
                    identity=idn[0:s_ap.shape[0], 0:s_ap.shape[0]]),
                    reads=list(src_bufs) + [idb], writes=[S.psbuf[pe_bank]])
            return pv

        if 'A' in cfg.phases:
            phase_A(S, cfg, locals())
        if 'B' in cfg.phases:
            phase_B(S, cfg, locals())
        if 'C' in cfg.phases:
            phase_C(S, cfg, locals())
        if cfg.gather:
            S.barrier()
            for ci in range(NMIX // CCH):
                S.collective(lambda e, ci=ci: e.collective_compute(
                    "AllGather", ALU.bypass, replica_groups=[[0, 1], [2, 3], [4, 5], [6, 7]],
                    ins=[mixT[ci * CCH:(ci + 1) * CCH, :].opt()],
                    outs=[mixG[ci * 2 * CCH:(ci + 1) * 2 * CCH, :].opt()]), writes=[mixGb])
        if 'D' in cfg.phases:
            phase_D(S, cfg, locals())
        S.barrier()
        S.emit()
    return nc


def phase_A(S, cfg, G):
    NH, NP = cfg.NH, cfg.NP
    NFC = 4 * NH
    ident, identb = G['ident'], G['identb']
    transposes_to = G['transposes_to']
    x_in, pos_in, norm_g = G['x_in'], G['pos_in'], G['norm_g']
    S.mark()
    hT, hTb = S.alloc([128, 16, SEQ], BF16, 'hT')
    cosq, cqb = S.alloc([128, NT, 8], F32, 'cosq')
    sinq, sqb = S.alloc([128, NT, 8], F32, 'sinq')
    cosk, ckb = S.alloc([128, NT, 8], F32, 'cosk')
    sink, skb = S.alloc([128, NT, 8], F32, 'sink')

    S.mark()
    posi, posib = S.alloc([128, NT], I32, 'posi')
    posf, posfb = S.alloc([128, NT], F32, 'posf')
    invt, invtb = S.alloc([128, 8], F32, 'invt')
    ang, angb = S.alloc([128, NT, 8], F32, 'ang')
    tq, tqb = S.alloc([128, NT, 8], F32, 'tq')
    tn, tnb = S.alloc([128, NT, 8], F32, 'tn')
    tr, trb = S.alloc([128, NT, 8], F32, 'tr')
    S.dma('sp', posi, G['pos_in'].rearrange("(c p) -> p c", p=128), writes=[posib], allow_slow_non_contiguous=True)
    S.dma('sp', invt, G['invf'].partition_broadcast(128), writes=[invtb])
    S.op('dve', lambda e: e.tensor_copy(out=posf, in_=posi), reads=[posib], writes=[posfb])
    S.op('dve', lambda e: e.tensor_tensor(out=ang, in0=posf.unsqueeze(2).to_broadcast([128, NT, 8]),
                                          in1=invt.unsqueeze(1).to_broadcast([128, NT, 8]), op=ALU.mult),
         reads=[posfb, invtb], writes=[angb])
    TWO_PI = 2.0 * math.pi
    C1 = 6.28125
    C2 = TWO_PI - C1
    MAGIC = 12582912.0

    def sin_of(dst, dstb, shift, scale):
        S.op('dve', lambda e: e.tensor_scalar(out=tq, in0=ang, scalar1=shift, scalar2=1.0 / TWO_PI, op0=ALU.add,
                                              op1=ALU.mult), reads=[angb], writes=[tqb])
        S.op('dve', lambda e: e.tensor_scalar(out=tn, in0=tq, scalar1=MAGIC, scalar2=None, op0=ALU.add),
             reads=[tqb], writes=[tnb])
        S.op('dve', lambda e: e.tensor_scalar(out=tn, in0=tn, scalar1=MAGIC, scalar2=None, op0=ALU.subtract),
             reads=[tnb], writes=[tnb])
        S.op('dve', lambda e: e.scalar_tensor_tensor(out=tr, in0=tn, scalar=-C1, in1=ang, op0=ALU.mult, op1=ALU.add),
             reads=[tnb, angb], writes=[trb])
        S.op('dve', lambda e: e.tensor_scalar(out=tr, in0=tr, scalar1=shift, scalar2=None, op0=ALU.add),
             reads=[trb], writes=[trb])
        S.op('dve', lambda e: e.scalar_tensor_tensor(out=tr, in0=tn, scalar=-C2, in1=tr, op0=ALU.mult, op1=ALU.add),
             reads=[tnb, trb], writes=[trb])
        S.op('dve', lambda e: e.tensor_scalar(out=tr, in0=tr, scalar1=3.1415925, scalar2=-3.1415925, op0=ALU.min,
                                              op1=ALU.max), reads=[trb], writes=[trb])
        S.op('act', lambda e: e.activation(out=dst, in_=tr, func=AF.Sin), reads=[trb], writes=[dstb])
        if scale != 1.0:
            pass

    sin_of(sink, skb, 0.0, 1.0)
    sin_of(cosk, ckb, math.pi / 2, 1.0)
    S.op('dve', lambda e: e.tensor_scalar(out=sinq, in0=sink, scalar1=0.125, scalar2=None, op0=ALU.mult),
         reads=[skb], writes=[sqb])
    S.op('dve', lambda e: e.tensor_scalar(out=cosq, in0=cosk, scalar1=0.125, scalar2=None, op0=ALU.mult),
         reads=[ckb], writes=[cqb])

    gt, gtb = S.alloc([128, DM], F32, 'gt')
    S.dma('sp', gt, norm_g.partition_broadcast(128), writes=[gtb])
    xb_ = [S.alloc([128, DM], F32, 'xt%d' % i) for i in range(2)]
    hb_ = [S.alloc([128, DM], BF16, 'hb%d' % i) for i in range(2)]
    junk, junkb = S.alloc([128, DM], BF16, 'junk')
    ss_ = [S.alloc([128, 1], F32, 'ss%d' % i) for i in range(2)]
    for tt in range(NT):
        xt, xtb = xb_[tt % 2]
        hb, hbb = hb_[tt % 2]
        ss, ssb = ss_[tt % 2]
        S.dma('sp', xt, x_in[tt * 128:(tt + 1) * 128, :], writes=[xtb])
        S.op('act', lambda e, xt=xt, ss=ss: e.activation(out=junk, in_=xt, func=AF.Square, accum_out=ss),
             reads=[xtb], writes=[junkb, ssb])
        S.op('dve', lambda e, ss=ss: e.tensor_scalar(out=ss, in0=ss, scalar1=1.0 / DM, scalar2=EPS, op0=ALU.mult,
                                                     op1=ALU.add), reads=[ssb], writes=[ssb])
        S.op('act', lambda e, ss=ss: e.activation(out=ss, in_=ss, func=AF.Sqrt), reads=[ssb], writes=[ssb])
        S.op('dve', lambda e, ss=ss: e.reciprocal(out=ss, in_=ss), reads=[ssb], writes=[ssb])
        S.op('dve', lambda e, xt=xt, ss=ss, hb=hb: e.scalar_tensor_tensor(out=hb, in0=xt, scalar=ss, in1=gt,
                                                                          op0=ALU.mult, op1=ALU.mult),
             reads=[xtb, ssb, gtb], writes=[hbb])
        for half in range(2):
            bank = 4 + half
            pv = transposes_to(bank, [hb[:, (half * 8 + i) * 128:(half * 8 + i + 1) * 128] for i in range(8)], [hbb])
            src = pv[:, 0:1024].rearrange("p (c t) -> p c t", c=8)
            dst = hT[:, half * 8:(half + 1) * 8, tt * 128:(tt + 1) * 128]
            if half == 0:
                S.op('act', lambda e, src=src, dst=dst: e.copy(out=dst, in_=src), reads=[S.psbuf[bank]], writes=[hTb])
            else:
                S.op('dve', lambda e, src=src, dst=dst: e.tensor_copy(out=dst, in_=src), reads=[S.psbuf[bank]],
                     writes=[hTb])
    S.barrier()
    S.release()
    if getattr(cfg, 'sub', '') == 'A0':
        S.release()
        return

    S.mark()
    qkT, w_fm = G['qkT'], G['w_fm']
    cw, cwb = S.alloc([128, NFC, 4], F32, 'cw')
    cb, cbb = S.alloc([128, NFC, 1], F32, 'cb')
    S.dma('sp', cw, G['convw'], writes=[cwb])
    S.dma('sp', cb, G['convb'], writes=[cbb])
    wfc_ = [S.alloc([128, 16, 128], BF16, 'wfc%d' % i) for i in range(2)]
    stage_ = [S.alloc([128, 515], F32, 'stage%d' % i) for i in range(2)]
    acc_ = [S.alloc([128, 512], F32, 'acc%d' % i) for i in range(2)]
    sg_ = [S.alloc([128, 512], F32, 'sg%d' % i) for i in range(2)]
    ob_ = [S.alloc([128, 512], BF16, 'ob%d' % i) for i in range(2)]
    it = 0
    for fc in range(NFC):
        wt, wtb = wfc_[fc % 2]
        S.dma('pool', wt, w_fm[:, fc * 128:(fc + 1) * 128].rearrange("(c p) n -> p c n", p=128), writes=[wtb])
        scale = 1.0 if fc < 2 * NH else 1.0 / 16.0
        s0, s0b = stage_[0]
        S.op('dve', lambda e, s0=s0: e.memset(s0[:, 0:3], 0.0), writes=[s0b])
        for tb in range(8):
            stg, stgb = stage_[tb % 2]
            acc, accb = acc_[it % 2]
            sg, sgb = sg_[it % 2]
            ob, obb = ob_[it % 2]
            bank = it % 2
            it += 1
            ps = S.ps(bank)
            for kc in range(16):
                S.op('pe', lambda e, ps=ps, wt=wt, kc=kc, tb=tb: e.matmul(
                    ps, lhsT=wt[:, kc, :], rhs=hT[:, kc, tb * 512:(tb + 1) * 512], start=(kc == 0), stop=(kc == 15)),
                    reads=[wtb, hTb], writes=[S.psbuf[bank]])
            S.op('act', lambda e, stg=stg, ps=ps: e.copy(out=stg[:, 3:515], in_=ps), reads=[S.psbuf[bank]],
                 writes=[stgb])
            S.op('dve', lambda e, acc=acc, stg=stg, fc=fc: e.tensor_scalar(
                out=acc, in0=stg[:, 3:515], scalar1=cw[:, fc, 3:4], scalar2=cb[:, fc, 0:1], op0=ALU.mult,
                op1=ALU.add), reads=[stgb, cwb, cbb], writes=[accb])
            for j in range(3):
                S.op('dve', lambda e, acc=acc, stg=stg, fc=fc, j=j: e.scalar_tensor_tensor(
                    out=acc, in0=stg[:, j:j + 512], scalar=cw[:, fc, j:j + 1], in1=acc, op0=ALU.mult, op1=ALU.add),
                    reads=[stgb, cwb, accb], writes=[accb])
            if tb < 7:
                nstg, nstgb = stage_[(tb + 1) % 2]
                S.op('act', lambda e, nstg=nstg, stg=stg: e.copy(out=nstg[:, 0:3], in_=stg[:, 512:515]),
                     reads=[stgb], writes=[nstgb])
            S.op('act', lambda e, sg=sg, acc=acc: e.activation(out=sg, in_=acc, func=AF.Sigmoid), reads=[accb],
                 writes=[sgb])
            S.op('dve', lambda e, ob=ob, acc=acc, sg=sg, scale=scale: e.scalar_tensor_tensor(
                out=ob, in0=acc, scalar=scale, in1=sg, op0=ALU.mult, op1=ALU.mult), reads=[accb, sgb], writes=[obb])
            S.dma('sp', qkT[fc * 128:(fc + 1) * 128, tb * 512:(tb + 1) * 512], ob, reads=[obb])
    wif, wifb = S.alloc([128, 16, 2 * NH], BF16, 'wif')
    S.dma('pool', wif, G['w_if'].rearrange("(c p) n -> p c n", p=128), writes=[wifb])
    rows_ = [S.alloc([NH, 2, 512], F32, 'rows%d' % i) for i in range(2)]
    for tb in range(8):
        rw, rwb = rows_[tb % 2]
        for k2 in range(2):
            bank = 2 + k2
            ps = S.ps(bank)
            for kc in range(16):
                S.op('pe', lambda e, ps=ps, kc=kc, tb=tb, k2=k2: e.matmul(
                    ps[0:NH, :], lhsT=wif[:, kc, k2 * NH:(k2 + 1) * NH], rhs=hT[:, kc, tb * 512:(tb + 1) * 512],
                    start=(kc == 0), stop=(kc == 15)), reads=[wifb, hTb], writes=[S.psbuf[bank]])
            S.op('act', lambda e, rw=rw, ps=ps, k2=k2: e.copy(out=rw[:, k2, :], in_=ps[0:NH, :]),
                 reads=[S.psbuf[bank]], writes=[rwb])
        S.dma('sp', G['gi'][:, tb * 512:(tb + 1) * 512], rw[:, 0, :], reads=[rwb])
        S.dma('sp', G['gf'][:, tb * 512:(tb + 1) * 512], rw[:, 1, :], reads=[rwb])
    S.barrier()
    S.release()
    if getattr(cfg, 'sub', '') == 'A1':
        S.release()
        return

    S.mark()
    w_tm = G['w_tm']
    wblk_ = [S.alloc([128, 16, 512], BF16, 'wblk%d' % i) for i in range(2)]
    sgt_ = [S.alloc([128, 512], F32, 'sgt%d' % i) for i in range(2)]
    t1_ = [S.alloc([128, 256], F32, 't1%d' % i) for i in range(2)]
    o_ = [S.alloc([128, 512], BF16, 'o%d' % i) for i in range(3)]
    ng_ = [S.alloc([128, 24], F32, 'ng%d' % i) for i in range(2)]
    rt_ = [S.alloc([128, 4, 8, 8], F32, 'rt%d' % i) for i in range(2)]
    stgT_ = [S.alloc([128, 4, 512], BF16, 'stgT%d' % i) for i in range(2)]
    it = 0
    for bi, (kind, idx, c0, W) in enumerate(cfg.blocks):
        if getattr(cfg, 'kinds', None) and kind not in cfg.kinds:
            continue
        wt, wtb = wblk_[bi % 2]
        S.dma('pool', wt[:, :, 0:W], w_tm[:, c0:c0 + W].rearrange("(c p) n -> p c n", p=128), writes=[wtb])
        for tt in range(NT):
            bank = it % 4
            ps = S.ps(bank)
            psb = S.psbuf[bank]
            o, ob = o_[it % 3]
            sgt, sgtb = sgt_[it % 2]
            t1, t1b = t1_[it % 2]
            it += 1
            rows = slice(tt * 128, (tt + 1) * 128)
            for kc in range(16):
                S.op('pe', lambda e, ps=ps, wt=wt, kc=kc, tt=tt, W=W: e.matmul(
                    ps[:, 0:W], lhsT=hT[:, kc, tt * 128:(tt + 1) * 128], rhs=wt[:, kc, 0:W], start=(kc == 0),
                    stop=(kc == 15)), reads=[wtb, hTb], writes=[psb])
            if kind == 'v_m':
                S.op('act', lambda e, o=o, ps=ps: e.copy(out=o, in_=ps), reads=[psb], writes=[ob])
                S.dma('sp', G['vm'][rows, idx * 512:(idx + 1) * 512], o, reads=[ob])
            elif kind == 'gate_m':
                S.op('act', lambda e, sgt=sgt, ps=ps: e.activation(out=sgt, in_=ps, func=AF.Sigmoid), reads=[psb],
                     writes=[sgtb])
                S.op('dve', lambda e, t1=t1, ps=ps, sgt=sgt: e.tensor_tensor(out=t1, in0=ps[:, 256:512],
                                                                             in1=sgt[:, 256:512], op=ALU.mult),
                     reads=[psb, sgtb], writes=[t1b])
                S.op('dve', lambda e, o=o, t1=t1, sgt=sgt: e.tensor_tensor(out=o[:, 0:256], in0=t1,
                                                                            in1=sgt[:, 0:256], op=ALU.mult),
                     reads=[t1b, sgtb], writes=[ob])
                S.dma('sp', G['gm'][rows, idx * 256:(idx + 1) * 256], o[:, 0:256], reads=[ob])
            elif kind in ('nq', 'nk'):
                nh = 8 if kind == 'nq' else 6
                ct, ctb, sn, snb = (cosk, ckb, sink, skb)
                sc = 0.125 if kind == 'nq' else 1.0
                rt, rtb = rt_[tt % 2]
                S.op('act', lambda e, sgt=sgt, ps=ps, sc=sc: e.activation(out=sgt, in_=ps, func=AF.Copy, scale=sc),
                     reads=[psb], writes=[sgtb])
                S.op('act', lambda e, o=o, sgt=sgt: e.copy(out=o, in_=sgt), reads=[sgtb], writes=[ob])
                ps3 = sgt.rearrange("p (h d) -> p h d", d=64)
                o3 = o.rearrange("p (h d) -> p h d", d=64)
                x1 = ps3[:, 0:nh, 0:8]
                x2 = ps3[:, 0:nh, 8:16]
                cbv = ct[:, tt:tt + 1, :].to_broadcast([128, nh, 8])
                sbv = sn[:, tt:tt + 1, :].to_broadcast([128, nh, 8])
                for k4, (a, bv) in enumerate([(x1, cbv), (x2, sbv), (x2, cbv), (x1, sbv)]):
                    S.op('dve', lambda e, rt=rt, k4=k4, a=a, bv=bv, nh=nh: e.tensor_tensor(
                        out=rt[:, k4, 0:nh, :], in0=a, in1=bv, op=ALU.mult), reads=[sgtb, ctb, snb], writes=[rtb])
                S.op('dve', lambda e, o3=o3, rt=rt, nh=nh: e.tensor_tensor(
                    out=o3[:, 0:nh, 0:8], in0=rt[:, 0, 0:nh, :], in1=rt[:, 1, 0:nh, :], op=ALU.subtract),
                    reads=[rtb], writes=[ob])
                S.op('dve', lambda e, o3=o3, rt=rt, nh=nh: e.tensor_tensor(
                    out=o3[:, 0:nh, 8:16], in0=rt[:, 2, 0:nh, :], in1=rt[:, 3, 0:nh, :], op=ALU.add),
                    reads=[rtb], writes=[ob])
                tbank = 4 + (tt % 2)
                pv = transposes_to(tbank, [o[:, j * 128:(j + 1) * 128] for j in range(4)], [ob])
                stg, stgb = stgT_[(tt // 4) % 2]
                q4 = tt % 4
                src = pv[:, 0:512].rearrange("p (j t) -> p j t", j=4)
                dst = stg[:, :, q4 * 128:(q4 + 1) * 128]
                if tt % 2 == 0:
                    S.op('act', lambda e, src=src, dst=dst: e.copy(out=dst, in_=src), reads=[S.psbuf[tbank]],
                         writes=[stgb])
                else:
                    S.op('dve', lambda e, src=src, dst=dst: e.tensor_copy(out=dst, in_=src), reads=[S.psbuf[tbank]],
                         writes=[stgb])
                if q4 == 3:
                    tb = tt // 4
                    if kind == 'nq':
                        dd = G['qTn'][idx].rearrange("(j p) t -> p j t", p=128)[:, :, tb * 512:(tb + 1) * 512]
                    else:
                        dd = G['kT4'][idx].rearrange("j p t -> p j t")[:, :, tb * 512:(tb + 1) * 512]
                    S.dma('sp', dd, stg, reads=[stgb])
            elif kind == 'nv':
                ngt, ngtb = ng_[tt % 2]
                S.op('act', lambda e, o=o, ps=ps: e.copy(out=o[:, 0:256], in_=ps[:, 0:256]), reads=[psb], writes=[ob])
                S.op('act', lambda e, ngt=ngt, ps=ps: e.activation(out=ngt, in_=ps[:, 256:280], func=AF.Sigmoid),
                     reads=[psb], writes=[ngtb])
                S.dma('sp', G['vsw'][idx][rows, :], o[:, 0:256], reads=[ob])
                S.dma('sp', G['ngs'][idx][rows, :], ngt, reads=[ngtb])
            elif kind == 'nz':
                S.op('act', lambda e, sgt=sgt, ps=ps: e.activation(out=sgt, in_=ps, func=AF.Sigmoid), reads=[psb],
                     writes=[sgtb])
                S.op('dve', lambda e, o=o, ps=ps, sgt=sgt: e.tensor_tensor(out=o, in0=ps, in1=sgt, op=ALU.mult),
                     reads=[psb, sgtb], writes=[ob])
                S.dma('sp', G['szn'][idx][rows, :], o, reads=[ob])
    S.barrier()
    S.release()
    S.release()


def rstd_inplace(S, ss, ssb, n, eps=EPS):
    S.op('dve', lambda e: e.tensor_scalar(out=ss, in0=ss, scalar1=1.0 / n, scalar2=eps, op0=ALU.mult, op1=ALU.add),
         reads=[ssb], writes=[ssb])
    S.op('act', lambda e: e.activation(out=ss, in_=ss, func=AF.Sqrt), reads=[ssb], writes=[ssb])
    S.op('dve', lambda e: e.reciprocal(out=ss, in_=ss), reads=[ssb], writes=[ssb])


def phase_B(S, cfg, G):
    NH = cfg.NH
    identf, identfb = G['identf'], G['identfb']
    transposes_to = G['transposes_to']
    qkT, vm, gm, mixT = G['qkT'], G['vm'], G['gm'], G['mixT']
    S.mark()
    cols, colsb = S.alloc([128, NT, 3, NH], F32, 'cols')
    decb, decbb = S.alloc([128, NH * NT], F32, 'decb')
    cm01, cm01b = S.alloc([128, 128], F32, 'cm01')
    S.op('pool', lambda e: e.memset(cm01, 1.0), writes=[cm01b])
    S.op('pool', lambda e: e.affine_select(out=cm01, in_=cm01, pattern=[[1, 128]], compare_op=ALU.is_ge, fill=0.0,
                                            base=0, channel_multiplier=-1), reads=[cm01b], writes=[cm01b])
    S.mark()
    T = [S.alloc([NH, SEQ], F32, 'T%d' % i) for i in range(8)]
    (T1, T1b), (T2, T2b), (T3, T3b), (T4, T4b), (T5, T5b), (T6, T6b), (T7, T7b), (T8, T8b) = T
    bi, bib = S.alloc([NH, 1], F32, 'bi')
    bf_, bfb = S.alloc([NH, 1], F32, 'bf')
    bs, bsb = S.alloc([NH, NT], F32, 'bs')
    be, beb = S.alloc([NH, NT], F32, 'be')
    dec, decrb = S.alloc([NH, NT], F32, 'dec')
    sel, selb = S.alloc([NH, NH, 128], F32, 'sel')
    S.dma('sp', T1, G['gi'], writes=[T1b])
    S.dma('sp', T2, G['gf'], writes=[T2b])
    S.dma('sp', bi, G['b_i'], writes=[bib])
    S.dma('sp', bf_, G['b_f'], writes=[bfb])
    S.op('dve', lambda e: e.tensor_scalar(out=bf_, in0=bf_, scalar1=-1.0, scalar2=None, op0=ALU.mult), reads=[bfb],
         writes=[bfb])
    S.op('dve', lambda e: e.tensor_scalar(out=T1, in0=T1, scalar1=bi, scalar2=None, op0=ALU.add), reads=[T1b, bib],
         writes=[T1b])
    S.op('act', lambda e: e.activation(out=T2, in_=T2, func=AF.Exp, scale=-1.0, bias=bf_), reads=[T2b, bfb],
         writes=[T2b])
    S.op('act', lambda e: e.activation(out=T2, in_=T2, func=AF.Ln, bias=1.0), reads=[T2b], writes=[T2b])
    S.op('pool', lambda e: e.memset(T8, 1.0), writes=[T8b])
    S.op('dve', lambda e: e.tensor_tensor_scan(out=T3, data0=T8, data1=T2, initial=0.0, op0=ALU.mult, op1=ALU.add),
         reads=[T8b, T2b], writes=[T3b])
    S.op('dve', lambda e: e.tensor_tensor(out=T1, in0=T1, in1=T3, op=ALU.add), reads=[T1b, T3b], writes=[T1b])
    S.op('dve', lambda e: e.tensor_tensor_scan(out=T4, data0=T1, data1=T1, initial=0.0, op0=ALU.max, op1=ALU.max),
         reads=[T1b], writes=[T4b])
    T3v = T3.rearrange("p (c l) -> p c l", l=128)
    T1v = T1.rearrange("p (c l) -> p c l", l=128)
    T4v = T4.rearrange("p (c l) -> p c l", l=128)
    S.op('dve', lambda e: e.memset(bs, 0.0), writes=[bsb])
    S.op('dve', lambda e: e.tensor_copy(out=bs[:, 1:NT], in_=T3v[:, 0:NT - 1, 127]), reads=[T3b, bsb], writes=[bsb])
    S.op('dve', lambda e: e.tensor_copy(out=be, in_=T3v[:, :, 127]), reads=[T3b], writes=[beb])
    bsB = bs.unsqueeze(2).to_broadcast([NH, NT, 128])
    beB = be.unsqueeze(2).to_broadcast([NH, NT, 128])
    for (dst, dstb, src, srcb, bb, bbb) in [(T5, T5b, T1v, T1b, bsB, bsb), (T6, T6b, T1v, T1b, beB, beb),
                                            (T7, T7b, T4v, T4b, bsB, bsb)]:
        dv = dst.rearrange("p (c l) -> p c l", l=128)
        S.op('dve', lambda e, dv=dv, src=src, bb=bb: e.tensor_tensor(out=dv, in0=src, in1=bb, op=ALU.subtract),
             reads=[srcb, bbb], writes=[dstb])
        S.op('act', lambda e, dst=dst: e.activation(out=dst, in_=dst, func=AF.Exp), reads=[dstb], writes=[dstb])
    S.op('dve', lambda e: e.tensor_tensor(out=dec, in0=bs, in1=be, op=ALU.subtract), reads=[bsb, beb], writes=[decrb])
    S.op('act', lambda e: e.activation(out=dec, in_=dec, func=AF.Exp), reads=[decrb], writes=[decrb])
    for c in range(NT):
        bank = 6 + (c % 2)
        pv = transposes_to(bank, [Tq[:, c * 128:(c + 1) * 128] for Tq in (T5, T6, T7)], [T5b, T6b, T7b], dtype=F32,
                           idn=identf, idb=identfb, width=NH)
        S.op('act' if c % 2 else 'dve',
             (lambda e, c=c, pv=pv: e.copy(out=cols[:, c, :, :], in_=pv[:, 0:3 * NH].rearrange("p (a h) -> p a h", a=3)))
             if c % 2 else
             (lambda e, c=c, pv=pv: e.tensor_copy(out=cols[:, c, :, :],
                                                  in_=pv[:, 0:3 * NH].rearrange("p (a h) -> p a h", a=3))),
             reads=[S.psbuf[bank]], writes=[colsb])
    S.op('pool', lambda e: e.memset(sel, 1.0), writes=[selb])
    S.op('pool', lambda e: e.affine_select(out=sel, in_=sel, pattern=[[-1, NH], [0, 128]], compare_op=ALU.is_equal,
                                            fill=0.0, base=0, channel_multiplier=1), reads=[selb], writes=[selb])
    pdec = S.ps(5)
    for h in range(NH):
        S.op('pe', lambda e, h=h: e.matmul(pdec[:, h * NT:(h + 1) * NT], lhsT=sel[:, h, :], rhs=dec, start=True,
                                           stop=True), reads=[selb, decrb], writes=[S.psbuf[5]])
    S.op('dve', lambda e: e.tensor_copy(out=decb, in_=pdec[:, 0:NH * NT]), reads=[S.psbuf[5]], writes=[decbb])
    S.barrier()
    S.release()

    S.mark()
    Sst, Sstb = S.alloc([128, 2, 257], F32, 'Sst')
    Sbf, Sbfb = S.alloc([128, 2, 257], BF16, 'Sbf')
    hd_ = []
    for i in range(2):
        hd_.append(dict(q=S.alloc([128, 2, SEQ], BF16, 'qTh%d' % i), k=S.alloc([128, 2, SEQ], BF16, 'kTh%d' % i),
                        v=S.alloc([128, NT, 257], BF16, 'vh%d' % i), g=S.alloc([128, NT, 256], BF16, 'gmh%d' % i),
                        n=S.alloc([128, 256], F32, 'mng%d' % i)))
    kt_ = [S.alloc([128, 256], BF16, 'kt%d' % i) for i in range(2)]
    wv_ = [S.alloc([128, 257], BF16, 'wv%d' % i) for i in range(2)]
    PT_ = [S.alloc([128, 128], BF16, 'PT%d' % i) for i in range(2)]
    dd_ = [S.alloc([128, 4], F32, 'dd%d' % i) for i in range(2)]
    hm_ = [S.alloc([128, 256], F32, 'hm%d' % i) for i in range(2)]
    st6_ = [S.alloc([128, 8], F32, 'st6%d' % i) for i in range(2)]
    y_ = [S.alloc([128, 256], F32, 'y%d' % i) for i in range(2)]
    mo_ = [S.alloc([128, 256], BF16, 'mo%d' % i) for i in range(2)]
    stgM_ = [S.alloc([128, 2, 512], BF16, 'stgM%d' % i) for i in range(2)]
    for hi in range(NH):
        H = hd_[hi % 2]
        (qT, qTb), (kT, kTb), (vh, vhb), (gh, ghb), (mn, mnb) = H['q'], H['k'], H['v'], H['g'], H['n']
        S.dma('sp', qT, qkT[hi * 256:(hi + 1) * 256, :].rearrange("(j p) t -> p j t", p=128), writes=[qTb])
        S.dma('sp', kT, qkT[(NH + hi) * 256:(NH + hi + 1) * 256, :].rearrange("(j p) t -> p j t", p=128), writes=[kTb])
        for c4 in range(4):
            S.dma('sp', vh[:, c4 * 8:(c4 + 1) * 8, 0:256],
                  vm[c4 * 1024:(c4 + 1) * 1024, hi * 256:(hi + 1) * 256].rearrange("(c p) d -> p c d", p=128),
                  writes=[vhb])
            S.dma('sp', gh[:, c4 * 8:(c4 + 1) * 8, :],
                  gm[c4 * 1024:(c4 + 1) * 1024, hi * 256:(hi + 1) * 256].rearrange("(c p) d -> p c d", p=128),
                  writes=[ghb])
        S.dma('sp', mn, G['mng'][hi * 256:(hi + 1) * 256].partition_broadcast(128), writes=[mnb])
        S.op('dve', lambda e, vh=vh: e.memset(vh[:, :, 256:257], 1.0), writes=[vhb])
        S.op('dve', lambda e: e.memset(Sst, 0.0), writes=[Sstb])
        S.op('dve', lambda e: e.memset(Sbf, 0.0), writes=[Sbfb])
        for c in range(NT):
            cs = slice(c * 128, (c + 1) * 128)
            kt, ktb = kt_[c % 2]
            wv, wvb = wv_[c % 2]
            PT, PTb = PT_[c % 2]
            dd, ddb = dd_[c % 2]
            hm, hmb = hm_[c % 2]
            st6, st6b = st6_[c % 2]
            y, yb = y_[c % 2]
            mo, mob = mo_[c % 2]
            stg, stgb = stgM_[(c // 4) % 2]
            pv = transposes_to(6, [kT[:, j, cs] for j in range(2)], [kTb])
            S.op('act', lambda e, kt=kt, pv=pv: e.copy(out=kt, in_=pv[:, 0:256]), reads=[S.psbuf[6]], writes=[ktb])
            S.op('dve', lambda e, wv=wv, vh=vh, c=c, hi=hi: e.tensor_scalar(
                out=wv, in0=vh[:, c, :], scalar1=cols[:, c, 1, hi:hi + 1], scalar2=None, op0=ALU.mult),
                reads=[vhb, colsb], writes=[wvb])
            b_s = c % 2
            psS = S.ps(b_s)
            for j in range(2):
                S.op('pe', lambda e, psS=psS, kT=kT, qT=qT, j=j, cs=cs: e.matmul(
                    psS[:, 0:128], lhsT=kT[:, j, cs], rhs=qT[:, j, cs], start=(j == 0), stop=(j == 1)),
                    reads=[kTb, qTb], writes=[S.psbuf[b_s]])
            S.op('dve', lambda e, PT=PT, psS=psS, c=c, hi=hi: e.scalar_tensor_tensor(
                out=PT, in0=psS[:, 0:128], scalar=cols[:, c, 0, hi:hi + 1], in1=cm01, op0=ALU.mult, op1=ALU.mult),
                reads=[S.psbuf[b_s], colsb, cm01b], writes=[PTb])
            b_a = 2 + (c % 2)
            pa = S.ps(b_a)
            S.op('pe', lambda e, pa=pa, PT=PT, vh=vh, c=c: e.matmul(pa[:, 0:257], lhsT=PT, rhs=vh[:, c, :], start=True,
                                                                   stop=False),
                 reads=[PTb, vhb], writes=[S.psbuf[b_a]])
            for j in range(2):
                S.op('pe', lambda e, pa=pa, qT=qT, j=j, cs=cs: e.matmul(pa[:, 0:257], lhsT=qT[:, j, cs], rhs=Sbf[:, j, :],
                                                                       start=False, stop=(j == 1)),
                     reads=[qTb, Sbfb], writes=[S.psbuf[b_a]])
            S.op('act', lambda e, dd=dd, pa=pa: e.activation(out=dd[:, 3:4], in_=pa[:, 256:257], func=AF.Abs),
                 reads=[S.psbuf[b_a]], writes=[ddb])
            S.op('dve', lambda e, dd=dd, c=c, hi=hi: e.tensor_scalar(
                out=dd[:, 0:1], in0=dd[:, 3:4], scalar1=cols[:, c, 2, hi:hi + 1], scalar2=None, op0=ALU.max),
                reads=[ddb, colsb], writes=[ddb])
            S.op('dve', lambda e, dd=dd: e.reciprocal(out=dd[:, 0:1], in_=dd[:, 0:1]), reads=[ddb], writes=[ddb])
            S.op('act', lambda e, hm=hm, pa=pa, dd=dd: e.activation(out=hm, in_=pa[:, 0:256], func=AF.Copy,
                                                                   scale=dd[:, 0:1]),
                 reads=[S.psbuf[b_a], ddb], writes=[hmb])
            S.op('dve', lambda e, st6=st6, hm=hm: e.bn_stats(out=st6[:, 0:6], in_=hm), reads=[hmb], writes=[st6b])
            S.op('dve', lambda e, st6=st6: e.bn_aggr(out=st6[:, 6:8], in_=st6[:, 0:6]), reads=[st6b], writes=[st6b])
            S.op('dve', lambda e, dd=dd, st6=st6: e.tensor_scalar(out=dd[:, 1:2], in0=st6[:, 7:8], scalar1=EPS,
                                                                 scalar2=None, op0=ALU.add),
                 reads=[st6b], writes=[ddb])
            S.op('act', lambda e, dd=dd: e.activation(out=dd[:, 1:2], in_=dd[:, 1:2], func=AF.Sqrt), reads=[ddb],
                 writes=[ddb])
            S.op('dve', lambda e, dd=dd: e.reciprocal(out=dd[:, 1:2], in_=dd[:, 1:2]), reads=[ddb], writes=[ddb])
            S.op('dve', lambda e, dd=dd, st6=st6: e.scalar_tensor_tensor(
                out=dd[:, 2:3], in0=st6[:, 6:7], scalar=-1.0, in1=dd[:, 1:2], op0=ALU.mult, op1=ALU.mult),
                reads=[st6b, ddb], writes=[ddb])
            S.op('act', lambda e, y=y, hm=hm, dd=dd: e.activation(out=y, in_=hm, func=AF.Identity, scale=dd[:, 1:2],
                                                                 bias=dd[:, 2:3]),
                 reads=[hmb, ddb], writes=[yb])
            S.op('dve', lambda e, y=y, mn=mn: e.tensor_tensor(out=y, in0=y, in1=mn, op=ALU.mult), reads=[yb, mnb],
                 writes=[yb])
            S.op('dve', lambda e, mo=mo, y=y, gh=gh, c=c: e.tensor_tensor(out=mo, in0=y, in1=gh[:, c, :], op=ALU.mult),
                 reads=[yb, ghb], writes=[mob])
            pv2 = transposes_to(7, [mo[:, j * 128:(j + 1) * 128] for j in range(2)], [mob])
            q4 = c % 4
            S.op('act', lambda e, stg=stg, pv2=pv2, q4=q4: e.copy(
                out=stg[:, :, q4 * 128:(q4 + 1) * 128], in_=pv2[:, 0:256].rearrange("p (j t) -> p j t", j=2)),
                reads=[S.psbuf[7]], writes=[stgb])
            if q4 == 3:
                c4 = c // 4
                S.dma('sp', mixT[hi * 256:(hi + 1) * 256, :].rearrange("(j p) t -> p j t", p=128)[
                    :, :, c4 * 512:(c4 + 1) * 512], stg, reads=[stgb])
            for j in range(2):
                pd = S.ps(4 + j)
                S.op('pe', lambda e, pd=pd, kt=kt, wv=wv, j=j: e.matmul(pd[:, 0:257], lhsT=kt[:, j * 128:(j + 1) * 128],
                                                                       rhs=wv, start=True, stop=True),
                     reads=[ktb, wvb], writes=[S.psbuf[4 + j]])
                S.op('dve', lambda e, pd=pd, j=j, c=c, hi=hi: e.scalar_tensor_tensor(
                    out=Sst[:, j, :], in0=Sst[:, j, :], scalar=decb[:, hi * NT + c:hi * NT + c + 1], in1=pd[:, 0:257],
                    op0=ALU.mult, op1=ALU.add), reads=[Sstb, decbb, S.psbuf[4 + j]], writes=[Sstb])
            S.op('act', lambda e: e.copy(out=Sbf, in_=Sst), reads=[Sstb], writes=[Sbfb])
    S.barrier()
    S.release()
    S.release()


def phase_C(S, cfg, G):
    NH, NP = cfg.NH, cfg.NP
    ident, identb = G['ident'], G['identb']
    transposes_to = G['transposes_to']
    mixT = G['mixT']
    S.mark()
    caus, causb = S.alloc([128, 128], F32, 'caus')
    onesf, onesfb = S.alloc([128, 128], F32, 'onesf')
    caus4, caus4b = S.alloc([128, 4, 128], BF16, 'caus4')
    anti4, anti4b = S.alloc([128, 4, 128], BF16, 'anti4')
    E, Eb = S.alloc([128, SEQ], BF16, 'E')
    S.op('pool', lambda e: e.memset(onesf, 1.0), writes=[onesfb])
    S.op('pool', lambda e: e.affine_select(out=caus, in_=onesf, pattern=[[1, 128]], compare_op=ALU.is_ge, fill=0.0,
                                            base=0, channel_multiplier=-1), reads=[onesfb], writes=[causb])
    S.op('dve', lambda e: e.tensor_copy(out=caus4, in_=caus.unsqueeze(1).to_broadcast([128, 4, 128])), reads=[causb],
         writes=[caus4b])
    S.op('dve', lambda e: e.tensor_scalar(out=anti4, in0=caus4, scalar1=-1.0, scalar2=1.0, op0=ALU.mult, op1=ALU.add),
         reads=[caus4b], writes=[anti4b])
    S.mark()
    Ef, Efb = S.alloc([128, SEQ], F32, 'Ef')
    Eg, Egb = S.alloc([128, SEQ], F32, 'Eg')
    for (Et, Etb, sh) in ((Ef, Efb, 0), (Eg, Egb, SEQ)):
        S.op('pool', lambda e, Et=Et: e.memset(Et, 1.0), writes=[Etb])
        S.op('pool', lambda e, Et=Et, sh=sh: e.affine_select(out=Et, in_=Et, pattern=[[1, SEQ]], compare_op=ALU.is_ge,
                                                             fill=0.0, base=sh, channel_multiplier=-64),
             reads=[Etb], writes=[Etb])
        S.op('pool', lambda e, Et=Et, sh=sh: e.affine_select(out=Et, in_=Et, pattern=[[-1, SEQ]], compare_op=ALU.is_ge,
                                                             fill=0.0, base=63 - sh, channel_multiplier=64),
             reads=[Etb], writes=[Etb])
    S.op('dve', lambda e: e.tensor_tensor(out=E, in0=Ef, in1=Eg, op=ALU.add), reads=[Efb, Egb], writes=[Eb])
    S.barrier()
    S.release()
    selb_t, selb_tb = S.alloc([128, NT, 64], F32, 'selb')
    S.dma('sp', selb_t, G['selbias'], writes=[selb_tb])
    ovt, ovtb = S.alloc([128, 2, 65], F32, 'ovt')
    S.dma('sp', ovt, G['ovl'].rearrange("(c p) n -> p c n", p=128), writes=[ovtb])
    w1 = {}
    w2 = {}
    pos = {}
    for nm, a1, a2, ap_ in (('k', G['ckw1'], G['ckw2'], G['posk']), ('v', G['cvw1'], G['cvw2'], G['posv'])):
        w1[nm] = S.alloc([128, 32, 128], BF16, 'w1' + nm)
        w2[nm] = S.alloc([128, 128], BF16, 'w2' + nm)
        pos[nm] = S.alloc([128, 32], BF16, 'pos' + nm)
        for hf in range(2):
            S.dma('pool', w1[nm][0][hf * 64:(hf + 1) * 64], a1, writes=[w1[nm][1]])
            S.dma('pool', pos[nm][0][hf * 64:(hf + 1) * 64], ap_, writes=[pos[nm][1]])
            S.dma('pool', w2[nm][0][:, hf * 64:(hf + 1) * 64], a2, writes=[w2[nm][1]])
    qTp, qTpb = S.alloc([128, 4, SEQ], BF16, 'qTp')
    k4, k4b = S.alloc([128, 4, SEQ], BF16, 'k4')
    vsw_t, vsw_tb = S.alloc([128, NT, 4, 65], BF16, 'vsw_t')
    ng_t, ng_tb = S.alloc([128, NT, 24], F32, 'ng_t')
    Rg, _ = S.alloc([128, 16384], BF16, 'Rg')
    kg1 = (Rg[:, 0:8192].rearrange("p (l c) -> p l c", l=32), Buf('kg'))
    kg = {'k': kg1, 'v': kg1}
    Pall = Rg.rearrange("p (k j t) -> p k j t", k=32, j=4)
    Pallb = [Buf('Pall%d' % i) for i in range(32)]
    Pw, _ = S.alloc([128, 5, 4, 128], BF16, 'Pw')
    Pwb = [Buf('Pw%d' % i) for i in range(5)]
    kcT, kcTb = S.alloc([128, 256], BF16, 'kcT')
    vca, vcab = S.alloc([128, 2, 2, 129], BF16, 'vca')
    biasc, biascb = S.alloc([128, 1], F32, 'biasc')
    xg, xgb = S.alloc([128, 256], F32, 'xg')
    x2, x2b = S.alloc([128, 256], F32, 'x2')
    gl, glb = S.alloc([128, 256], BF16, 'gl')
    sz_ = [S.alloc([128, 512], BF16, 'sz%d' % i) for i in range(2)]
    P_ = [S.alloc([128, 4, 128], BF16, 'P%d' % i) for i in range(4)]
    mk_ = [S.alloc([128, 128], F32, 'mk%d' % i) for i in range(2)]
    mkh_ = [S.alloc([128, 128], BF16, 'mkh%d' % i) for i in range(2)]
    rs_ = [S.alloc([128, 16], F32, 'rs%d' % i) for i in range(2)]
    imp_ = [S.alloc([128, 64], F32, 'imp%d' % i) for i in range(2)]
    sc2_ = [S.alloc([128, 64], F32, 'sc2%d' % i) for i in range(2)]
    m8_ = [S.alloc([128, 16], F32, 'm8%d' % i) for i in range(2)]
    nm_ = [S.alloc([128, 128], BF16, 'nm%d' % i) for i in range(2)]
    nmT1_ = [S.alloc([128, 128], BF16, 'nmT1%d' % i) for i in range(2)]
    nmT4_ = [S.alloc([128, 4, 128], BF16, 'qn%d' % i) for i in range(2)]
    ksE_ = [S.alloc([128, SEQ], BF16, 'ksE%d' % i) for i in range(2)]
    nacc_ = [S.alloc([128, 4, 64], F32, 'nacc%d' % i) for i in range(2)]
    no_ = [S.alloc([128, 256], BF16, 'no%d' % i) for i in range(2)]
    stgN_ = [S.alloc([128, 4, 512], BF16, 'stgN%d' % i) for i in range(2)]
    pit = 0
    for pi in range(NP):
        S.barrier()
        S.dma('sp', qTp, G['qTn'][pi].rearrange("(j p) t -> p j t", p=128), writes=[qTpb])
        S.dma('sp', k4, G['kT4'][pi].rearrange("j p t -> p j t"), writes=[k4b])
        for c4 in range(4):
            for a in range(4):
                S.dma('sp', vsw_t[:, c4 * 8:(c4 + 1) * 8, a, 0:64],
                      G['vsw'][pi][c4 * 1024:(c4 + 1) * 1024, a * 64:(a + 1) * 64].rearrange("(c p) d -> p c d", p=128),
                      writes=[vsw_tb])
        S.op('dve', lambda e: e.memset(vsw_t[:, :, :, 64:65], 1.0), writes=[vsw_tb])
        S.dma('sp', ng_t, G['ngs'][pi].rearrange("(c p) n -> p c n", p=128), writes=[ng_tb])
        for g2 in range(2):
            kst, kstb = ksE_[g2]
            ksl = slice(64 * g2, 64 * g2 + 64)
            esl = slice(64 * (1 - g2), 64 * (1 - g2) + 64)
            S.op('act', lambda e, kst=kst, ksl=ksl: e.copy(out=kst[ksl, :], in_=k4[ksl, 1, :]), reads=[k4b],
                 writes=[kstb])
            S.op('dve', lambda e, kst=kst, esl=esl: e.tensor_copy(out=kst[esl, :], in_=E[esl, :]), reads=[Eb],
                 writes=[kstb])

        for nm, srcidx in (('k', 0), ('v', 3)):
            k4v = k4[:, srcidx, :].rearrange("p (c s) -> p c s", s=16)
            for l in range(32):
                srcv = k4v[:, 0:255, l] if l < 16 else k4v[:, 1:256, l - 16]
                S.op('dve' if l % 2 else 'act',
                     (lambda e, l=l, srcv=srcv, nm=nm: e.tensor_copy(out=kg[nm][0][:, l, 0:255], in_=srcv)) if l % 2 else
                     (lambda e, l=l, srcv=srcv, nm=nm: e.copy(out=kg[nm][0][:, l, 0:255], in_=srcv)),
                     reads=[k4b], writes=[kg[nm][1]])
            for g2 in range(2):
                pb = 64 * g2
                w1t, w1b = w1[nm]
                w2t, w2b = w2[nm]
                pst, psb_ = pos[nm]
                pbias = S.ps(0)
                for l in range(32):
                    S.op('pe', lambda e, l=l, w1t=w1t, pst=pst, pb=pb: e.matmul(
                        pbias[:, 0:1], lhsT=w1t[pb:pb + 64, l, :], rhs=pst[pb:pb + 64, l:l + 1], start=(l == 0),
                        stop=(l == 31)), reads=[w1b, psb_], writes=[S.psbuf[0]])
                S.op('act', lambda e: e.copy(out=biasc, in_=pbias[:, 0:1]), reads=[S.psbuf[0]], writes=[biascb])
                phid = S.ps(1)
                for l in range(32):
                    S.op('pe', lambda e, l=l, w1t=w1t, pb=pb, srcidx=srcidx, nm=nm: e.matmul(
                        phid[:, 0:255], lhsT=w1t[pb:pb + 64, l, :], rhs=kg[nm][0][pb:pb + 64, l, 0:255],
                        start=(l == 0), stop=(l == 31)), reads=[w1b, kg[nm][1]], writes=[S.psbuf[1]])
                S.op('dve', lambda e: e.memset(xg, 0.0), writes=[xgb])
                S.op('act', lambda e: e.activation(out=xg[:, 0:255], in_=phid[:, 0:255], func=AF.Identity, bias=biasc),
                     reads=[S.psbuf[1], biascb, xgb], writes=[xgb])
                S.op('dve', lambda e: e.tensor_tensor(out=x2, in0=xg, in1=xg, op=ALU.mult), reads=[xgb], writes=[x2b])
                S.op('dve', lambda e: e.tensor_scalar(out=x2, in0=x2, scalar1=0.044715, scalar2=1.0, op0=ALU.mult,
                                                      op1=ALU.add), reads=[x2b], writes=[x2b])
                S.op('dve', lambda e: e.tensor_tensor(out=x2, in0=x2, in1=xg, op=ALU.mult), reads=[x2b, xgb], writes=[x2b])
                S.op('act', lambda e: e.activation(out=x2, in_=x2, func=AF.Sigmoid, scale=1.5957691216057308),
                     reads=[x2b], writes=[x2b])
                S.op('dve', lambda e: e.tensor_tensor(out=gl, in0=xg, in1=x2, op=ALU.mult), reads=[xgb, x2b], writes=[glb])
                if nm == 'k':
                    pk = S.ps(2)
                    S.op('pe', lambda e, w2t=w2t: e.matmul(pk[:, 0:256], lhsT=w2t, rhs=gl, start=True, stop=True),
                         reads=[w2b, glb], writes=[S.psbuf[2]])
                    S.op('act', lambda e, pb=pb: e.copy(out=kcT[pb:pb + 64, :], in_=pk[pb:pb + 64, 0:256]),
                         reads=[S.psbuf[2]], writes=[kcTb])
                else:
                    for ct in range(2):
                        pvv = S.ps(2)
                        S.op('pe', lambda e, w2t=w2t, ct=ct: e.matmul(pvv[:, 0:64], lhsT=gl[:, ct * 128:(ct + 1) * 128],
                                                                     rhs=w2t[:, 0:64], start=True, stop=True),
                             reads=[w2b, glb], writes=[S.psbuf[2]])
                        S.op('act', lambda e, g2=g2, ct=ct: e.copy(out=vca[:, g2, ct, 0:64], in_=pvv[:, 0:64]),
                             reads=[S.psbuf[2]], writes=[vcab])
                        S.op('dve', lambda e, g2=g2, ct=ct: e.tensor_copy(out=vca[:, g2, ct, 64:129], in_=ovt[:, ct, :]),
                             reads=[ovtb, vcab], writes=[vcab])
        base_row = NH * 256 + pi * 512
        lvl = getattr(cfg, 'clevel', 9)
        S.barrier()
        for qt in getattr(cfg, 'qts', range(NT)):
            szt, sztb = sz_[qt % 2]
            S.dma('sp', szt, G['szn'][pi][qt * 128:(qt + 1) * 128, :], writes=[sztb])
            stg, stgb = stgN_[(qt // 4) % 2]
            for g2 in range(2):
                pb = 64 * g2
                u = (qt * 2 + g2) % 2
                rs, rsb = rs_[u]
                imp, impb = imp_[u]
                sc2, sc2b = sc2_[u]
                m8, m8b = m8_[u]
                nmt, nmtb = nm_[u]
                nmT1, nmT1b = nmT1_[u]
                nmT4, nmT4b = nmT4_[u]
                nacc, naccb = nacc_[u]
                no, nob = no_[u]
                q4 = qTp[pb:pb + 64, :, qt * 128:(qt + 1) * 128]
                ngv = ng_t[:, qt, g2 * 12:(g2 + 1) * 12].rearrange("p (j b) -> p j b", b=3)
                nct = 1 if qt < 16 else 2
                pO = [S.ps(1), S.ps(2)]
                cslot = []
                for ct in range(nct):
                    P, Pb = P_[pit % 4]
                    cslot.append((P, Pb))
                    mk, mkb = mk_[pit % 2]
                    pit += 1
                    psc = S.ps(0)
                    S.op('pe', lambda e, psc=psc, ct=ct, pb=pb, q4=q4: e.matmul(
                        psc, lhsT=kcT[pb:pb + 64, ct * 128:(ct + 1) * 128], rhs=q4, start=True, stop=True),
                        reads=[kcTb, qTpb], writes=[S.psbuf[0]])
                    S.op('act', lambda e, P=P, psc=psc: e.activation(out=P.rearrange("p j t -> p (j t)"), in_=psc,
                                                                    func=AF.Exp), reads=[S.psbuf[0]], writes=[Pb])
                    S.op('pool', lambda e, mk=mk, ct=ct, qt=qt: e.affine_select(
                        out=mk, in_=onesf, pattern=[[1, 128]], compare_op=ALU.is_ge, fill=0.0,
                        base=-(2048 * ct - 128 * qt + 31), channel_multiplier=-16), reads=[onesfb], writes=[mkb])
                    mkh, mkhb = mkh_[pit % 2]
                    S.op('dve', lambda e, mkh=mkh, mk=mk: e.tensor_copy(out=mkh, in_=mk), reads=[mkb], writes=[mkhb])
                    S.op('dve', lambda e, P=P, mkh=mkh: e.tensor_tensor(
                        out=P, in0=P, in1=mkh.unsqueeze(1).to_broadcast([128, 4, 128]), op=ALU.mult), reads=[Pb, mkhb],
                        writes=[Pb])
                if lvl < 2:
                    continue
                for j in range(4):
                    for ct in range(nct):
                        P, Pb = cslot[ct]
                        S.op('pe', lambda e, j=j, P=P, ct=ct, g2=g2, nct=nct: e.matmul(
                            pO[j // 2][:, (j % 2) * 256:(j % 2) * 256 + 129], lhsT=P[:, j, :], rhs=vca[:, g2, ct, :],
                            start=(ct == 0), stop=(ct == nct - 1)), reads=[Pb, vcab], writes=[S.psbuf[1 + j // 2]])
                if lvl < 2:
                    continue
                for j in range(4):
                    S.op('act', lambda e, rs=rs, j=j: e.copy(
                        out=rs[:, j:j + 1], in_=pO[j // 2][:, (j % 2) * 256 + 64:(j % 2) * 256 + 65]),
                        reads=[S.psbuf[1 + j // 2]], writes=[rsb])
                S.op('dve', lambda e, rs=rs: e.tensor_scalar(out=rs[:, 0:4], in0=rs[:, 0:4], scalar1=1e-30, scalar2=None,
                                                             op0=ALU.max), reads=[rsb], writes=[rsb])
                S.op('dve', lambda e, rs=rs: e.reciprocal(out=rs[:, 0:4], in_=rs[:, 0:4]), reads=[rsb], writes=[rsb])
                S.op('act', lambda e, imp=imp, rs=rs: e.activation(out=imp, in_=pO[0][:, 65:129], func=AF.Copy,
                                                                  scale=rs[:, 0:1]),
                     reads=[S.psbuf[1], rsb], writes=[impb])
                for j in range(1, 4):
                    S.op('dve', lambda e, imp=imp, rs=rs, j=j: e.scalar_tensor_tensor(
                        out=imp, in0=pO[j // 2][:, (j % 2) * 256 + 65:(j % 2) * 256 + 129], scalar=rs[:, j:j + 1],
                        in1=imp, op0=ALU.mult, op1=ALU.add), reads=[S.psbuf[1 + j // 2], rsb, impb], writes=[impb])
                S.op('dve', lambda e, imp=imp, qt=qt: e.tensor_tensor(out=imp, in0=imp, in1=selb_t[:, qt, :], op=ALU.add),
                     reads=[impb, selb_tb], writes=[impb])
                if lvl < 3:
                    continue
                S.op('dve', lambda e, m8=m8, imp=imp: e.max(out=m8[:, 0:8], in_=imp), reads=[impb], writes=[m8b])
                S.op('dve', lambda e, sc2=sc2, m8=m8, imp=imp: e.match_replace(
                    out=sc2, in_to_replace=m8[:, 0:8], in_values=imp, imm_value=-3e38), reads=[impb, m8b], writes=[sc2b])
                S.op('dve', lambda e, m8=m8, sc2=sc2: e.max(out=m8[:, 8:16], in_=sc2), reads=[sc2b, m8b], writes=[m8b])
                for hf in range(2):
                    S.op('dve', lambda e, nmt=nmt, imp=imp, m8=m8, hf=hf: e.tensor_scalar(
                        out=nmt[:, hf * 64:(hf + 1) * 64], in0=imp, scalar1=m8[:, 15:16], scalar2=-30000.0,
                        op0=ALU.is_lt, op1=ALU.mult), reads=[impb, m8b], writes=[nmtb])
                pvt = transposes_to(7, [nmt], [nmtb])
                S.op('act', lambda e, nmT1=nmT1, pvt=pvt: e.copy(out=nmT1, in_=pvt[:, 0:128]), reads=[S.psbuf[7]],
                     writes=[nmT1b])
                qsl = slice(64 * g2, 64 * g2 + 64)
                msl = slice(64 * (1 - g2), 64 * (1 - g2) + 64)
                S.op('dve', lambda e, nmT4=nmT4, nmT1=nmT1, msl=msl: e.tensor_copy(
                    out=nmT4[msl], in_=nmT1[msl].unsqueeze(1).to_broadcast([64, 4, 128])), reads=[nmT1b],
                    writes=[nmT4b])
                S.op('act', lambda e, nmT4=nmT4, qsl=qsl, qt=qt: e.copy(
                    out=nmT4[qsl], in_=qTp[qsl, :, qt * 128:(qt + 1) * 128]), reads=[qTpb], writes=[nmT4b])
                if lvl < 4:
                    continue
                S.op('dve', lambda e, rs=rs, ngv=ngv: e.tensor_tensor(out=rs[:, 4:8], in0=rs[:, 0:4], in1=ngv[:, :, 0],
                                                                     op=ALU.mult), reads=[rsb, ng_tb], writes=[rsb])
                for j in range(4):
                    S.op('act', lambda e, nacc=nacc, rs=rs, j=j: e.activation(
                        out=nacc[:, j, :], in_=pO[j // 2][:, (j % 2) * 256:(j % 2) * 256 + 64], func=AF.Copy,
                        scale=rs[:, 4 + j:5 + j]), reads=[S.psbuf[1 + j // 2], rsb], writes=[naccb])
                for br in (1, 2):
                    if lvl < 5 or (br == 2 and lvl < 6):
                        continue
                    kts = list(range(qt + 1)) if br == 1 else list(range(max(0, qt - 4), qt + 1))
                    bankO = 5 if br == 1 else 6
                    pA = S.ps(bankO)
                    va = g2 if br == 1 else 2 + g2
                    slot = {}
                    for ki, kt in enumerate(kts):
                        if br == 1:
                            P, Pb = Pall[:, kt], Pallb[kt]
                        else:
                            P, Pb = Pw[:, ki], Pwb[ki]
                        slot[kt] = (P, Pb)
                        bsc = 3 + (pit % 2)
                        pit += 1
                        pss = S.ps(bsc)
                        if br == 1:
                            kst, kstb = ksE_[g2]
                            S.op('pe', lambda e, pss=pss, kt=kt, nmT4=nmT4, kst=kst: e.matmul(
                                pss, lhsT=kst[:, kt * 128:(kt + 1) * 128], rhs=nmT4, start=True, stop=True),
                                reads=[kstb, nmT4b], writes=[S.psbuf[bsc]])
                        else:
                            S.op('pe', lambda e, pss=pss, br=br, kt=kt, pb=pb, q4=q4: e.matmul(
                                pss, lhsT=k4[pb:pb + 64, br, kt * 128:(kt + 1) * 128], rhs=q4, start=True, stop=True),
                                reads=[k4b, qTpb], writes=[S.psbuf[bsc]])
                        S.op('act', lambda e, P=P, pss=pss: e.activation(out=P.rearrange("p j t -> p (j t)"), in_=pss,
                                                                        func=AF.Exp), reads=[S.psbuf[bsc]], writes=[Pb])
                        if kt == qt:
                            S.op('dve', lambda e, P=P: e.tensor_tensor(out=P, in0=P, in1=caus4, op=ALU.mult),
                                 reads=[Pb, caus4b], writes=[Pb])
                        if br == 2 and kt == qt - 4:
                            S.op('dve', lambda e, P=P: e.tensor_tensor(out=P, in0=P, in1=anti4, op=ALU.mult),
                                 reads=[Pb, anti4b], writes=[Pb])
                    for j in range(4):
                        for kt in kts:
                            P, Pb = slot[kt]
                            S.op('pe', lambda e, pA=pA, j=j, P=P, kt=kt, va=va, kts=kts: e.matmul(
                                pA[:, j * 128:j * 128 + 65], lhsT=P[:, j, :], rhs=vsw_t[:, kt, va, :],
                                start=(kt == kts[0]), stop=(kt == kts[-1])), reads=[Pb, vsw_tb], writes=[S.psbuf[bankO]])
                    o8 = 8 if br == 1 else 12
                    for j in range(4):
                        S.op('act', lambda e, rs=rs, pA=pA, o8=o8, j=j: e.copy(
                            out=rs[:, o8 + j:o8 + j + 1], in_=pA[:, j * 128 + 64:j * 128 + 65]),
                            reads=[S.psbuf[bankO]], writes=[rsb])
                    S.op('dve', lambda e, rs=rs, o8=o8: e.tensor_scalar(
                        out=rs[:, o8:o8 + 4], in0=rs[:, o8:o8 + 4], scalar1=1e-30, scalar2=None, op0=ALU.max),
                        reads=[rsb], writes=[rsb])
                    S.op('dve', lambda e, rs=rs, o8=o8: e.reciprocal(out=rs[:, o8:o8 + 4], in_=rs[:, o8:o8 + 4]),
                         reads=[rsb], writes=[rsb])
                    S.op('dve', lambda e, rs=rs, ngv=ngv, o8=o8, br=br: e.tensor_tensor(
                        out=rs[:, o8:o8 + 4], in0=rs[:, o8:o8 + 4], in1=ngv[:, :, br], op=ALU.mult),
                        reads=[rsb, ng_tb], writes=[rsb])
                    for j in range(4):
                        S.op('dve', lambda e, nacc=nacc, pA=pA, rs=rs, j=j, o8=o8: e.scalar_tensor_tensor(
                            out=nacc[:, j, :], in0=pA[:, j * 128:j * 128 + 64], scalar=rs[:, o8 + j:o8 + j + 1],
                            in1=nacc[:, j, :], op0=ALU.mult, op1=ALU.add), reads=[S.psbuf[bankO], rsb, naccb],
                            writes=[naccb])
                if lvl < 7:
                    continue
                S.op('dve', lambda e, no=no, nacc=nacc, szt=szt, g2=g2: e.tensor_tensor(
                    out=no, in0=nacc.rearrange("p j d -> p (j d)"), in1=szt[:, g2 * 256:(g2 + 1) * 256], op=ALU.mult),
                    reads=[naccb, sztb], writes=[nob])
                pv2 = transposes_to(7, [no[:, j * 128:(j + 1) * 128] for j in range(2)], [nob])
                q4i = qt % 4
                S.op('act', lambda e, stg=stg, pv2=pv2, q4i=q4i, g2=g2: e.copy(
                    out=stg[:, g2 * 2:g2 * 2 + 2, q4i * 128:(q4i + 1) * 128],
                    in_=pv2[:, 0:256].rearrange("p (j t) -> p j t", j=2)), reads=[S.psbuf[7]], writes=[stgb])
            if qt % 4 == 3 and lvl >= 7:
                tb = qt // 4
                S.dma('sp', mixT[base_row:base_row + 512, :].rearrange("(j p) t -> p j t", p=128)[
                    :, :, tb * 512:(tb + 1) * 512], stg, reads=[stgb])
    S.barrier()
    S.release()


def phase_D(S, cfg, G):
    TF = cfg.TF
    ident, identb = G['ident'], G['identb']
    transposes_to = G['transposes_to']
    mixT, xf_in, pf_in, out = G['mixG'], G['xf_in'], G['pf_in'], G['out']
    mixGb = G['mixGb']
    S.mark()
    wo, wob = S.alloc([128, 16, DM], BF16, 'wo')
    wg, wgb = S.alloc([128, 16, DM], BF16, 'wg')
    wp, wpb = S.alloc([128, 2, DM], BF16, 'wp')
    pgt, pgtb = S.alloc([128, DM], F32, 'pgt')
    fgt, fgtb = S.alloc([128, DM], F32, 'fgt')
    for cb in range(4):
        S.dma('pool', wo[:, :, cb * 512:(cb + 1) * 512],
              G['w_out'][:, cb * 512:(cb + 1) * 512].rearrange("(c p) n -> p c n", p=128), writes=[wob])
    for cb in range(4):
        S.dma('pool', wg[:, :, cb * 512:(cb + 1) * 512],
              G['w_pg'][:, cb * 512:(cb + 1) * 512].rearrange("(c p) n -> p c n", p=128), writes=[wgb])
    S.dma('pool', wp, G['w_pp'].rearrange("(c p) n -> p c n", p=128), writes=[wpb])
    S.dma('sp', pgt, G['ple_g'].partition_broadcast(128), writes=[pgtb])
    S.dma('sp', fgt, G['fin_g'].partition_broadcast(128), writes=[fgtb])
    mT_ = [S.alloc([128, 16, 128], BF16, 'mT%d' % i) for i in range(2)]
    pt_ = [S.alloc([128, 256], F32, 'pt%d' % i) for i in range(2)]
    xt, xtb = S.alloc([128, DM], F32, 'xt')
    x1, x1b = S.alloc([128, DM], F32, 'x1')
    x1h, x1hb = S.alloc([128, DM], BF16, 'x1h')
    x1T, x1Tb = S.alloc([128, 16, 128], BF16, 'x1T')
    pl, plb = S.alloc([128, DM], F32, 'pl')
    junk, junkb = x1h, x1hb
    ph, phb = S.alloc([128, 256], BF16, 'ph')
    pT, pTb = S.alloc([128, 2, 128], BF16, 'pT')
    sgd_ = [S.alloc([128, 512], F32, 'sgd%d' % i) for i in range(2)]
    ss_ = [S.alloc([128, 1], F32, 'ssD%d' % i) for i in range(2)]
    mixv = mixT.rearrange("(c p) t -> p c t", p=128)
    it = 0
    for tt in range(TF // 128):
        rows = slice(tt * 128, (tt + 1) * 128)
        mT, mTb = mT_[tt % 2]
        pt, ptb = pt_[tt % 2]
        if tt == 0:
            S.dma('sp', mT, mixv[:, :, rows], reads=[mixGb], writes=[mTb])
            S.dma('sp', pt, pf_in[rows, :], writes=[ptb])
        if tt + 1 < TF // 128:
            nrows = slice((tt + 1) * 128, (tt + 2) * 128)
            S.dma('sp', mT_[(tt + 1) % 2][0], mixv[:, :, nrows], reads=[mixGb], writes=[mT_[(tt + 1) % 2][1]])
            S.dma('sp', pt_[(tt + 1) % 2][0], pf_in[nrows, :], writes=[pt_[(tt + 1) % 2][1]])
        S.dma('sp', xt, xf_in[rows, :], writes=[xtb])
        S.op('act', lambda e, pt=pt: e.copy(out=ph, in_=pt), reads=[ptb], writes=[phb])
        pv = transposes_to(7, [ph[:, j * 128:(j + 1) * 128] for j in range(2)], [phb])
        S.op('act', lambda e, pv=pv: e.copy(out=pT, in_=pv[:, 0:256].rearrange("p (j t) -> p j t", j=2)),
             reads=[S.psbuf[7]], writes=[pTb])
        for cb in range(4):
            bank = it % 4
            it += 1
            ps = S.ps(bank)
            cs = slice(cb * 512, (cb + 1) * 512)
            for kc in range(16):
                S.op('pe', lambda e, ps=ps, kc=kc, cs=cs, mT=mT: e.matmul(ps, lhsT=mT[:, kc, :], rhs=wo[:, kc, cs],
                                                                  start=(kc == 0), stop=(kc == 15)),
                     reads=[mTb, wob], writes=[S.psbuf[bank]])
            S.op('dve', lambda e, ps=ps, cs=cs: e.tensor_tensor(out=x1[:, cs], in0=ps, in1=xt[:, cs], op=ALU.add),
                 reads=[S.psbuf[bank], xtb], writes=[x1b])
        S.op('act', lambda e: e.copy(out=x1h, in_=x1), reads=[x1b], writes=[x1hb])
        for half in range(2):
            bank = 4 + half
            pv = transposes_to(bank, [x1h[:, (half * 8 + i) * 128:(half * 8 + i + 1) * 128] for i in range(8)], [x1hb])
            src = pv[:, 0:1024].rearrange("p (c t) -> p c t", c=8)
            dst = x1T[:, half * 8:(half + 1) * 8, :]
            if half == 0:
                S.op('act', lambda e, src=src, dst=dst: e.copy(out=dst, in_=src), reads=[S.psbuf[bank]], writes=[x1Tb])
            else:
                S.op('dve', lambda e, src=src, dst=dst: e.tensor_copy(out=dst, in_=src), reads=[S.psbuf[bank]],
                     writes=[x1Tb])
        for cb in range(4):
            bank = it % 4
            it += 1
            ps = S.ps(bank)
            cs = slice(cb * 512, (cb + 1) * 512)
            for k in range(2):
                S.op('pe', lambda e, ps=ps, k=k, cs=cs: e.matmul(ps, lhsT=pT[:, k, :], rhs=wp[:, k, cs], start=(k == 0),
                                                                stop=(k == 1)),
                     reads=[pTb, wpb], writes=[S.psbuf[bank]])
            S.op('act', lambda e, ps=ps, cs=cs: e.copy(out=pl[:, cs], in_=ps), reads=[S.psbuf[bank]], writes=[plb])
        ss, ssb = ss_[0]
        S.op('act', lambda e, ss=ss: e.activation(out=junk, in_=pl, func=AF.Square, accum_out=ss), reads=[plb],
             writes=[junkb, ssb])
        rstd_inplace(S, ss, ssb, DM)
        S.op('dve', lambda e, ss=ss: e.scalar_tensor_tensor(out=pl, in0=pl, scalar=ss, in1=pgt, op0=ALU.mult,
                                                            op1=ALU.mult), reads=[plb, ssb, pgtb], writes=[plb])
        for cb in range(4):
            bank = it % 4
            it += 1
            ps = S.ps(bank)
            cs = slice(cb * 512, (cb + 1) * 512)
            sgd, sgdb = sgd_[cb % 2]
            for kc in range(16):
                S.op('pe', lambda e, ps=ps, kc=kc, cs=cs: e.matmul(ps, lhsT=x1T[:, kc, :], rhs=wg[:, kc, cs],
                                                                  start=(kc == 0), stop=(kc == 15)),
                     reads=[x1Tb, wgb], writes=[S.psbuf[bank]])
            S.op('act', lambda e, ps=ps, sgd=sgd: e.activation(out=sgd, in_=ps, func=AF.Sigmoid), reads=[S.psbuf[bank]],
                 writes=[sgdb])
            S.op('dve', lambda e, sgd=sgd, cs=cs: e.tensor_tensor(out=sgd, in0=sgd, in1=pl[:, cs], op=ALU.mult),
                 reads=[sgdb, plb], writes=[sgdb])
            S.op('dve', lambda e, sgd=sgd, cs=cs: e.tensor_tensor(out=x1[:, cs], in0=x1[:, cs], in1=sgd, op=ALU.add),
                 reads=[sgdb, x1b], writes=[x1b])
        ss2, ss2b = ss_[1]
        S.op('act', lambda e, ss2=ss2: e.activation(out=junk, in_=x1, func=AF.Square, accum_out=ss2), reads=[x1b],
             writes=[junkb, ss2b])
        rstd_inplace(S, ss2, ss2b, DM)
        S.op('dve', lambda e, ss2=ss2: e.scalar_tensor_tensor(out=xt, in0=x1, scalar=ss2, in1=fgt, op0=ALU.mult,
                                                              op1=ALU.mult), reads=[x1b, ss2b, fgtb, xtb], writes=[xtb])
        S.dma('sp', out[rows, :], xt, reads=[xtb])
    S.barrier()
    S.release()


def core_inputs(inp, cfg, b, tok0, wout_rows):
    f = np.float32
    w_in = inp['w_in'][0]
    sb, ov, inv = host_consts()
    heads = cfg.heads
    fm = cfg.fm_cols
    cw = inp['conv_w'][0]
    cbias = inp['conv_b'][0]
    ch = [c - OFF['mq'] for c in fm]
    d = {
        'x': np.ascontiguousarray(inp['x'][b]),
        'xf': np.ascontiguousarray(inp['x'][b, tok0:tok0 + cfg.TF]),
        'pf': np.ascontiguousarray(inp['p'][0, b, tok0:tok0 + cfg.TF]),
        'pos': np.ascontiguousarray(inp['positions'][b]).astype(np.int32),
        'norm_g': np.ascontiguousarray(inp['norm_g'][0]),
        'w_fm': np.ascontiguousarray(w_in[:, fm]),
        'w_if': np.ascontiguousarray(w_in[:, cfg.if_cols]),
        'w_tm': np.ascontiguousarray(w_in[:, cfg.tm_cols]),
        'convw': np.ascontiguousarray(cw[:, ch].T.reshape(-1, 128, 4).transpose(1, 0, 2)),
        'convb': np.ascontiguousarray(cbias[ch].reshape(-1, 128).T[:, :, None]),
        'b_i': np.ascontiguousarray(inp['b_igate'][0][heads][:, None]),
        'b_f': np.ascontiguousarray(inp['b_fgate'][0][heads][:, None]),
        'mng': np.ascontiguousarray(np.concatenate([inp['m_norm_g'][0][h * 256:(h + 1) * 256] for h in heads])),
        'posk': np.ascontiguousarray(inp['cmp_pos_k'][0].T),
        'posv': np.ascontiguousarray(inp['cmp_pos_v'][0].T),
        'ckw1': np.ascontiguousarray(inp['cmp_k_w1'][0].reshape(32, 64, 128).transpose(1, 0, 2)),
        'ckw2': np.ascontiguousarray(inp['cmp_k_w2'][0]),
        'cvw1': np.ascontiguousarray(inp['cmp_v_w1'][0].reshape(32, 64, 128).transpose(1, 0, 2)),
        'cvw2': np.ascontiguousarray(inp['cmp_v_w2'][0]),
        'w_out': np.ascontiguousarray(inp['w_out'][0][wout_rows]),
        'w_pg': np.ascontiguousarray(inp['ple_gate_w'][0]),
        'w_pp': np.ascontiguousarray(inp['ple_proj_w'][0]),
        'ple_g': np.ascontiguousarray(inp['ple_norm_g'][0]),
        'fin_g': np.ascontiguousarray(inp['final_norm_g']),
        'selbias': sb, 'ovl': ov, 'invf': inv,
    }
    return {k: (v if v.dtype == np.int32 else v.astype(f)) for k, v in d.items()}


_NC_CACHE = {}


def kernel(**inputs):
    inp = {k: np.asarray(v) for k, v in inputs.items()}
    cfgs = [Cfg([2 * hh, 2 * hh + 1], [(2 * hh, 2 * hh + 1)], SEQ, gather=True) for hh in range(2)]
    if 'nc' not in _NC_CACHE:
        _NC_CACHE['nc'] = build(cfgs[0])
    nc = _NC_CACHE['nc']
    rows = []
    for ci in range(len(cfgs[0].mix_rows) // 128):
        for hh in range(2):
            rows.extend(cfgs[hh].mix_rows[ci * 128:(ci + 1) * 128])
    in_maps = []
    for c in range(8):
        in_maps.append(core_inputs(inp, cfgs[c % 2], c // 2, 0, rows))
    res = run_bass_kernel_spmd(nc, in_maps, core_ids=list(range(8)))
    out = np.stack([np.asarray(res.results[2 * b]["out"]) for b in range(4)], axis=0)
    return out.astype(np.float32)
```
